# Optimizing a Trainium2 kernel written in Bass

```python
import math
import jax, jax.numpy as jnp
from jax import lax
import numpy as np

D_MODEL = 2048
BATCH = 2
SEQ = 4096
DEPTH = 1
DEC_BATCH = 128
DEC_SEQ = 4
PAST_LEN = 16384
PAGE_SIZE = 128

N_META = 16
WINDOW = 128
BLOCK = 128
A_HEADS = 16
A_KV_HEADS = 4
A_GROUP = A_HEADS // A_KV_HEADS
A_HEAD_DIM = 64
A_ROT_DIM = A_HEAD_DIM // 4
A_ROPE_THETA = 500000.0
A_WIDTH = A_HEADS * A_HEAD_DIM
A_KV_WIDTH = A_KV_HEADS * A_HEAD_DIM
R_HEADS = 8
R_KEY_DIM = 128
R_VAL_DIM = 256
R_QK_WIDTH = R_HEADS * R_KEY_DIM
R_V_WIDTH = R_HEADS * R_VAL_DIM
R_ROPE_THETA = 10000.0
R_CHUNK = 128
EPS = 1e-6
GN_EPS = 1e-5
NEG_INF = -1e30
SPLITS = (A_WIDTH, A_KV_WIDTH, A_KV_WIDTH, A_WIDTH, R_QK_WIDTH, R_QK_WIDTH, R_V_WIDTH, R_V_WIDTH, D_MODEL, D_MODEL)
SPLIT_POINTS = tuple(int(s) for s in np.cumsum(SPLITS)[:-1])
IN_WIDTH = sum(SPLITS)

kernel_name = "hybrid_swa_sink_retention_meta_step"


def rms_norm(x, g):
    xf = x.astype(jnp.float32)
    y = xf * lax.rsqrt(jnp.mean(xf * xf, axis=-1, keepdims=True) + EPS) * g.astype(jnp.float32)
    return y.astype(x.dtype)


def rope(x, pos, rot_dim, theta):
    half = rot_dim // 2
    inv = jnp.exp(-math.log(theta) * 2.0 * jnp.arange(half, dtype=jnp.float32) / rot_dim)
    ang = pos.astype(jnp.float32)[:, None] * inv[None, :]
    cos = jnp.cos(ang)[:, None, :]
    sin = jnp.sin(ang)[:, None, :]
    xf = x[..., :rot_dim].astype(jnp.float32)
    x1, x2 = xf[..., :half], xf[..., half:]
    rot = jnp.concatenate([x1 * cos - x2 * sin, x2 * cos + x1 * sin], axis=-1).astype(x.dtype)
    return jnp.concatenate([rot, x[..., rot_dim:]], axis=-1)


def layer_inputs(h, pos, norm_g, w_in, q_g, k_g):
    b, t = h.shape[0], h.shape[1]
    u = jnp.einsum('btd,de->bte', rms_norm(h, norm_g), w_in)
    qa, ka, va, za, qr, kr, vr, zr, ga, gr = jnp.split(u, SPLIT_POINTS, axis=-1)
    qa = rope(rms_norm(qa.reshape(b, t, A_HEADS, A_HEAD_DIM), q_g), pos, A_ROT_DIM, A_ROPE_THETA)
    qa = qa.reshape(b, t, A_KV_HEADS, A_GROUP, A_HEAD_DIM)
    ka = rope(rms_norm(ka.reshape(b, t, A_KV_HEADS, A_HEAD_DIM), k_g), pos, A_ROT_DIM, A_ROPE_THETA)
    va = va.reshape(b, t, A_KV_HEADS, A_HEAD_DIM)
    qr = rope(qr.reshape(b, t, R_HEADS, R_KEY_DIM), pos, R_KEY_DIM, R_ROPE_THETA).astype(jnp.float32)
    kr = (rope(kr.reshape(b, t, R_HEADS, R_KEY_DIM), pos, R_KEY_DIM, R_ROPE_THETA).astype(jnp.float32)
          * (R_KEY_DIM ** -0.5))
    vr = vr.reshape(b, t, R_HEADS, R_VAL_DIM).astype(jnp.float32)
    return (qa, ka, va), (qr, kr, vr), (za, zr, ga, gr)


def sink_attend(q, k, v, mask, sink):
    s = jnp.einsum('...qhgd,...khd->...hgqk', q, k).astype(jnp.float32) * (A_HEAD_DIM ** -0.5)
    s = jnp.where(mask, s, NEG_INF)
    sk = sink.astype(jnp.float32)[:, :, None, None]
    m = jnp.maximum(jnp.max(s, axis=-1, keepdims=True), sk)
    p = jnp.exp(s - m)
    p = p / (jnp.sum(p, axis=-1, keepdims=True) + jnp.exp(sk - m))
    o = jnp.einsum('...hgqk,...khd->...qhgd', p.astype(v.dtype), v)
    return o.reshape(o.shape[:-3] + (A_WIDTH,))


def retention_chunk(S, q, k, v, lg):
    S = S.astype(jnp.float32)
    c = q.shape[1]
    idx = jnp.arange(c, dtype=jnp.float32)
    rel = idx[:, None] - idx[None, :]
    decay = jnp.where(rel >= 0, jnp.exp(jnp.maximum(rel, 0.0)[None] * lg[:, None, None]), 0.0)
    inner = jnp.einsum('bihd,bjhd->bhij', q, k) * decay
    o = jnp.einsum('bhij,bjhe->bihe', inner, v)
    o = o + jnp.einsum('bihd,bhde->bihe', q, S) * jnp.exp((idx[:, None] + 1.0) * lg[None, :])[..., None]
    wk = jnp.exp((c - 1.0 - idx)[:, None] * lg[None, :])
    S_new = jnp.exp(c * lg)[:, None, None] * S + jnp.einsum('bjhd,bjhe->bhde', k * wk[..., None], v)
    return o, S_new


def retention_scan(S0, q, k, v, lg):
    b, t, h = q.shape[0], q.shape[1], q.shape[2]
    n = t // R_CHUNK

    def to_chunks(a):
        return a.reshape(b, n, R_CHUNK, h, a.shape[-1]).swapaxes(0, 1)

    def step(S, qkv):
        o, S = retention_chunk(S, qkv[0], qkv[1], qkv[2], lg)
        return S, o

    S, o = lax.scan(step, S0, (to_chunks(q), to_chunks(k), to_chunks(v)))
    return o.swapaxes(0, 1).reshape(b, t, h, R_VAL_DIM), S


def merge_out(o_a, o_r, gates, gn_g, gn_b, w_pa, w_pr, w_o):
    za, zr, ga, gr = gates
    mu = jnp.mean(o_r, axis=-1, keepdims=True)
    var = jnp.mean(jnp.square(o_r - mu), axis=-1, keepdims=True)
    o_r = ((o_r - mu) * lax.rsqrt(var + GN_EPS)).reshape(o_r.shape[:-2] + (R_V_WIDTH,))
    o_r = (o_r * gn_g.astype(jnp.float32) + gn_b.astype(jnp.float32)).astype(zr.dtype)
    y_a = (o_a * jax.nn.silu(za)) @ w_pa
    y_r = (o_r * jax.nn.silu(zr)) @ w_pr
    return (jax.nn.sigmoid(ga) * y_a + jax.nn.sigmoid(gr) * y_r) @ w_o


def setup_inputs(seed: int = 0) -> dict:
    key = jax.random.key(seed)
    ks = jax.random.split(key, 18)
    win_buf = min(WINDOW, PAST_LEN)
    f = jnp.float32
    nrm = jax.random.normal
    return {
        "x_prompt": nrm(ks[0], (BATCH, SEQ, D_MODEL), f),
        "x_sample": nrm(ks[1], (DEC_BATCH, DEC_SEQ, D_MODEL), f),
        "cache_win_k": nrm(ks[2], (DEPTH, DEC_BATCH, win_buf, A_KV_HEADS, A_HEAD_DIM), f),
        "cache_win_v": nrm(ks[3], (DEPTH, DEC_BATCH, win_buf, A_KV_HEADS, A_HEAD_DIM), f),
        "state_ret": 0.5 * nrm(ks[4], (DEPTH, DEC_BATCH, R_HEADS, R_KEY_DIM, R_VAL_DIM), f),
        "meta_tokens": nrm(ks[5], (N_META, D_MODEL), f),
        "norm_gain": 1.0 + 0.02 * nrm(ks[6], (DEPTH, D_MODEL), f),
        "w_in": nrm(ks[7], (DEPTH, D_MODEL, IN_WIDTH), f) * D_MODEL ** -0.5,
        "q_norm_gain": 1.0 + 0.02 * nrm(ks[8], (DEPTH, A_HEAD_DIM), f),
        "k_norm_gain": 1.0 + 0.02 * nrm(ks[9], (DEPTH, A_HEAD_DIM), f),
        "attn_sinks": 0.5 * nrm(ks[10], (DEPTH, A_HEADS), f),
        "ret_gn_gain": 1.0 + 0.02 * nrm(ks[11], (DEPTH, R_V_WIDTH), f),
        "ret_gn_bias": 0.02 * nrm(ks[12], (DEPTH, R_V_WIDTH), f),
        "w_branch_attn": nrm(ks[13], (DEPTH, A_WIDTH, D_MODEL), f) * A_WIDTH ** -0.5,
        "w_branch_ret": nrm(ks[14], (DEPTH, R_V_WIDTH, D_MODEL), f) * R_V_WIDTH ** -0.5,
        "w_out": nrm(ks[15], (DEPTH, D_MODEL, D_MODEL), f) * D_MODEL ** -0.5,
    }


def reference(x_prompt, x_sample, cache_win_k, cache_win_v, state_ret, meta_tokens, norm_gain, w_in,
              q_norm_gain, k_norm_gain, attn_sinks, ret_gn_gain, ret_gn_bias, w_branch_attn,
              w_branch_ret, w_out):
    lg = jnp.asarray(np.log(1.0 - np.exp(np.linspace(np.log(1.0 / 32), np.log(1.0 / 512), R_HEADS)))
                     .astype(np.float32))
    b_p, t_p = x_prompt.shape[0], x_prompt.shape[1]
    b_s, t_s = x_sample.shape[0], x_sample.shape[1]
    win_buf = cache_win_k.shape[2]
    nb = t_p // BLOCK

    pos_m = jnp.arange(N_META)
    pos_p = N_META + jnp.arange(t_p)
    pos_s = PAST_LEN + jnp.arange(t_s)

    mask_m = jnp.tril(jnp.ones((N_META, N_META), bool))
    qi = jnp.arange(BLOCK)[:, None]
    kj = jnp.arange(2 * BLOCK)[None, :] - BLOCK
    band = (kj <= qi) & (kj > qi - WINDOW)
    band = band[None] & ((jnp.arange(nb)[:, None, None] * BLOCK + kj[None]) >= 0)
    mask_p = jnp.concatenate([jnp.ones((nb, BLOCK, N_META), bool), band], axis=-1)[None, :, None, None]
    kpos = jnp.concatenate([PAST_LEN - win_buf + jnp.arange(win_buf), pos_s])
    band_s = ((kpos[None] <= pos_s[:, None]) & (kpos[None] > pos_s[:, None] - WINDOW)
              & (kpos[None] >= N_META))
    mask_s = jnp.concatenate([jnp.ones((t_s, N_META), bool), band_s], axis=-1)

    h_m = meta_tokens[None].astype(x_prompt.dtype)
    h_p = x_prompt
    h_s = x_sample
    wk_p, wv_p, rs_p, wk_s, wv_s, rs_s = [], [], [], [], [], []
    for l in range(DEPTH):
        sink = attn_sinks[l].reshape(A_KV_HEADS, A_GROUP)
        lw = (norm_gain[l], w_in[l], q_norm_gain[l], k_norm_gain[l])
        mw = (ret_gn_gain[l], ret_gn_bias[l], w_branch_attn[l], w_branch_ret[l], w_out[l])

        (qm, km, vm), (rqm, rkm, rvm), gm = layer_inputs(h_m, pos_m, *lw)
        o_am = sink_attend(qm, km, vm, mask_m, sink)
        o_rm, S_m = retention_chunk(jnp.zeros((1, R_HEADS, R_KEY_DIM, R_VAL_DIM), jnp.float32),
                                    rqm, rkm, rvm, lg)
        h_m_next = h_m + merge_out(o_am, o_rm, gm, *mw)

        (qp, kp, vp), (rqp, rkp, rvp), gp = layer_inputs(h_p, pos_p, *lw)
        qb = qp.reshape(b_p, nb, BLOCK, A_KV_HEADS, A_GROUP, A_HEAD_DIM)
        kb = kp.reshape(b_p, nb, BLOCK, A_KV_HEADS, A_HEAD_DIM)
        vb = vp.reshape(b_p, nb, BLOCK, A_KV_HEADS, A_HEAD_DIM)
        shp = (b_p, nb, N_META, A_KV_HEADS, A_HEAD_DIM)
        k_all = jnp.concatenate([jnp.broadcast_to(km[:, None], shp).astype(kb.dtype),
                                 jnp.concatenate([jnp.zeros_like(kb[:, :1]), kb[:, :-1]], axis=1), kb], axis=2)
        v_all = jnp.concatenate([jnp.broadcast_to(vm[:, None], shp).astype(vb.dtype),
                                 jnp.concatenate([jnp.zeros_like(vb[:, :1]), vb[:, :-1]], axis=1), vb], axis=2)
        o_ap = sink_attend(qb, k_all, v_all, mask_p, sink).reshape(b_p, t_p, A_WIDTH)
        S0 = jnp.broadcast_to(S_m, (b_p, R_HEADS, R_KEY_DIM, R_VAL_DIM))
        o_rp, S_p = retention_scan(S0, rqp, rkp, rvp, lg)
        h_p = h_p + merge_out(o_ap, o_rp, gp, *mw)
        wk_p.append(kp[:, -win_buf:])
        wv_p.append(vp[:, -win_buf:])
        rs_p.append(S_p)

        (qs, ks_, vs), (rqs, rks, rvs), gs = layer_inputs(h_s, pos_s, *lw)
        kw = jnp.concatenate([cache_win_k[l].astype(ks_.dtype), ks_], axis=1)
        vw = jnp.concatenate([cache_win_v[l].astype(vs.dtype), vs], axis=1)
        shs = (b_s, N_META, A_KV_HEADS, A_HEAD_DIM)
        k_all_s = jnp.concatenate([jnp.broadcast_to(km[0], shs).astype(kw.dtype), kw], axis=1)
        v_all_s = jnp.concatenate([jnp.broadcast_to(vm[0], shs).astype(vw.dtype), vw], axis=1)
        o_as = sink_attend(qs, k_all_s, v_all_s, mask_s, sink)
        o_rs, S_s = retention_chunk(state_ret[l], rqs, rks, rvs, lg)
        h_s = h_s + merge_out(o_as, o_rs, gs, *mw)
        wk_s.append(kw[:, -win_buf:])
        wv_s.append(vw[:, -win_buf:])
        rs_s.append(S_s)

        h_m = h_m_next

    win_k_prompt = jnp.stack(wk_p)
    win_v_prompt = jnp.stack(wv_p)
    ret_prompt = jnp.stack(rs_p)
    win_k_sample = jnp.stack(wk_s)
    win_v_sample = jnp.stack(wv_s)
    ret_sample = jnp.stack(rs_s)
    return (h_p, h_s, win_k_prompt, win_v_prompt, ret_prompt, win_k_sample, win_v_sample, ret_sample)
```

```python
import math
import os
import types
from contextlib import ExitStack
import numpy as np
import concourse.bass as bass
import concourse.mybir as mybir
from concourse.bass_utils import run_bass_kernel_spmd

F32 = mybir.dt.float32
BF16 = mybir.dt.bfloat16
ALU = mybir.AluOpType
AF = mybir.ActivationFunctionType
AX = mybir.AxisListType

NCORES = 8
D = 2048
INW = 12800
NT = 9
NTH = 10
TOK = NT * 128
EPS = 1e-6
GN_EPS = 1e-5
C_QA, C_KA, C_VA, C_ZA, C_QR, C_KR, C_VR, C_ZR, C_GA, C_GR = 0, 1024, 1280, 1536, 2560, 3584, 4608, 6656, 8704, 10752
NDS = 30
SUB = os.environ.get("MK_SUB", "")
OVERLAP = os.environ.get("MK_OVERLAP", "1") == "1"
NPRE = 24

_lg = np.log(1.0 - np.exp(np.linspace(np.log(1.0 / 32), np.log(1.0 / 512), 8))).astype(np.float32)
GAM = np.exp(_lg.astype(np.float64))


def _freeze(fn):
    if fn.__closure__ is None:
        return fn
    cells = []
    for c in fn.__closure__:
        try:
            cells.append(types.CellType(c.cell_contents))
        except ValueError:
            cells.append(c)
    return types.FunctionType(fn.__code__, fn.__globals__, fn.__name__, fn.__defaults__, tuple(cells))


class Sched:
    def __init__(self, nc):
        self.nc = nc
        self.sem = {k: nc.alloc_semaphore(name="s_" + k) for k in ["pe", "act", "dve", "pool"]}
        self.cnt = {k: 0 for k in self.sem}
        self.dsem = [nc.alloc_semaphore(name="d%d" % i) for i in range(NDS)]
        self.dval = [0] * NDS
        self.dnext = {"sp": 0, "pool": 0, "act": 0}
        self.dring = {"sp": list(range(0, 14)), "pool": list(range(14, 24)), "act": list(range(24, NDS))}
        self.ccsem = nc.alloc_semaphore(name="ccs")
        self.queues = ["pe", "act", "dve", "pool", "sp"]
        self.seen = {q: {} for q in self.queues}
        self.reg = {}
        self.prog = {q: [] for q in self.queues}
        self.phase = 0
        self.maxphase = int(os.environ.get("MK_MAXPHASE", "99"))
        self.minphase = int(os.environ.get("MK_MINPHASE", "0"))

    def set_phase(self, n):
        self.phase = n

    @property
    def on(self):
        return self.phase <= self.maxphase and (self.phase == 0 or self.phase >= self.minphase)

    def _semh(self, k):
        if k[0] == "e":
            return self.sem[k[1]]
        if k[0] == "c":
            return self.ccsem
        return self.dsem[k[1]]

    def _deps(self, reads, writes):
        need = {}

        def add(st):
            if st is None:
                return
            k, t = st
            if need.get(k, 0) < t:
                need[k] = t
        for r in reads:
            e = self.reg.get(r)
            if e:
                add(e[0])
        for w in writes:
            e = self.reg.get(w)
            if e:
                add(e[0])
                for k, t in e[1].items():
                    add((k, t))
        return need

    def _emit_waits(self, q, need):
        for k, t in need.items():
            if k == ("e", "pe") and q == "pe":
                continue
            if self.seen[q].get(k, 0) >= t:
                continue
            self.seen[q][k] = t
            sem = self._semh(k)
            self.prog[q].append(lambda e, sem=sem, t=t: e.wait_ge(sem, t))

    def _mark(self, reads, writes, st):
        k, t = st
        for r in reads:
            e = self.reg.setdefault(r, [None, {}])
            e[1][k] = max(e[1].get(k, 0), t)
        for w in writes:
            self.reg[w] = [st, {}]

    def op(self, q, fn, reads=(), writes=(), signal=True):
        if not self.on:
            return
        fn = _freeze(fn)
        need = self._deps(reads, writes)
        self._emit_waits(q, need)
        tick = self.cnt[q] + 1
        if signal:
            self.cnt[q] = tick
            sem = self.sem[q]
            self.prog[q].append(lambda e, fn=fn, sem=sem: fn(e).then_inc(sem, 1))
        else:
            self.prog[q].append(lambda e, fn=fn: fn(e))
        self._mark(reads, writes, (("e", q), tick))

    def dma(self, q, out, in_, reads=(), writes=()):
        if not self.on:
            return
        need = self._deps(reads, writes)
        ring = self.dring[q]
        i = ring[self.dnext[q] % len(ring)]
        self.dnext[q] += 1
        if self.dval[i] > 0:
            need[("d", i)] = max(need.get(("d", i), 0), self.dval[i])
        self._emit_waits(q, need)
        self.dval[i] += 16
        v = self.dval[i]
        sem = self.dsem[i]
        self.prog[q].append(lambda e, out=out, in_=in_, sem=sem: e.dma_start(out=out, in_=in_).then_inc(sem, 16))
        self._mark(reads, writes, (("d", i), v))

    def barrier(self):
        if not self.on:
            return
        for q in self.queues:
            need = {}
            for k in self.sem:
                if self.cnt[k] > 0 and k != q:
                    need[("e", k)] = self.cnt[k]
            for i in range(NDS):
                if self.dval[i] > 0:
                    need[("d", i)] = self.dval[i]
            self._emit_waits(q, need)


def build_program():
    nc = bass.Bass("TRN2", target_bir_lowering=False)
    S = Sched(nc)

    def din(name, shape, dt=F32):
        return nc.dram_tensor(name, list(shape), dt, kind="ExternalInput").ap()

    def dout(name, shape):
        return nc.dram_tensor(name, list(shape), F32, kind="ExternalOutput").ap()

    x_all = din("x_all", [NTH, 128, D])
    w_in = din("w_in", [D, INW])
    w_pa = din("w_pa", [1024, D])
    w_pr = din("w_pr", [D, D])
    w_out = din("w_out", [D, D])
    norm_g = din("norm_g", [1, D])
    gqk_in = din("gqk", [1, 20 * 64])
    esink_in = din("sinks", [1, 16])
    gn_g = din("gn_g", [1, D])
    gn_b = din("gn_b", [1, D])
    cache_k = din("cache_k", [16, 128, 256])
    cache_v = din("cache_v", [16, 128, 256])
    state = din("state", [16, 8, 128, 256])
    ropeR_in = din("ropeR", [128, NT, 2, 64])
    ropeA_in = din("ropeA", [128, NTH, 2, 8])
    kwsc_in = din("kwsc", [128, NT, 8])
    dmask_in = din("dmask", [128, 8, 2, 128])
    qgt_in = din("qgt", [128, 8, 128])
    qgs_in = din("qgs", [128, 8, 4])
    bsel_in = din("bsel", [128, 16])
    amask_in = din("amask", [128, 3, 128])
    smn_in = din("smn", [128, 64])
    smc_in = din("smc", [128, 16])
    coef_in = din("coef", [128, 9, 8])
    ident_in = din("ident", [128, 128])
    x_pre = din("x_pre", [NPRE, 128, D])
    ropeP_in = din("ropeP", [128, NPRE, 2, 64])
    kwp_in = din("kwp", [128, NPRE, 8])

    y_p = dout("y_p", [1024, D])
    y_s = dout("y_s", [64, D])
    wk_p = dout("wk_p", [128, 256])
    wv_p = dout("wv_p", [128, 256])
    ret_p = dout("ret_p", [8, 128, 256])
    wk_s = dout("wk_s", [16, 128, 256])
    wv_s = dout("wv_s", [16, 128, 256])
    ret_s = dout("ret_s", [16, 8, 128, 256])

    U = nc.dram_tensor("U_scr", [NTH * 128, INW], F32, kind="Internal").ap()
    OA = nc.dram_tensor("OA_scr", [NT * 128, 1024], BF16, kind="Internal").ap()

    uid = [0]

    def sb(st, shape, dt, name):
        uid[0] += 1
        return st.enter_context(nc.sbuf_tensor("%s_%d" % (name, uid[0]), list(shape), dt))

    top = ExitStack()
    Wh = {}
    Spre = sb(top, [128, 8, 256], F32, "Spre")
    ident32 = sb(top, [128, 128], F32, "ident32")
    identb = sb(top, [128, 128], BF16, "identb")
    psA = top.enter_context(nc.psum_tensor("psA", [128, 2, 512], F32))
    psT = top.enter_context(nc.psum_tensor("psT", [128, 2, 8, 128], BF16))
    psM = top.enter_context(nc.psum_tensor("psM", [128, 4, 512], F32))

    S.dma("sp", ident32[:], ident_in, writes=["ident32"])
    S.op("dve", lambda e: e.tensor_copy(out=identb[:], in_=ident32[:]), reads=["ident32"], writes=["identb"])

    wstate = {"n": 0}

    def load_w(src, c0, nkc, ncols=512):
        b = wstate["n"] % 2
        wstate["n"] += 1
        srcv = src.rearrange("(kc p) n -> p kc n", p=128)
        W = Wh["W"]
        for k0 in range(0, nkc, 4):
            S.dma("pool", W[:, b, k0:k0 + 4, 0:ncols], srcv[:, k0:k0 + 4, c0:c0 + ncols], writes=["W%d_%d" % (b, k0)])
        return b

    def wkeys(b, nkc):
        return ["W%d_%d" % (b, k0) for k0 in range(0, nkc, 4)]

    rr = {"ev": 0}

    def evq():
        rr["ev"] += 1
        return "act" if rr["ev"] % 2 else "dve"

    def copy_op(q, out, in_, reads, writes):
        if q == "act":
            S.op("act", lambda e: e.activation(out=out, in_=in_, func=AF.Copy), reads=reads, writes=writes)
        else:
            S.op(q, lambda e: e.tensor_copy(out=out, in_=in_), reads=reads, writes=writes)


    S.set_phase(1)
    with ExitStack() as stP:
        Wkv = sb(stP, [128, 16, 3072], BF16, "Wkv")
        xstP = sb(stP, [128, 3, D], F32, "xstP")
        gbcP = sb(stP, [128, D], F32, "gbcP")
        sqjP = sb(stP, [128, D], BF16, "sqjP")
        xnbP = sb(stP, [128, 3, D], BF16, "xnbP")
        xnTp = sb(stP, [128, 2, 16, 128], BF16, "xnTp")
        ropeP = sb(stP, [128, NPRE, 2, 64], F32, "ropeP")
        kwp = sb(stP, [128, NPRE, 8], F32, "kwp")
        ssP = sb(stP, [128, NPRE], F32, "ssP")
        rsP = sb(stP, [128, NPRE], F32, "rsP")
        taP = sb(stP, [128, 4, 64], F32, "taP")
        tbP = sb(stP, [128, 4, 64], F32, "tbP")
        kro = sb(stP, [128, 8, 128], F32, "kro")
        kwb = sb(stP, [128, 2, 1024], BF16, "kwb")
        vb = sb(stP, [128, 2, D], BF16, "vb")
        ztile = sb(stP, [128, 512], BF16, "ztile")
        srcv = w_in.rearrange("(kc p) n -> p kc n", p=128)
        for cb in range(6):
            for k0 in range(0, 16, 4):
                S.dma("pool", Wkv[:, k0:k0 + 4, cb * 512:(cb + 1) * 512], srcv[:, k0:k0 + 4, C_KR + cb * 512:C_KR + (cb + 1) * 512], writes=["Wkv%d_%d" % (cb, k0)])
        S.dma("sp", gbcP[:], norm_g[0].partition_broadcast(128), writes=["gbcP"])
        S.dma("sp", ropeP[:], ropeP_in, writes=["ropeP"])
        S.dma("sp", kwp[:], kwp_in, writes=["kwp"])
        itpc = [0]

        def prepA(t):
            b = t % 3
            S.dma("sp", xstP[:, b, :], x_pre[t], writes=["xstP%d" % b])
            S.op("act", lambda e, b=b, t=t: e.activation(out=sqjP[:], in_=xstP[:, b, :], func=AF.Square, accum_out=ssP[:, t:t + 1]),
                 reads=["xstP%d" % b], writes=["sqjP", "ssP%d" % t])
            S.op("dve", lambda e, t=t: e.tensor_scalar(out=rsP[:, t:t + 1], in0=ssP[:, t:t + 1], scalar1=1.0 / D, scalar2=EPS, op0=ALU.mult, op1=ALU.add),
                 reads=["ssP%d" % t], writes=["rsP%d" % t])
            S.op("act", lambda e, t=t: e.activation(out=rsP[:, t:t + 1], in_=rsP[:, t:t + 1], func=AF.Sqrt), reads=["rsP%d" % t], writes=["rsP%d" % t])
            S.op("dve", lambda e, t=t: e.reciprocal(out=rsP[:, t:t + 1], in_=rsP[:, t:t + 1]), reads=["rsP%d" % t], writes=["rsP%d" % t])
            S.op("dve", lambda e, b=b, t=t: e.scalar_tensor_tensor(out=xnbP[:, b, :], in0=xstP[:, b, :], scalar=rsP[:, t:t + 1], in1=gbcP[:], op0=ALU.mult, op1=ALU.mult),
                 reads=["xstP%d" % b, "rsP%d" % t, "gbcP"], writes=["xnbP%d" % b])

        def prepB(t):
            b = t % 3
            b2 = t % 2
            for kc in range(16):
                S.op("pe", lambda e, b=b, kc=kc: e.transpose(out=psT[:, kc // 8, kc % 8, :], in_=xnbP[:, b, kc * 128:(kc + 1) * 128], identity=identb[:]),
                     reads=["xnbP%d" % b, "identb"], writes=["psT%d" % (kc // 8)], signal=(kc % 8 == 7))
            for hh in range(2):
                copy_op(evq(), xnTp[:, b2, hh * 8:(hh + 1) * 8, :], psT[:, hh, :, :], ["psT%d" % hh], ["xnTp%d" % b2])

        def stateP(t):
            b = t % 2
            for h in range(8):
                S.op("pe", lambda e, b=b, h=h, t=t: e.matmul(psM[:, h // 2, (h % 2) * 256:(h % 2) * 256 + 256], lhsT=kwb[:, b, h * 128:(h + 1) * 128], rhs=vb[:, b, h * 256:(h + 1) * 256],
                                                         start=False, stop=(t == NPRE - 1), skip_group_check=True),
                     reads=["kwb%d_%d" % (b, h // 4), "vb%d_%d" % (b, h // 2)], writes=["psMacc"], signal=(h == 7))

        def mainP(t):
            b = t % 2
            for cb in range(6):
                pb = itpc[0] % 2
                itpc[0] += 1
                for kc in range(16):
                    S.op("pe", lambda e, pb=pb, kc=kc, b=b, cb=cb: e.matmul(psA[:, pb, :], lhsT=xnTp[:, b, kc, :], rhs=Wkv[:, kc, cb * 512:(cb + 1) * 512], start=(kc == 0), stop=(kc == 15)),
                         reads=["xnTp%d" % b, "Wkv%d_%d" % (cb, (kc // 4) * 4)], writes=["psA%d" % pb], signal=(kc == 15))
                if cb < 2:
                    xv = psA[:, pb, :].rearrange("p (h two d) -> p h two d", h=4, two=2)
                    x1 = xv[:, :, 0, :]
                    x2 = xv[:, :, 1, :]
                    cosb = ropeP[:, t, 0, :].unsqueeze(1).to_broadcast([128, 4, 64])
                    sinb = ropeP[:, t, 1, :].unsqueeze(1).to_broadcast([128, 4, 64])
                    ov = kro[:, cb * 4:(cb + 1) * 4, :].rearrange("p h (two d) -> p h two d", two=2)
                    rk = ["psA%d" % pb, "ropeP"]
                    S.op("dve", lambda e, x1=x1, cosb=cosb: e.tensor_tensor(out=taP[:], in0=x1, in1=cosb, op=ALU.mult), reads=rk, writes=["taP"])
                    S.op("dve", lambda e, x2=x2, sinb=sinb: e.tensor_tensor(out=tbP[:], in0=x2, in1=sinb, op=ALU.mult), reads=rk, writes=["tbP"])
                    S.op("dve", lambda e, ov=ov: e.tensor_tensor(out=ov[:, :, 0, :], in0=taP[:], in1=tbP[:], op=ALU.subtract), reads=["taP", "tbP"], writes=["kro%d" % cb])
                    S.op("dve", lambda e, x2=x2, cosb=cosb: e.tensor_tensor(out=taP[:], in0=x2, in1=cosb, op=ALU.mult), reads=rk, writes=["taP"])
                    S.op("dve", lambda e, x1=x1, sinb=sinb: e.tensor_tensor(out=tbP[:], in0=x1, in1=sinb, op=ALU.mult), reads=rk, writes=["tbP"])
                    S.op("dve", lambda e, ov=ov: e.tensor_tensor(out=ov[:, :, 1, :], in0=taP[:], in1=tbP[:], op=ALU.add), reads=["taP", "tbP"], writes=["kro%d" % cb])
                    S.op("dve", lambda e, b=b, t=t, cb=cb: e.tensor_tensor(out=kwb[:, b, cb * 512:(cb + 1) * 512].rearrange("p (h d) -> p h d", h=4), in0=kro[:, cb * 4:(cb + 1) * 4, :],
                                                                     in1=kwp[:, t, cb * 4:(cb + 1) * 4].unsqueeze(2).to_broadcast([128, 4, 128]), op=ALU.mult),
                         reads=["kro%d" % cb, "kwp"], writes=["kwb%d_%d" % (b, cb)])
                else:
                    vc = cb - 2
                    copy_op("act", vb[:, b, vc * 512:(vc + 1) * 512], psA[:, pb, :], ["psA%d" % pb], ["vb%d_%d" % (b, vc)])
                if cb == 0 and t > 0:
                    stateP(t - 1)

        S.op("pool", lambda e: e.memset(ztile[:], 0.0), writes=["ztile"])
        for bk in range(4):
            S.op("pe", lambda e, bk=bk: e.matmul(psM[:, bk, :], lhsT=ztile[:, 0:128], rhs=ztile[:, :], start=True, stop=False, skip_group_check=True),
                 reads=["ztile"], writes=["psMacc"], signal=(bk == 3))
        prepA(0)
        prepA(1)
        prepB(0)
        for t in range(NPRE):
            if t + 2 < NPRE:
                prepA(t + 2)
            if t + 1 < NPRE:
                prepB(t + 1)
            mainP(t)
        stateP(NPRE - 1)
        for h in range(8):
            S.op("dve", lambda e, h=h: e.tensor_copy(out=Spre[:, h, :], in_=psM[:, h // 2, (h % 2) * 256:(h % 2) * 256 + 256]), reads=["psMacc"], writes=["Spre%d" % h])
        S.barrier()
    bufA = sb(top, [128, NT, D], BF16, "bufA")
    bufB = sb(top, [128, 16, TOK], BF16, "bufB")

    def attn_gen():
        with ExitStack() as st7:
            oast = sb(st7, [128, 2, 1024], BF16, "oast")
            bufBf = bufB[:].rearrange("p a b -> p (a b)")
            qTa = bufA[0:64, :, :].rearrange("p t d -> p (t d)").rearrange("p (h c) -> p h c", h=16)
            kTa = bufBf[0:64, 0:5120].rearrange("p (h t) -> p h t", h=4)
            vaug = bufBf[:, 5120:7720].rearrange("p (t h d) -> p t h d", t=NTH, h=4)
            vmeta = sb(st7, [16, 4, 65], BF16, "vmeta")
            ropeA = sb(st7, [128, NTH, 2, 8], F32, "ropeA")
            gqk = sb(st7, [128, 20, 64], F32, "gqk")
            esink = sb(st7, [128, 16], F32, "esink")
            amask = sb(st7, [128, 3, 128], F32, "amask")
            amb = sb(st7, [128, 3, 128], BF16, "amb")
            smn = sb(st7, [128, 64], F32, "smn")
            smc = sb(st7, [128, 16], F32, "smc")
            smnb = sb(st7, [128, 64], BF16, "smnb")
            smcb = sb(st7, [128, 16], BF16, "smcb")
            k32 = sb(st7, [128, 2, 4, 64], F32, "k32")
            S.dma("sp", ropeA[:], ropeA_in, writes=["ropeA"])
            S.dma("sp", gqk[:].rearrange("p h d -> p (h d)"), gqk_in[0].partition_broadcast(128), writes=["gqk"])
            S.dma("sp", esink[:], esink_in[0].partition_broadcast(128), writes=["esink"])
            S.op("act", lambda e: e.activation(out=esink[:], in_=esink[:], func=AF.Exp), reads=["esink"], writes=["esink"])
            S.dma("sp", amask[:], amask_in, writes=["amask"])
            S.dma("sp", smn[:], smn_in, writes=["smn"])
            S.dma("sp", smc[:], smc_in, writes=["smc"])
            S.op("dve", lambda e: e.tensor_copy(out=amb[:], in_=amask[:]), reads=["amask"], writes=["amb"])
            S.op("dve", lambda e: e.tensor_copy(out=smnb[:], in_=smn[:]), reads=["smn"], writes=["smnb"])
            S.op("dve", lambda e: e.tensor_copy(out=smcb[:], in_=smc[:]), reads=["smc"], writes=["smcb"])
            S.op("pool", lambda e: e.memset(vaug[:], 1.0), writes=["vaug%d" % t for t in range(NTH)])
            with ExitStack() as stp:
                ua = sb(stp, [128, 2, 1536], F32, "ua")
                sqa = sb(stp, [128, 20, 64], F32, "sqa")
                xna = sb(stp, [128, 20, 64], F32, "xna")
                ssa = sb(stp, [128, 20], F32, "ssa")
                r1 = sb(stp, [128, 20, 8], F32, "r1")
                r2 = sb(stp, [128, 20, 8], F32, "r2")
                qkb = sb(stp, [128, 2, 20, 64], BF16, "qkb")
                for t in range(NTH):
                    b = t % 2
                    rows = slice(t * 128, (t + 1) * 128)
                    h0 = 0 if t < 9 else 16
                    c0 = 0 if t < 9 else 1024
                    def _ua_load(tt):
                        cc0 = 0 if tt < 9 else 1024
                        S.dma("sp", ua[:, tt % 2, cc0:1536], U[tt * 128:(tt + 1) * 128, cc0:1536], writes=["ua%d" % (tt % 2)])
                    if t == 0:
                        _ua_load(0)
                    if t + 1 < NTH:
                        _ua_load(t + 1)
                    xq = ua[:, b, 0:1280].rearrange("p (h d) -> p h d", d=64)[:, h0:20, :]
                    nh = 20 - h0
                    S.op("act", lambda e, xq=xq, h0=h0: e.activation(out=sqa[:, h0:20, :], in_=xq, func=AF.Square), reads=["ua%d" % b], writes=["sqa"])
                    S.op("dve", lambda e, h0=h0: e.reduce_sum(out=ssa[:, h0:20], in_=sqa[:, h0:20, :], axis=AX.X), reads=["sqa"], writes=["ssa"])
                    S.op("dve", lambda e, h0=h0: e.tensor_scalar(out=ssa[:, h0:20], in0=ssa[:, h0:20], scalar1=1.0 / 64, scalar2=EPS, op0=ALU.mult, op1=ALU.add), reads=["ssa"], writes=["ssa"])
                    S.op("act", lambda e, h0=h0: e.activation(out=ssa[:, h0:20], in_=ssa[:, h0:20], func=AF.Sqrt), reads=["ssa"], writes=["ssa"])
                    S.op("dve", lambda e, h0=h0: e.reciprocal(out=ssa[:, h0:20], in_=ssa[:, h0:20]), reads=["ssa"], writes=["ssa"])
                    S.op("dve", lambda e, xq=xq, h0=h0, nh=nh: e.tensor_tensor(out=xna[:, h0:20, :], in0=xq, in1=ssa[:, h0:20].unsqueeze(2).to_broadcast([128, nh, 64]), op=ALU.mult),
                         reads=["ua%d" % b, "ssa"], writes=["xna"])
                    S.op("dve", lambda e, h0=h0: e.tensor_tensor(out=xna[:, h0:20, :], in0=xna[:, h0:20, :], in1=gqk[:, h0:20, :], op=ALU.mult), reads=["xna", "gqk"], writes=["xna"])
                    cosb = ropeA[:, t, 0, :].unsqueeze(1).to_broadcast([128, nh, 8])
                    sinb = ropeA[:, t, 1, :].unsqueeze(1).to_broadcast([128, nh, 8])
                    x1 = xna[:, h0:20, 0:8]
                    x2 = xna[:, h0:20, 8:16]
                    S.op("dve", lambda e, h0=h0, b=b: e.tensor_copy(out=qkb[:, b, h0:20, :], in_=xna[:, h0:20, :]), reads=["xna"], writes=["qkb%d" % b])
                    if t in (7, 8):
                        ki = t - 7
                        S.op("dve", lambda e, ki=ki: e.tensor_copy(out=k32[:, ki, :, :], in_=xna[:, 16:20, :]), reads=["xna"], writes=["k32_%d" % ki])
                    S.op("dve", lambda e, x1=x1, cosb=cosb, h0=h0: e.tensor_tensor(out=r1[:, h0:20, :], in0=x1, in1=cosb, op=ALU.mult), reads=["xna", "ropeA"], writes=["r1"])
                    S.op("dve", lambda e, x2=x2, sinb=sinb, h0=h0: e.tensor_tensor(out=r2[:, h0:20, :], in0=x2, in1=sinb, op=ALU.mult), reads=["xna", "ropeA"], writes=["r2"])
                    S.op("dve", lambda e, h0=h0, b=b: e.tensor_tensor(out=qkb[:, b, h0:20, 0:8], in0=r1[:, h0:20, :], in1=r2[:, h0:20, :], op=ALU.subtract), reads=["r1", "r2"], writes=["qkb%d" % b])
                    if t in (7, 8):
                        S.op("dve", lambda e, ki=ki: e.tensor_tensor(out=k32[:, ki, :, 0:8], in0=r1[:, 16:20, :], in1=r2[:, 16:20, :], op=ALU.subtract), reads=["r1", "r2"], writes=["k32_%d" % ki])
                    S.op("dve", lambda e, x2=x2, cosb=cosb, h0=h0: e.tensor_tensor(out=r1[:, h0:20, :], in0=x2, in1=cosb, op=ALU.mult), reads=["xna", "ropeA"], writes=["r1"])
                    S.op("dve", lambda e, x1=x1, sinb=sinb, h0=h0: e.tensor_tensor(out=r2[:, h0:20, :], in0=x1, in1=sinb, op=ALU.mult), reads=["xna", "ropeA"], writes=["r2"])
                    S.op("dve", lambda e, h0=h0, b=b: e.tensor_tensor(out=qkb[:, b, h0:20, 8:16], in0=r1[:, h0:20, :], in1=r2[:, h0:20, :], op=ALU.add), reads=["r1", "r2"], writes=["qkb%d" % b])
                    if t in (7, 8):
                        S.op("dve", lambda e, ki=ki: e.tensor_tensor(out=k32[:, ki, :, 8:16], in0=r1[:, 16:20, :], in1=r2[:, 16:20, :], op=ALU.add), reads=["r1", "r2"], writes=["k32_%d" % ki])
                    copy_op("act", vaug[:, t, :, 0:64], ua[:, b, 1280:1536].rearrange("p (h d) -> p h d", d=64), ["ua%d" % b], ["vaug%d" % t])
                    cols = slice(t * 128, (t + 1) * 128)
                    yield 3
                    if t < 9:
                        for j in range(16):
                            S.op("pe", lambda e, b=b, j=j: e.transpose(out=psT[0:64, j // 8, j % 8, :], in_=qkb[:, b, j, :], identity=identb[:]),
                                 reads=["qkb%d" % b, "identb"], writes=["psT%d" % (j // 8)], signal=(j % 8 == 7))
                        for hh in range(2):
                            copy_op(evq(), qTa[:, hh * 8:(hh + 1) * 8, cols], psT[0:64, hh, :, :], ["psT%d" % hh], ["qTa%d" % t])
                    for j in range(4):
                        S.op("pe", lambda e, b=b, j=j: e.transpose(out=psT[0:64, 0, j, :], in_=qkb[:, b, 16 + j, :], identity=identb[:]),
                             reads=["qkb%d" % b, "identb"], writes=["psT0"], signal=(j == 3))
                    copy_op(evq(), kTa[:, :, cols], psT[0:64, 0, 0:4, :], ["psT0"], ["kTa%d" % t])
                    yield
                S.dma("sp", wk_p, k32[:, 0, :, :].rearrange("p h d -> p (h d)"), reads=["k32_0"])
                S.dma("sp", wv_p, U[7 * 128:8 * 128, C_VA:C_VA + 256], reads=["U7_2"])
                S.dma("sp", wk_s[:, 0:124, :], cache_k[:, 4:128, :])
                S.dma("sp", wv_s[:, 0:124, :], cache_v[:, 4:128, :])
                for bq in range(16):
                    S.dma("sp", wk_s[bq, 124:128, :], k32[bq * 4:(bq + 1) * 4, 1, :, :].rearrange("p h d -> p (h d)"), reads=["k32_1"])
                    S.dma("sp", wv_s[bq, 124:128, :], U[8 * 128 + bq * 4:8 * 128 + (bq + 1) * 4, C_VA:C_VA + 256], reads=["U8_2"])
                S.dma("sp", vmeta[:], vaug[64:80, 8, :, :], reads=["vaug8"], writes=["vmeta"])
                S.barrier()
            with ExitStack() as stc:
                PT = sb(stc, [128, 2, 3, 512], BF16, "PT")
                den = sb(stc, [128, 2, 4], F32, "den")
                it = 0
                for t in range(8):
                    tp = t - 1 if t > 0 else 9
                    cols = slice(t * 128, (t + 1) * 128)
                    pcols = slice(tp * 128, (tp + 1) * 128)
                    for h in range(4):
                        pb = it % 2
                        it += 1
                        qrhs = qTa[:, 4 * h:4 * h + 4, cols]
                        qk_ = ["qTa%d" % t]
                        S.op("pe", lambda e, h=h, cols=cols, qrhs=qrhs: e.matmul(psM[:, 0, :], lhsT=kTa[:, h, cols], rhs=qrhs, start=True, stop=True), reads=qk_ + ["kTa%d" % t], writes=["psM0"])
                        S.op("pe", lambda e, h=h, pcols=pcols, qrhs=qrhs: e.matmul(psM[:, 1, :], lhsT=kTa[:, h, pcols], rhs=qrhs, start=True, stop=True), reads=qk_ + ["kTa%d" % tp], writes=["psM1"])
                        S.op("pe", lambda e, h=h, qrhs=qrhs: e.matmul(psM[0:16, 2, :], lhsT=kTa[:, h, 1024 + 64:1024 + 80], rhs=qrhs, start=True, stop=True), reads=qk_ + ["kTa8"], writes=["psM2"])
                        for j in range(3):
                            np_ = 16 if j == 2 else 128
                            S.op("act", lambda e, j=j, pb=pb, np_=np_: e.activation(out=PT[0:np_, pb, j, :], in_=psM[0:np_, j, :], func=AF.Exp, scale=0.125),
                                 reads=["psM%d" % j], writes=["PT%d_%d" % (pb, j)])
                        for j in range(2):
                            mi = j if (j == 0 or t > 0) else 2
                            pv = PT[:, pb, j, :].rearrange("p (g q) -> p g q", g=4)
                            S.op("dve", lambda e, pv=pv, mi=mi: e.tensor_tensor(out=pv, in0=pv, in1=amb[:, mi, :].unsqueeze(1).to_broadcast([128, 4, 128]), op=ALU.mult),
                                 reads=["PT%d_%d" % (pb, j), "amb"], writes=["PT%d_%d" % (pb, j)])
                        yield 1
                        for g in range(4):
                            gc = slice(g * 128, (g + 1) * 128)
                            oap = psM[:, 3, g * 65:(g + 1) * 65]
                            S.op("pe", lambda e, oap=oap, pb=pb, gc=gc, t=t, h=h: e.matmul(oap, lhsT=PT[:, pb, 0, gc], rhs=vaug[:, t, h, :], start=True, stop=False),
                                 reads=["PT%d_0" % pb, "vaug%d" % t], writes=["psM3"], signal=False)
                            S.op("pe", lambda e, oap=oap, pb=pb, gc=gc, tp=tp, h=h: e.matmul(oap, lhsT=PT[:, pb, 1, gc], rhs=vaug[:, tp, h, :], start=False, stop=False),
                                 reads=["PT%d_1" % pb, "vaug%d" % tp], writes=["psM3"], signal=False)
                            S.op("pe", lambda e, oap=oap, pb=pb, gc=gc, h=h: e.matmul(oap, lhsT=PT[0:16, pb, 2, gc], rhs=vmeta[:, h, :], start=False, stop=True),
                                 reads=["PT%d_2" % pb, "vmeta"], writes=["psM3"], signal=(g == 3))
                        ov = psM[:, 3, 0:260].rearrange("p (g d) -> p g d", g=4)
                        S.op("dve", lambda e, ov=ov, pb=pb, h=h: e.tensor_tensor(out=den[:, pb, :], in0=ov[:, :, 64], in1=esink[:, 4 * h:4 * h + 4], op=ALU.add), reads=["psM3", "esink"], writes=["den%d" % pb])
                        S.op("dve", lambda e, pb=pb: e.reciprocal(out=den[:, pb, :], in_=den[:, pb, :]), reads=["den%d" % pb], writes=["den%d" % pb])
                        S.op("dve", lambda e, ov=ov, pb=pb, t=t, h=h: e.tensor_tensor(out=oast[:, t % 2, h * 256:(h + 1) * 256].rearrange("p (g d) -> p g d", g=4), in0=ov[:, :, 0:64],
                                                                                      in1=den[:, pb, :].unsqueeze(2).to_broadcast([128, 4, 64]), op=ALU.mult),
                             reads=["psM3", "den%d" % pb], writes=["oast%d" % (t % 2)])
                        yield
                    S.dma("sp", OA[t * 128:(t + 1) * 128, :], oast[:, t % 2, :], reads=["oast%d" % (t % 2)])
                S.barrier()
            with ExitStack() as sts:
                Kc = sb(sts, [128, 16, 256], BF16, "Kc")
                Vc = sb(sts, [128, 16, 4, 65], BF16, "Vc")
                KcT = sb(sts, [64, 16, 128], BF16, "KcT")
                PTn = sb(sts, [128, 256], BF16, "PTn")
                PTc = sb(sts, [128, 256], BF16, "PTc")
                OT = sb(sts, [128, 256], F32, "OT")
                den2 = sb(sts, [128, 4], F32, "den2")
                S.op("pool", lambda e: e.memset(Vc[:], 1.0), writes=["Vc"])
                S.dma("pool", Kc[:], cache_k.rearrange("b p c -> p b c"), writes=["Kc"])
                for bq in range(16):
                    S.dma("pool", Vc[:, bq, :, 0:64], cache_v[bq].rearrange("p (h d) -> p h d", h=4), reads=["Vc"], writes=["Vc"])
                S.op("pool", lambda e: e.memset(oast[:, 0, :], 0.0), writes=["oast0"])
                for h in range(4):
                    S.op("pe", lambda e, h=h: e.matmul(psM[0:80, 0, 0:256], lhsT=kTa[:, h, 1024:1104], rhs=qTa[:, 4 * h:4 * h + 4, 1024:1088], start=True, stop=True),
                         reads=["kTa8", "qTa8"], writes=["psM0"])
                    S.op("act", lambda e: e.activation(out=PTn[0:80, :], in_=psM[0:80, 0, 0:256], func=AF.Exp, scale=0.125), reads=["psM0"], writes=["PTn"])
                    pnv = PTn[0:80, :].rearrange("p (g q) -> p g q", g=4)
                    S.op("dve", lambda e, pnv=pnv: e.tensor_tensor(out=pnv, in0=pnv, in1=smnb[0:80, :].unsqueeze(1).to_broadcast([80, 4, 64]), op=ALU.mult), reads=["PTn", "smnb"], writes=["PTn"])
                    for bq in range(16):
                        S.op("pe", lambda e, bq=bq, h=h: e.transpose(out=psT[0:64, bq // 8, bq % 8, :], in_=Kc[:, bq, h * 64:(h + 1) * 64], identity=identb[:]),
                             reads=["Kc", "identb"], writes=["psT%d" % (bq // 8)], signal=(bq % 8 == 7))
                    for hh in range(2):
                        copy_op(evq(), KcT[:, hh * 8:(hh + 1) * 8, :], psT[0:64, hh, :, :], ["psT%d" % hh], ["KcT"])
                    yield 1
                    for bq in range(16):
                        S.op("pe", lambda e, bq=bq, h=h: e.matmul(psM[:, 1, bq * 16:(bq + 1) * 16], lhsT=KcT[:, bq, :],
                                                                  rhs=qTa[:, 4 * h:4 * h + 4, 1024 + 4 * bq:1024 + 4 * bq + 4], start=True, stop=True),
                             reads=["KcT", "qTa8"], writes=["psM1"], signal=(bq == 15))
                    S.op("act", lambda e: e.activation(out=PTc[:], in_=psM[:, 1, 0:256], func=AF.Exp, scale=0.125), reads=["psM1"], writes=["PTc"])
                    pcv = PTc[:].rearrange("p (b k) -> p b k", b=16)
                    S.op("dve", lambda e, pcv=pcv: e.tensor_tensor(out=pcv, in0=pcv, in1=smcb[:].unsqueeze(1).to_broadcast([128, 16, 16]), op=ALU.mult), reads=["PTc", "smcb"], writes=["PTc"])
                    yield 1
                    S.op("pe", lambda e, h=h: e.matmul(psM[0:65, 2, 0:256], lhsT=vaug[0:80, 8, h, :], rhs=PTn[0:80, :].rearrange("p (g b i) -> p b g i", g=4, b=16), start=True, stop=False),
                         reads=["vaug8", "PTn"], writes=["psM2"], signal=False)
                    for bq in range(16):
                        S.op("pe", lambda e, bq=bq, h=h: e.matmul(psM[0:65, 2, bq * 16:(bq + 1) * 16], lhsT=Vc[:, bq, h, :],
                                                                  rhs=PTc[:, bq * 16:(bq + 1) * 16], start=False, stop=(bq == 15)),
                             reads=["Vc", "PTc"], writes=["psM2"], signal=(bq == 15))
                    copy_op("act", OT[0:65, :].rearrange("p (g b i) -> p g b i", g=4, b=16), psM[0:65, 2, 0:256].rearrange("p (b g i) -> p g b i", b=16, g=4), ["psM2"], ["OT"])
                    yield 1
                    for g in range(4):
                        S.op("pe", lambda e, g=g: e.transpose(out=psM[0:64, 3, g * 65:(g + 1) * 65], in_=OT[0:65, g * 64:(g + 1) * 64], identity=ident32[0:65, 0:65]),
                             reads=["OT", "ident32"], writes=["psM3"], signal=(g == 3))
                    ov = psM[0:64, 3, 0:260].rearrange("p (g d) -> p g d", g=4)
                    S.op("dve", lambda e, ov=ov, h=h: e.tensor_tensor(out=den2[0:64, :], in0=ov[:, :, 64], in1=esink[0:64, 4 * h:4 * h + 4], op=ALU.add), reads=["psM3", "esink"], writes=["den2"])
                    S.op("dve", lambda e: e.reciprocal(out=den2[0:64, :], in_=den2[0:64, :]), reads=["den2"], writes=["den2"])
                    S.op("dve", lambda e, ov=ov, h=h: e.tensor_tensor(out=oast[0:64, 0, h * 256:(h + 1) * 256].rearrange("p (g d) -> p g d", g=4), in0=ov[:, :, 0:64],
                                                                      in1=den2[0:64, :].unsqueeze(2).to_broadcast([64, 4, 64]), op=ALU.mult),
                         reads=["psM3", "den2"], writes=["oast0"])
                    yield
                S.dma("sp", OA[8 * 128:9 * 128, :], oast[:, 0, :], reads=["oast0"])
                S.barrier()
        yield

    S.set_phase(2)
    with ExitStack() as st:
        xnT = sb(st, [128, 16, NTH * 128], BF16, "xnT")
        with ExitStack() as st1:
            xst = sb(st1, [128, 2, D], F32, "xst")
            gbc = sb(st1, [128, D], F32, "gbc")
            xnb = sb(st1, [128, 2, D], BF16, "xnb")
            sqj = sb(st1, [128, D], BF16, "sqj")
            ss = sb(st1, [128, NTH], F32, "ss")
            rstd = sb(st1, [128, NTH], F32, "rstd")
            S.dma("sp", gbc[:], norm_g[0].partition_broadcast(128), writes=["gbc"])
            for t in range(NTH):
                b = t % 2
                if t == 0:
                    S.dma("sp", xst[:, 0, :], x_all[0], writes=["xst0"])
                if t + 1 < NTH:
                    S.dma("sp", xst[:, (t + 1) % 2, :], x_all[t + 1], writes=["xst%d" % ((t + 1) % 2)])
                S.op("act", lambda e, b=b, t=t: e.activation(out=sqj[:], in_=xst[:, b, :], func=AF.Square, accum_out=ss[:, t:t + 1]),
                     reads=["xst%d" % b], writes=["sqj", "ss%d" % t])
                S.op("dve", lambda e, t=t: e.tensor_scalar(out=rstd[:, t:t + 1], in0=ss[:, t:t + 1], scalar1=1.0 / D, scalar2=EPS, op0=ALU.mult, op1=ALU.add),
                     reads=["ss%d" % t], writes=["rstd%d" % t])
                S.op("act", lambda e, t=t: e.activation(out=rstd[:, t:t + 1], in_=rstd[:, t:t + 1], func=AF.Sqrt), reads=["rstd%d" % t], writes=["rstd%d" % t])
                S.op("dve", lambda e, t=t: e.reciprocal(out=rstd[:, t:t + 1], in_=rstd[:, t:t + 1]), reads=["rstd%d" % t], writes=["rstd%d" % t])
                S.op("dve", lambda e, b=b, t=t: e.scalar_tensor_tensor(out=xnb[:, b, :], in0=xst[:, b, :], scalar=rstd[:, t:t + 1], in1=gbc[:], op0=ALU.mult, op1=ALU.mult),
                     reads=["xst%d" % b, "rstd%d" % t, "gbc"], writes=["xnb%d" % b])
                for kc in range(16):
                    S.op("pe", lambda e, b=b, kc=kc: e.transpose(out=psT[:, kc // 8, kc % 8, :], in_=xnb[:, b, kc * 128:(kc + 1) * 128], identity=identb[:]),
                         reads=["xnb%d" % b, "identb"], writes=["psT%d" % (kc // 8)], signal=(kc % 8 == 7))
                for hh in range(2):
                    copy_op(evq(), xnT[:, hh * 8:(hh + 1) * 8, t * 128:(t + 1) * 128], psT[:, hh, :, :], ["psT%d" % hh], ["xnT%d" % t])
            S.barrier()
        S.set_phase(3)
        with ExitStack() as st2:
            ust = sb(st2, [128, 4, 512], F32, "ust")
            W = sb(st2, [128, 2, 16, 512], BF16, "W")
            Wh["W"] = W
            nblk = INW // 512

            def inproj_gen():
                bufs = {0: load_w(w_in, 0, 16)}
                it = 0
                for c in range(nblk):
                    if c + 1 < nblk:
                        bufs[c + 1] = load_w(w_in, (c + 1) * 512, 16)
                    wb = bufs[c]
                    tiles = list(range(NT)) + ([9] if c == 2 else [])
                    for t in tiles:
                        pb = it % 2
                        for kc in range(16):
                            S.op("pe", lambda e, pb=pb, kc=kc, t=t, wb=wb: e.matmul(psA[:, pb, :], lhsT=xnT[:, kc, t * 128:(t + 1) * 128], rhs=W[:, wb, kc, :], start=(kc == 0), stop=(kc == 15)),
                                 reads=["xnT%d" % t] + wkeys(wb, 16), writes=["psA%d" % pb], signal=(kc == 15))
                        ub = it % 4
                        copy_op("act", ust[:, ub, :], psA[:, pb, :], ["psA%d" % pb], ["ust%d" % ub])
                        S.dma("sp", U[t * 128:(t + 1) * 128, c * 512:(c + 1) * 512], ust[:, ub, :], reads=["ust%d" % ub], writes=["U%d_%d" % (t, c)])
                        it += 1
                        yield c

            gB = None
            alive = True
            wait = 0
            for c in inproj_gen():
                if c >= 3 and OVERLAP:
                    if gB is None:
                        S.barrier()
                        gB = attn_gen()
                    wait -= 1
                    if alive and wait <= 0:
                        r = next(gB, "END")
                        if r == "END":
                            alive = False
                        else:
                            wait = r or 1
            if gB is None:
                gB = attn_gen()
            for _ in gB:
                pass
            S.barrier()

    with ExitStack() as st:
        RT = bufB
        mr = bufA
        Sfin_keep = None
        with ExitStack() as stR:
            olocal = bufA
            qgall = sb(stR, [128, 8, 1024], BF16, "qgall")
            Send = sb(stR, [128, 8, 256], F32, "Send")
            Sm = sb(stR, [128, 8, 256], F32, "Sm")
            with ExitStack() as st3:
                qT = bufB[:, 0:8, :]
                kT = bufB[:, 8:16, :]
                kw = sb(st3, [128, NT, 1024], BF16, "kw")
                vtm = sb(st3, [128, NT, D], BF16, "vtm")
                qgs = sb(st3, [128, 8, 4], F32, "qgs")
                bsel = sb(st3, [128, 16], F32, "bsel")
                S.dma("sp", qgs[:], qgs_in, writes=["qgs"])
                S.dma("sp", bsel[:], bsel_in, writes=["bsel"])
                S.set_phase(4)
                with ExitStack() as stp:
                    uq = sb(stp, [128, 1, 2048], F32, "uq")
                    ropeR = sb(stp, [128, NT, 2, 64], F32, "ropeR")
                    kwsc = sb(stp, [128, NT, 8], F32, "kwsc")
                    qgt = sb(stp, [128, 8, 128], F32, "qgt")
                    S.dma("sp", ropeR[:], ropeR_in, writes=["ropeR"])
                    S.dma("sp", kwsc[:], kwsc_in, writes=["kwsc"])
                    S.dma("sp", qgt[:], qgt_in, writes=["qgt"])
                    ta = sb(stp, [128, 16, 64], F32, "ta")
                    tb = sb(stp, [128, 16, 64], F32, "tb")
                    qkr = sb(stp, [128, 1, 16, 128], BF16, "qkr")
                    for t in range(NT):
                        b = 0
                        rows = slice(t * 128, (t + 1) * 128)
                        S.dma("sp", uq[:, b, :], U[rows, C_QR:C_QR + 2048], writes=["uq%d" % b])
                        for vq in range(4 if SUB != "a" else 0):
                            S.dma("pool", vtm[:, t, vq * 512:(vq + 1) * 512], U[rows, C_VR + vq * 512:C_VR + (vq + 1) * 512], writes=["vtm%d_%d" % (t, vq)])
                        xv = uq[:, b, :].rearrange("p (h two d) -> p h two d", h=16, two=2)
                        x1 = xv[:, :, 0, :]
                        x2 = xv[:, :, 1, :]
                        cosb = ropeR[:, t, 0, :].unsqueeze(1).to_broadcast([128, 16, 64])
                        sinb = ropeR[:, t, 1, :].unsqueeze(1).to_broadcast([128, 16, 64])
                        ov = qkr[:, b, :, :].rearrange("p h (two d) -> p h two d", two=2)
                        rk = ["uq%d" % b, "ropeR"]
                        S.op("dve", lambda e, x1=x1, cosb=cosb: e.tensor_tensor(out=ta[:], in0=x1, in1=cosb, op=ALU.mult), reads=rk, writes=["ta"])
                        S.op("dve", lambda e, x2=x2, sinb=sinb: e.tensor_tensor(out=tb[:], in0=x2, in1=sinb, op=ALU.mult), reads=rk, writes=["tb"])
                        S.op("dve", lambda e, ov=ov: e.tensor_tensor(out=ov[:, :, 0, :], in0=ta[:], in1=tb[:], op=ALU.subtract), reads=["ta", "tb"], writes=["qkr%d" % b])
                        S.op("dve", lambda e, x2=x2, cosb=cosb: e.tensor_tensor(out=ta[:], in0=x2, in1=cosb, op=ALU.mult), reads=rk, writes=["ta"])
                        S.op("dve", lambda e, x1=x1, sinb=sinb: e.tensor_tensor(out=tb[:], in0=x1, in1=sinb, op=ALU.mult), reads=rk, writes=["tb"])
                        S.op("dve", lambda e, ov=ov: e.tensor_tensor(out=ov[:, :, 1, :], in0=ta[:], in1=tb[:], op=ALU.add), reads=["ta", "tb"], writes=["qkr%d" % b])
                        S.op("dve", lambda e, b=b, t=t: e.tensor_tensor(out=kw[:, t, :].rearrange("p (h d) -> p h d", h=8), in0=qkr[:, b, 8:16, :],
                                                                        in1=kwsc[:, t, :].unsqueeze(2).to_broadcast([128, 8, 128]), op=ALU.mult),
                             reads=["qkr%d" % b, "kwsc"], writes=["kw%d" % t])
                        if SUB == "b":
                            continue
                        for j in range(16):
                            S.op("pe", lambda e, b=b, j=j: e.transpose(out=psT[:, j // 8, j % 8, :], in_=qkr[:, b, j, :], identity=identb[:]),
                                 reads=["qkr%d" % b, "identb"], writes=["psT%d" % (j // 8)], signal=(j % 8 == 7))
                        cols = slice(t * 128, (t + 1) * 128)
                        copy_op("act", qT[:, :, cols], psT[:, 0, :, :], ["psT0"], ["qT%d" % t])
                        if t < 8:
                            S.op("dve", lambda e, cols=cols: e.tensor_tensor(out=qgall[:, :, cols], in0=qT[:, :, cols], in1=qgt[:], op=ALU.mult),
                                 reads=["qT%d" % t, "qgt"], writes=["qg%d" % t])
                        copy_op("act", kT[:, :, cols], psT[:, 1, :, :], ["psT1"], ["kT%d" % t])
                    S.barrier()
                S.set_phase(5)
                with ExitStack() as stc:
                    innerT = sb(stc, [128, 2, 128], BF16, "innerT")
                    dmask = sb(stc, [128, 8, 2, 128], F32, "dmask")
                    S.dma("sp", dmask[:], dmask_in, writes=["dmask"])
                    Slbf = sb(stc, [128, 8, 2, 256], BF16, "Slbf")
                    qz = sb(stc, [128, 16 * 68], F32, "qz")
                    kz = sb(stc, [128, 16, 128], BF16, "kz")
                    S32 = sb(stc, [128, 3, 4, 256], F32, "S32")
                    S.op("pool", lambda e: e.memset(qz[:], 0.0), writes=["qz"])
                    S.op("pool", lambda e: e.memset(olocal[:, 8, :], 0.0), writes=["ol8"])
                    qd_ctr = 0
                    for t in range(8):
                        cols = slice(t * 128, (t + 1) * 128)
                        vk = ["vtm%d_%d" % (t, i) for i in range(4)]
                        for h in range(8):
                            par = h % 2
                            ba, bc = 2 * par, 2 * par + 1
                            hc = slice(h * 256, (h + 1) * 256)
                            g128 = float(GAM[h] ** 128)
                            S.op("pe", lambda e, h=h, cols=cols, ba=ba: e.matmul(psM[:, ba, 0:128], lhsT=kT[:, h, cols], rhs=qT[:, h, cols], start=True, stop=True),
                                 reads=["kT%d" % t, "qT%d" % t], writes=["pb%d" % par])
                            S.op("dve", lambda e, h=h, par=par, ba=ba: e.tensor_tensor(out=innerT[:, par, :], in0=psM[:, ba, 0:128], in1=dmask[:, h, 0, :], op=ALU.mult),
                                 reads=["pb%d" % par, "dmask"], writes=["innerT%d" % par])
                            S.op("pe", lambda e, par=par, ba=ba, t=t, hc=hc: e.matmul(psM[:, ba, 128:384], lhsT=innerT[:, par, :], rhs=vtm[:, t, hc], start=True, stop=(t == 0)),
                                 reads=["innerT%d" % par] + vk, writes=["pb%d" % par], signal=(t == 0))
                            if t > 0:
                                S.op("pe", lambda e, h=h, cols=cols, ba=ba, t=t: e.matmul(psM[:, ba, 128:384], lhsT=qgall[:, h, cols], rhs=Slbf[:, h, t % 2, :], start=False, stop=True),
                                     reads=["qg%d" % t, "Slbf%d_%d" % (h, t % 2)], writes=["pb%d" % par])
                            copy_op("act", olocal[:, t, hc], psM[:, ba, 128:384], ["pb%d" % par], ["ol%d_%d" % (t, h)])
                            S.op("pe", lambda e, h=h, t=t, hc=hc, bc=bc: e.matmul(psM[:, bc, 0:256], lhsT=kw[:, t, h * 128:(h + 1) * 128], rhs=vtm[:, t, hc], start=True, stop=True),
                                 reads=["kw%d" % t] + vk, writes=["pc%d" % par])
                            if t == 0:
                                S.op("dve", lambda e, h=h, bc=bc: e.tensor_copy(out=Send[:, h, :], in_=psM[:, bc, 0:256]), reads=["pc%d" % par], writes=["Send%d" % h])
                            else:
                                S.op("dve", lambda e, h=h, g128=g128, bc=bc: e.scalar_tensor_tensor(out=Send[:, h, :], in0=Send[:, h, :], scalar=g128, in1=psM[:, bc, 0:256], op0=ALU.mult, op1=ALU.add),
                                     reads=["pc%d" % par, "Send%d" % h], writes=["Send%d" % h])
                            if t < 7:
                                copy_op("act", Slbf[:, h, (t + 1) % 2, :], Send[:, h, :], ["Send%d" % h], ["Slbf%d_%d" % (h, (t + 1) % 2)])
                    S.barrier()

                    def issue_state_load(idx):
                        hh_, qd_ = idx // 4, idx % 4
                        bi_ = idx % 3
                        src_ = state[qd_ * 4:(qd_ + 1) * 4, hh_, :, :].rearrange("b p d -> p b d")
                        S.dma("sp", S32[:, bi_, :, :], src_, writes=["S32_%d" % bi_])
                    issue_state_load(0)
                    issue_state_load(1)
                    for h in range(8):
                        hc = slice(h * 256, (h + 1) * 256)
                        g4 = float(GAM[h] ** 4)
                        cols8 = slice(1024, 1152)
                        S.op("pe", lambda e, h=h: e.matmul(psM[:, 0, 0:128], lhsT=kT[:, h, cols8], rhs=qT[:, h, cols8], start=True, stop=True),
                             reads=["kT8", "qT8"], writes=["psM0"])
                        S.op("dve", lambda e, h=h: e.tensor_tensor(out=innerT[:, 0, :], in0=psM[:, 0, 0:128], in1=dmask[:, h, 1, :], op=ALU.mult),
                             reads=["psM0", "dmask"], writes=["innerT0"])
                        S.op("dve", lambda e, h=h: e.tensor_tensor(out=qz[:].rearrange("p (b s) -> p b s", s=68)[:, :, 0:4],
                                                                   in0=qT[:, h, 1024:1088].rearrange("p (b i) -> p b i", i=4),
                                                                   in1=qgs[:, h, :].unsqueeze(1).to_broadcast([128, 16, 4]), op=ALU.mult),
                             reads=["qT8", "qgs"], writes=["qz"])
                        S.op("dve", lambda e, h=h: e.tensor_tensor(out=kz[0:64, :, :], in0=kw[0:64, 8, h * 128:(h + 1) * 128].unsqueeze(1).to_broadcast([64, 16, 128]),
                                                                   in1=bsel[0:64, :].unsqueeze(2).to_broadcast([64, 16, 128]), op=ALU.mult),
                             reads=["kw8", "bsel"], writes=["kz"])
                        S.op("pe", lambda e, hc=hc: e.matmul(psM[0:64, 1, 0:256], lhsT=innerT[:, 0, 0:64], rhs=vtm[:, 8, hc], start=True, stop=False),
                             reads=["innerT0", "vtm8_0", "vtm8_1", "vtm8_2", "vtm8_3"], writes=["psM1"], signal=False)
                        for qd in range(4):
                            sbuf_i = qd_ctr % 3
                            if qd_ctr + 2 < 32:
                                issue_state_load(qd_ctr + 2)
                            qd_ctr += 1
                            for bb in range(4):
                                b = qd * 4 + bb
                                last = (b == 15)
                                S.op("pe", lambda e, b=b, bb=bb, sbuf_i=sbuf_i, last=last: e.matmul(psM[0:64, 1, 0:256], lhsT=qz[:, b * 64:(b + 1) * 64], rhs=S32[:, sbuf_i, bb, :], start=False, stop=last),
                                     reads=["qz", "S32_%d" % sbuf_i], writes=["psM1"], signal=last)
                            for bb in range(4):
                                b = qd * 4 + bb
                                pci = b % 4
                                pcap = psM[:, 2 + pci, 0:256] if pci < 2 else psA[:, pci - 2, 0:256]
                                pck = "pcs%d" % pci
                                S.op("pe", lambda e, b=b, pcap=pcap, hc=hc: e.matmul(pcap, lhsT=kz[0:64, b, :], rhs=vtm[0:64, 8, hc], start=True, stop=True),
                                     reads=["kz", "vtm8_0", "vtm8_1", "vtm8_2", "vtm8_3"], writes=[pck])
                                S.op("dve", lambda e, sbuf_i=sbuf_i, bb=bb, pcap=pcap, g4=g4: e.scalar_tensor_tensor(out=S32[:, sbuf_i, bb, :], in0=S32[:, sbuf_i, bb, :], scalar=g4, in1=pcap, op0=ALU.mult, op1=ALU.add),
                                     reads=[pck, "S32_%d" % sbuf_i], writes=["S32_%d" % sbuf_i])
                            S.dma("act", ret_s[qd * 4:(qd + 1) * 4, h, :, :].rearrange("b p d -> p b d"), S32[:, sbuf_i, :, :], reads=["S32_%d" % sbuf_i])
                        copy_op("act", olocal[0:64, 8, hc], psM[0:64, 1, 0:256], ["psM1"], ["ol8"])
                        S.op("pe", lambda e, h=h, hc=hc: e.matmul(psM[:, 2, 0:256], lhsT=kw[64:80, 8, h * 128:(h + 1) * 128], rhs=vtm[64:80, 8, hc], start=True, stop=True),
                             reads=["kw8", "vtm8_0", "vtm8_1", "vtm8_2", "vtm8_3"], writes=["pcs0"])
                        S.op("dve", lambda e, h=h: e.tensor_copy(out=Sm[:, h, :], in_=psM[:, 2, 0:256]), reads=["pcs0"], writes=["Sm%d" % h])
                S.barrier()
            S.set_phase(6)
            with ExitStack() as st5:
                coef = sb(st5, [128, 9, 8], F32, "coef")
                Sst = sb(st5, [128, 8, 256], F32, "Sst")
                Sstb = sb(st5, [128, 8, 256], BF16, "Sstb")
                of32 = sb(st5, [128, 2, 8, 256], F32, "of32")
                sq32 = sb(st5, [128, 8, 256], F32, "sq32")
                zr = sb(st5, [128, 2, D], F32, "zr")
                gg = sb(st5, [128, D], F32, "gg")
                gb = sb(st5, [128, D], F32, "gb")
                Rb = sb(st5, [128, 2, D], BF16, "Rb")
                stt = sb(st5, [128, 2, 6, 8], F32, "stt")
                S.dma("sp", coef[:], coef_in, writes=["coef"])
                S.dma("sp", gg[:], gn_g[0].partition_broadcast(128), writes=["gg"])
                S.dma("sp", gb[:], gn_b[0].partition_broadcast(128), writes=["gb"])
                for h in range(8):
                    S.op("dve", lambda e, h=h: e.scalar_tensor_tensor(out=Sst[:, h, :], in0=Sm[:, h, :], scalar=coef[:, 8, h:h + 1], in1=Spre[:, h, :], op0=ALU.mult, op1=ALU.add),
                         reads=["Sm%d" % h, "coef"], writes=["Sst%d" % h])
                sk = ["Sst%d" % h for h in range(8)]
                copy_op("act", Sstb[:], Sst[:], sk, ["Sstb"])
                for h in range(8):
                    g1024 = float(GAM[h] ** 1024)
                    S.op("dve", lambda e, h=h, g1024=g1024: e.scalar_tensor_tensor(out=Send[:, h, :], in0=Sst[:, h, :], scalar=g1024, in1=Send[:, h, :], op0=ALU.mult, op1=ALU.add),
                         reads=["Sst%d" % h, "Send%d" % h], writes=["Send%d" % h])
                S.dma("sp", ret_p.rearrange("h p d -> p h d"), Send[:], reads=["Send%d" % h for h in range(8)])
                def p5A(t):
                    rows = slice(t * 128, (t + 1) * 128)
                    cols = slice(t * 128, (t + 1) * 128)
                    b2 = t % 2
                    ofk, zk, sk2 = "of32_%d" % b2, "zr%d" % b2, "stt%d" % b2
                    S.dma("sp", zr[:, b2, :], U[rows, C_ZR:C_ZR + 2048], writes=[zk])
                    if t < 8:
                        for h in range(8):
                            pm = h % 4
                            gt = float(GAM[h] ** (128 * t))
                            S.op("pe", lambda e, h=h, cols=cols, pm=pm: e.matmul(psM[:, pm, 0:256], lhsT=qgall[:, h, cols], rhs=Sstb[:, h, :], start=True, stop=True),
                                 reads=["qg%d" % t, "Sstb"], writes=["psM%d" % pm])
                            S.op("dve", lambda e, h=h, t=t, pm=pm, gt=gt, b2=b2: e.scalar_tensor_tensor(out=of32[:, b2, h, :], in0=psM[:, pm, 0:256], scalar=gt, in1=olocal[:, t, h * 256:(h + 1) * 256], op0=ALU.mult, op1=ALU.add),
                                 reads=["psM%d" % pm, "ol%d_%d" % (t, h)], writes=[ofk])
                    else:
                        S.op("dve", lambda e, b2=b2: e.tensor_copy(out=of32[:, b2, :, :].rearrange("p h d -> p (h d)"), in_=olocal[:, 8, :]), reads=["ol8"], writes=[ofk])
                    S.op("dve", lambda e, b2=b2: e.reduce_sum(out=stt[:, b2, 0, :], in_=of32[:, b2, :, :], axis=AX.X), reads=[ofk], writes=[sk2])
                    S.op("act", lambda e, b2=b2: e.activation(out=sq32[:], in_=of32[:, b2, :, :], func=AF.Square), reads=[ofk], writes=["sq32"])
                    S.op("dve", lambda e, b2=b2: e.reduce_sum(out=stt[:, b2, 1, :], in_=sq32[:], axis=AX.X), reads=["sq32", sk2], writes=[sk2])
                    S.op("dve", lambda e, b2=b2: e.tensor_scalar(out=stt[:, b2, 2, :], in0=stt[:, b2, 0, :], scalar1=1.0 / 256, scalar2=None, op0=ALU.mult), reads=[sk2], writes=[sk2])
                    S.op("dve", lambda e, b2=b2: e.tensor_tensor(out=stt[:, b2, 3, :], in0=stt[:, b2, 2, :], in1=stt[:, b2, 2, :], op=ALU.mult), reads=[sk2], writes=[sk2])
                    S.op("dve", lambda e, b2=b2: e.scalar_tensor_tensor(out=stt[:, b2, 4, :], in0=stt[:, b2, 1, :], scalar=1.0 / 256, in1=stt[:, b2, 3, :], op0=ALU.mult, op1=ALU.subtract), reads=[sk2], writes=[sk2])
                    S.op("dve", lambda e, b2=b2: e.tensor_scalar(out=stt[:, b2, 4, :], in0=stt[:, b2, 4, :], scalar1=GN_EPS, scalar2=None, op0=ALU.add), reads=[sk2], writes=[sk2])
                    S.op("act", lambda e, b2=b2: e.activation(out=stt[:, b2, 4, :], in_=stt[:, b2, 4, :], func=AF.Sqrt), reads=[sk2], writes=[sk2])
                    S.op("dve", lambda e, b2=b2: e.reciprocal(out=stt[:, b2, 4, :], in_=stt[:, b2, 4, :]), reads=[sk2], writes=[sk2])
                    S.op("dve", lambda e, b2=b2: e.scalar_tensor_tensor(out=stt[:, b2, 5, :], in0=stt[:, b2, 2, :], scalar=-1.0, in1=stt[:, b2, 4, :], op0=ALU.mult, op1=ALU.mult), reads=[sk2], writes=[sk2])

                def p5B(t):
                    rows = slice(t * 128, (t + 1) * 128)
                    cols = slice(t * 128, (t + 1) * 128)
                    b2 = t % 2
                    ofk, zk, sk2 = "of32_%d" % b2, "zr%d" % b2, "stt%d" % b2
                    for h in range(8):
                        S.op("act", lambda e, b2=b2, h=h: e.activation(out=of32[:, b2, h, :], in_=of32[:, b2, h, :], func=AF.Identity, bias=stt[:, b2, 5, h:h + 1], scale=stt[:, b2, 4, h:h + 1]),
                             reads=[ofk, sk2], writes=[ofk])
                    ofl = of32[:, b2, :, :].rearrange("p h d -> p (h d)")
                    S.op("pool", lambda e, ofl=ofl: e.tensor_tensor(out=ofl, in0=ofl, in1=gg[:], op=ALU.mult), reads=[ofk, "gg"], writes=[ofk])
                    S.op("pool", lambda e, ofl=ofl: e.tensor_tensor(out=ofl, in0=ofl, in1=gb[:], op=ALU.add), reads=[ofk, "gb"], writes=[ofk])
                    S.op("act", lambda e, b2=b2: e.activation(out=zr[:, b2, :], in_=zr[:, b2, :], func=AF.Silu), reads=[zk], writes=[zk])
                    rb = t % 2
                    S.op("dve", lambda e, ofl=ofl, rb=rb, b2=b2: e.tensor_tensor(out=Rb[:, rb, :], in0=ofl, in1=zr[:, b2, :], op=ALU.mult), reads=[ofk, zk], writes=["Rb%d" % rb])
                    for kc in range(16):
                        S.op("pe", lambda e, rb=rb, kc=kc: e.transpose(out=psT[:, kc // 8, kc % 8, :], in_=Rb[:, rb, kc * 128:(kc + 1) * 128], identity=identb[:]),
                             reads=["Rb%d" % rb, "identb"], writes=["psT%d" % (kc // 8)], signal=(kc % 8 == 7))
                    for hh in range(2):
                        copy_op("act", RT[:, hh * 8:(hh + 1) * 8, cols], psT[:, hh, :, :], ["psT%d" % hh], ["RT%d" % t])

                p5A(0)
                for t in range(NT):
                    if t + 1 < NT:
                        p5A(t + 1)
                    p5B(t)
                S.barrier()
        S.set_phase(7)
        with ExitStack() as st6:
            gr = sb(st6, [128, 2, 512], F32, "gr")
            W = sb(st6, [128, 2, 16, 512], BF16, "W")
            Wh["W"] = W
            bufs = {0: load_w(w_pr, 0, 16)}
            it = 0
            for c in range(4):
                if c + 1 < 4:
                    bufs[c + 1] = load_w(w_pr, (c + 1) * 512, 16)
                wb = bufs[c]
                for t in range(NT):
                    pb = it % 2
                    rows = slice(t * 128, (t + 1) * 128)
                    def _gr_load(cc, tt, ii):
                        S.dma("sp", gr[:, ii % 2, :], U[tt * 128:(tt + 1) * 128, C_GR + cc * 512:C_GR + (cc + 1) * 512], writes=["gr%d" % (ii % 2)])
                    if it == 0:
                        _gr_load(0, 0, 0)
                    nxt = (c, t + 1) if t + 1 < NT else ((c + 1, 0) if c + 1 < 4 else None)
                    if nxt is not None:
                        _gr_load(nxt[0], nxt[1], it + 1)
                    S.op("act", lambda e, pb=pb: e.activation(out=gr[:, pb, :], in_=gr[:, pb, :], func=AF.Sigmoid), reads=["gr%d" % pb], writes=["gr%d" % pb])
                    for kc in range(16):
                        S.op("pe", lambda e, pb=pb, kc=kc, t=t, wb=wb: e.matmul(psA[:, pb, :], lhsT=RT[:, kc, t * 128:(t + 1) * 128], rhs=W[:, wb, kc, :], start=(kc == 0), stop=(kc == 15)),
                             reads=["RT%d" % t] + wkeys(wb, 16), writes=["psA%d" % pb], signal=(kc == 15))
                    S.op("dve", lambda e, pb=pb, t=t, c=c: e.tensor_tensor(out=mr[:, t, c * 512:(c + 1) * 512], in0=psA[:, pb, :], in1=gr[:, pb, :], op=ALU.mult),
                         reads=["psA%d" % pb, "gr%d" % pb], writes=["mr%d_%d" % (t, c)])
                    it += 1
            S.barrier()

        with ExitStack() as stA:
            AT = bufB[:, 0:8, :]
            with ExitStack() as st7:
                S.set_phase(11)
                with ExitStack() as st9:
                    za = sb(st9, [128, 2, 1024], F32, "za")
                    Ab = sb(st9, [128, 2, 1024], BF16, "Ab")
                    oab = sb(st9, [128, 2, 1024], BF16, "oab")
                    for t in range(NT):
                        b = t % 2
                        rows = slice(t * 128, (t + 1) * 128)
                        cols = slice(t * 128, (t + 1) * 128)
                        def _za_load(tt):
                            rr = slice(tt * 128, (tt + 1) * 128)
                            S.dma("sp", za[:, tt % 2, :], U[rr, C_ZA:C_ZA + 1024], writes=["za%d" % (tt % 2)])
                            S.dma("sp", oab[:, tt % 2, :], OA[rr, :], writes=["oab%d" % (tt % 2)])
                        if t == 0:
                            _za_load(0)
                        if t + 1 < NT:
                            _za_load(t + 1)
                        S.op("act", lambda e, b=b: e.activation(out=za[:, b, :], in_=za[:, b, :], func=AF.Silu), reads=["za%d" % b], writes=["za%d" % b])
                        S.op("dve", lambda e, b=b, t=t: e.tensor_tensor(out=Ab[:, b, :], in0=oab[:, b, :], in1=za[:, b, :], op=ALU.mult), reads=["oab%d" % b, "za%d" % b], writes=["Ab%d" % b])
                        for kc in range(8):
                            S.op("pe", lambda e, b=b, kc=kc: e.transpose(out=psT[:, 0, kc, :], in_=Ab[:, b, kc * 128:(kc + 1) * 128], identity=identb[:]),
                                 reads=["Ab%d" % b, "identb"], writes=["psT0"], signal=(kc == 7))
                        copy_op(evq(), AT[:, :, cols], psT[:, 0, :, :], ["psT0"], ["AT%d" % t])
                S.barrier()
            S.set_phase(12)
            with ExitStack() as st9b:
                ga = sb(st9b, [128, 2, 512], F32, "ga")
                W = sb(st9b, [128, 2, 16, 512], BF16, "W")
                Wh["W"] = W
                tmpm = sb(st9b, [128, 2, 512], F32, "tmpm")
                bufs = {0: load_w(w_pa, 0, 8)}
                it = 0
                for c in range(4):
                    if c + 1 < 4:
                        bufs[c + 1] = load_w(w_pa, (c + 1) * 512, 8)
                    wb = bufs[c]
                    for t in range(NT):
                        pb = it % 2
                        rows = slice(t * 128, (t + 1) * 128)
                        def _ga_load(cc, tt, ii):
                            S.dma("sp", ga[:, ii % 2, :], U[tt * 128:(tt + 1) * 128, C_GA + cc * 512:C_GA + (cc + 1) * 512], writes=["ga%d" % (ii % 2)])
                        if it == 0:
                            _ga_load(0, 0, 0)
                        nxt = (c, t + 1) if t + 1 < NT else ((c + 1, 0) if c + 1 < 4 else None)
                        if nxt is not None:
                            _ga_load(nxt[0], nxt[1], it + 1)
                        S.op("act", lambda e, pb=pb: e.activation(out=ga[:, pb, :], in_=ga[:, pb, :], func=AF.Sigmoid), reads=["ga%d" % pb], writes=["ga%d" % pb])
                        for kc in range(8):
                            S.op("pe", lambda e, pb=pb, kc=kc, t=t, wb=wb: e.matmul(psA[:, pb, :], lhsT=AT[:, kc, t * 128:(t + 1) * 128], rhs=W[:, wb, kc, :], start=(kc == 0), stop=(kc == 7)),
                                 reads=["AT%d" % t] + wkeys(wb, 8), writes=["psA%d" % pb], signal=(kc == 7))
                        S.op("dve", lambda e, pb=pb: e.tensor_tensor(out=tmpm[:, pb, :], in0=psA[:, pb, :], in1=ga[:, pb, :], op=ALU.mult),
                             reads=["psA%d" % pb, "ga%d" % pb], writes=["tmpm%d" % pb])
                        mk = "mr%d_%d" % (t, c)
                        S.op("dve", lambda e, pb=pb, t=t, c=c: e.tensor_tensor(out=mr[:, t, c * 512:(c + 1) * 512], in0=tmpm[:, pb, :], in1=mr[:, t, c * 512:(c + 1) * 512], op=ALU.add),
                             reads=["tmpm%d" % pb, mk], writes=[mk])
                        it += 1
                S.barrier()
                for t in range(NT):
                    cols = slice(t * 128, (t + 1) * 128)
                    for kc in range(16):
                        S.op("pe", lambda e, t=t, kc=kc: e.transpose(out=psT[:, kc // 8, kc % 8, :], in_=mr[:, t, kc * 128:(kc + 1) * 128], identity=identb[:]),
                             reads=["mr%d_%d" % (t, kc // 4), "identb"], writes=["psT%d" % (kc // 8)], signal=(kc % 8 == 7))
                    for hh in range(2):
                        copy_op(evq(), RT[:, hh * 8:(hh + 1) * 8, cols], psT[:, hh, :, :], ["psT%d" % hh], ["RT%d" % t])
                S.barrier()
        S.set_phase(13)
        with ExitStack() as st10:
            xr = sb(st10, [128, 2, 512], F32, "xr")
            W = sb(st10, [128, 2, 16, 512], BF16, "W")
            Wh["W"] = W
            yo = sb(st10, [128, 2, 512], F32, "yo")
            bufs = {0: load_w(w_out, 0, 16)}
            it = 0
            for c in range(4):
                if c + 1 < 4:
                    bufs[c + 1] = load_w(w_out, (c + 1) * 512, 16)
                wb = bufs[c]
                cs = slice(c * 512, (c + 1) * 512)
                for t in range(NT):
                    pb = it % 2
                    def _xr_load(cc, tt, ii):
                        S.dma("sp", xr[:, ii % 2, :], x_all[tt, :, cc * 512:(cc + 1) * 512], writes=["xr%d" % (ii % 2)])
                    if it == 0:
                        _xr_load(0, 0, 0)
                    nxt = (c, t + 1) if t + 1 < NT else ((c + 1, 0) if c + 1 < 4 else None)
                    if nxt is not None:
                        _xr_load(nxt[0], nxt[1], it + 1)
                    for kc in range(16):
                        S.op("pe", lambda e, pb=pb, kc=kc, t=t, wb=wb: e.matmul(psA[:, pb, :], lhsT=RT[:, kc, t * 128:(t + 1) * 128], rhs=W[:, wb, kc, :], start=(kc == 0), stop=(kc == 15)),
                             reads=["RT%d" % t] + wkeys(wb, 16), writes=["psA%d" % pb], signal=(kc == 15))
                    S.op("dve", lambda e, pb=pb: e.tensor_tensor(out=yo[:, pb, :], in0=psA[:, pb, :], in1=xr[:, pb, :], op=ALU.add),
                         reads=["psA%d" % pb, "xr%d" % pb], writes=["yo%d" % pb])
                    if t < 8:
                        S.dma("act", y_p[t * 128:(t + 1) * 128, cs], yo[:, pb, :], reads=["yo%d" % pb])
                    else:
                        S.dma("act", y_s[:, cs], yo[0:64, pb, :], reads=["yo%d" % pb])
                    it += 1
            S.barrier()

    S.set_phase(0)
    S.barrier()
    with nc.Block() as block:
        @block.tensor
        def _(e):
            for f in S.prog["pe"]:
                f(e)

        @block.scalar
        def _(e):
            for f in S.prog["act"]:
                f(e)

        @block.vector
        def _(e):
            for f in S.prog["dve"]:
                f(e)

        @block.gpsimd
        def _(e):
            for f in S.prog["pool"]:
                f(e)

        @block.sync
        def _(e):
            for f in S.prog["sp"]:
                f(e)
    top.close()
    return nc


def _tables(core):
    s, r = core // 4, core % 4
    gam = GAM
    p = np.arange(128)
    posR = np.zeros((128, NT), np.float64)
    posA = np.zeros((128, NTH), np.float64)
    for t in range(8):
        posR[:, t] = 16 + r * 1024 + t * 128 + p
        posA[:, t] = posR[:, t]
    p8 = np.zeros(128)
    p8[:64] = 16384 + (p[:64] % 4)
    p8[64:80] = np.arange(16)
    posR[:, 8] = p8
    posA[:, 8] = p8
    posA[:, 9] = 16 + r * 1024 - 128 + p
    invR = np.exp(-math.log(10000.0) * 2.0 * np.arange(64, dtype=np.float32) / 128).astype(np.float32)
    invA = np.exp(-math.log(500000.0) * 2.0 * np.arange(8, dtype=np.float32) / 16).astype(np.float32)
    angR = posR.astype(np.float32)[:, :, None] * invR[None, None, :]
    angA = posA.astype(np.float32)[:, :, None] * invA[None, None, :]
    ropeR = np.stack([np.cos(angR), np.sin(angR)], axis=2).astype(np.float32)
    ropeA = np.stack([np.cos(angA), np.sin(angA)], axis=2).astype(np.float32)
    sc = 128.0 ** -0.5
    kwsc = np.zeros((128, NT, 8), np.float64)
    for h in range(8):
        kwsc[:, :8, h] = (gam[h] ** (127 - p))[:, None] * sc
        kwsc[:64, 8, h] = gam[h] ** (3 - (p[:64] % 4)) * sc
        kwsc[64:80, 8, h] = gam[h] ** (15 - np.arange(16)) * sc
    dmask = np.zeros((128, 8, 2, 128), np.float64)
    j = p[:, None]
    i = p[None, :]
    for h in range(8):
        dmask[:, h, 0, :] = np.where(i >= j, gam[h] ** np.maximum(i - j, 0), 0.0) * sc
        m1 = (i < 64) & (j < 64) & ((i // 4) == (j // 4)) & (i >= j)
        dmask[:, h, 1, :] = np.where(m1, gam[h] ** np.maximum(i - j, 0), 0.0) * sc
    qgt = np.zeros((128, 8, 128), np.float64)
    qgs = np.zeros((128, 8, 4), np.float64)
    for h in range(8):
        qgt[:, h, :] = (gam[h] ** (p + 1))[None, :]
        qgs[:, h, :] = (gam[h] ** (np.arange(4) + 1))[None, :]
    bsel = np.zeros((128, 16), np.float32)
    for q in range(64):
        bsel[q, q // 4] = 1.0
    amask = np.zeros((128, 3, 128), np.float32)
    amask[:, 0, :] = (j <= i)
    amask[:, 1, :] = (j > i)
    amask[:, 2, :] = (j > i) if r > 0 else 0.0
    smn = np.zeros((128, 64), np.float32)
    for kk in range(64):
        for q in range(64):
            if kk // 4 == q // 4 and kk % 4 <= q % 4:
                smn[kk, q] = 1.0
    smn[64:80, :] = 1.0
    smc = np.zeros((128, 16), np.float32)
    for g in range(4):
        for ii in range(4):
            smc[:, g * 4 + ii] = (p > ii)
    posP = np.zeros((128, NPRE), np.float64)
    kwp = np.zeros((128, NPRE, 8), np.float64)
    for jslot in range(3):
        ch = r - 1 - jslot
        for tt in range(8):
            ti = jslot * 8 + tt
            posP[:, ti] = 16 + max(ch, 0) * 1024 + tt * 128 + p
            for h in range(8):
                kwp[:, ti, h] = gam[h] ** (1024.0 * jslot + 1023 - (tt * 128 + p)) * sc
    angP = posP.astype(np.float32)[:, :, None] * invR[None, None, :]
    ropeP = np.stack([np.cos(angP), np.sin(angP)], axis=2).astype(np.float32)
    coef = np.zeros((128, 9, 8), np.float64)
    for h in range(8):
        for rp in range(r):
            coef[:, 4 * s + rp, h] = gam[h] ** (1024 * (r - 1 - rp))
        coef[:, 8, h] = gam[h] ** (1024 * r)
    return dict(ropeR=ropeR, ropeA=ropeA, kwsc=kwsc.astype(np.float32), dmask=dmask.astype(np.float32),
                qgt=qgt.astype(np.float32), qgs=qgs.astype(np.float32), bsel=bsel, amask=amask, smn=smn, smc=smc,
                coef=coef.astype(np.float32), ident=np.eye(128, dtype=np.float32), ropeP=ropeP, kwp=kwp.astype(np.float32))


_NC_CACHE = {}


def kernel(x_prompt, x_sample, cache_win_k, cache_win_v, state_ret, meta_tokens, norm_gain, w_in,
           q_norm_gain, k_norm_gain, attn_sinks, ret_gn_gain, ret_gn_bias, w_branch_attn, w_branch_ret, w_out):
    f = np.float32
    x_prompt = np.asarray(x_prompt, f)
    x_sample = np.asarray(x_sample, f)
    ck = np.asarray(cache_win_k, f)[0].reshape(128, 128, 256)
    cv = np.asarray(cache_win_v, f)[0].reshape(128, 128, 256)
    st = np.asarray(state_ret, f)[0]
    meta = np.asarray(meta_tokens, f)
    w_in_ = np.ascontiguousarray(np.asarray(w_in, f)[0])
    w_pa_ = np.ascontiguousarray(np.asarray(w_branch_attn, f)[0])
    w_pr_ = np.ascontiguousarray(np.asarray(w_branch_ret, f)[0])
    w_out_ = np.ascontiguousarray(np.asarray(w_out, f)[0])
    gqk = np.concatenate([np.tile(np.asarray(q_norm_gain, f)[0], 16), np.tile(np.asarray(k_norm_gain, f)[0], 4)])[None, :]
    if "nc" not in _NC_CACHE:
        _NC_CACHE["nc"] = build_program()
    nc = _NC_CACHE["nc"]
    in_maps = []
    for c in range(NCORES):
        s, r = c // 4, c % 4
        xa = np.zeros((NTH, 128, D), f)
        xa[:8] = x_prompt[s, r * 1024:(r + 1) * 1024].reshape(8, 128, D)
        xa[8, :64] = x_sample[16 * c:16 * c + 16].reshape(64, D)
        xa[8, 64:80] = meta
        if r > 0:
            xa[9] = x_prompt[s, r * 1024 - 128:r * 1024]
        m = dict(x_all=xa, w_in=w_in_, w_pa=w_pa_, w_pr=w_pr_, w_out=w_out_,
                 norm_g=np.asarray(norm_gain, f).reshape(1, D), gqk=np.ascontiguousarray(gqk),
                 sinks=np.asarray(attn_sinks, f).reshape(1, 16),
                 gn_g=np.asarray(ret_gn_gain, f).reshape(1, D), gn_b=np.asarray(ret_gn_bias, f).reshape(1, D),
                 cache_k=np.ascontiguousarray(ck[16 * c:16 * c + 16]), cache_v=np.ascontiguousarray(cv[16 * c:16 * c + 16]),
                 state=np.ascontiguousarray(st[16 * c:16 * c + 16]))
        xp = np.zeros((NPRE, 128, D), f)
        for jslot in range(3):
            ch = r - 1 - jslot
            if ch >= 0:
                xp[jslot * 8:(jslot + 1) * 8] = x_prompt[s, ch * 1024:(ch + 1) * 1024].reshape(8, 128, D)
        m["x_pre"] = xp
        m.update(_tables(c))
        in_maps.append(m)
    res = run_bass_kernel_spmd(nc, in_maps, core_ids=list(range(NCORES)))
    R = res.results
    y_prompt = np.zeros((2, 4096, D), f)
    y_sample = np.zeros((128, 4, D), f)
    wkp = np.zeros((1, 2, 128, 4, 64), f)
    wvp = np.zeros((1, 2, 128, 4, 64), f)
    retp = np.zeros((1, 2, 8, 128, 256), f)
    wks = np.zeros((1, 128, 128, 4, 64), f)
    wvs = np.zeros((1, 128, 128, 4, 64), f)
    rets = np.zeros((1, 128, 8, 128, 256), f)
    for c in range(NCORES):
        s, r = c // 4, c % 4
        y_prompt[s, r * 1024:(r + 1) * 1024] = R[c]["y_p"]
        y_sample[16 * c:16 * c + 16] = R[c]["y_s"].reshape(16, 4, D)
        if r == 3:
            wkp[0, s] = R[c]["wk_p"].reshape(128, 4, 64)
            wvp[0, s] = R[c]["wv_p"].reshape(128, 4, 64)
            retp[0, s] = R[c]["ret_p"]
        wks[0, 16 * c:16 * c + 16] = R[c]["wk_s"].reshape(16, 128, 4, 64)
        wvs[0, 16 * c:16 * c + 16] = R[c]["wv_s"].reshape(16, 128, 4, 64)
        rets[0, 16 * c:16 * c + 16] = R[c]["ret_s"]
    return (y_prompt, y_sample, wkp, wvp, retp, wks, wvs, rets)
```

```python
import math
import os
import types
from contextlib import ExitStack
import numpy as np
import concourse.bass as bass
import concourse.mybir as mybir
from concourse.bass_utils import run_bass_kernel_spmd

F32 = mybir.dt.float32
BF16 = mybir.dt.bfloat16
ALU = mybir.AluOpType
AF = mybir.ActivationFunctionType
AX = mybir.AxisListType

NCORES = 8
D = 2048
INW = 12800
NT = 9
NTH = 10
TOK = NT * 128
EPS = 1e-6
GN_EPS = 1e-5
C_QA, C_KA, C_VA, C_ZA, C_QR, C_KR, C_VR, C_ZR, C_GA, C_GR = 0, 1024, 1280, 1536, 2560, 3584, 4608, 6656, 8704, 10752
NDS = 30
SUB = os.environ.get("MK_SUB", "")
OVERLAP = os.environ.get("MK_OVERLAP", "1") == "1"
NPRE = 24

_lg = np.log(1.0 - np.exp(np.linspace(np.log(1.0 / 32), np.log(1.0 / 512), 8))).astype(np.float32)
GAM = np.exp(_lg.astype(np.float64))


def _freeze(fn):
    if fn.__closure__ is None:
        return fn
    cells = []
    for c in fn.__closure__:
        try:
            cells.append(types.CellType(c.cell_contents))
        except ValueError:
            cells.append(c)
    return types.FunctionType(fn.__code__, fn.__globals__, fn.__name__, fn.__defaults__, tuple(cells))


class Sched:
    def __init__(self, nc):
        self.nc = nc
        self.sem = {k: nc.alloc_semaphore(name="s_" + k) for k in ["pe", "act", "dve", "pool"]}
        self.cnt = {k: 0 for k in self.sem}
        self.dsem = [nc.alloc_semaphore(name="d%d" % i) for i in range(NDS)]
        self.dval = [0] * NDS
        self.dnext = {"sp": 0, "pool": 0, "act": 0}
        self.dring = {"sp": list(range(0, 14)), "pool": list(range(14, 24)), "act": list(range(24, NDS))}
        self.ccsem = nc.alloc_semaphore(name="ccs")
        self.queues = ["pe", "act", "dve", "pool", "sp"]
        self.seen = {q: {} for q in self.queues}
        self.reg = {}
        self.prog = {q: [] for q in self.queues}
        self.phase = 0
        self.maxphase = int(os.environ.get("MK_MAXPHASE", "99"))
        self.minphase = int(os.environ.get("MK_MINPHASE", "0"))

    def set_phase(self, n):
        self.phase = n

    @property
    def on(self):
        return self.phase <= self.maxphase and (self.phase == 0 or self.phase >= self.minphase)

    def _semh(self, k):
        if k[0] == "e":
            return self.sem[k[1]]
        if k[0] == "c":
            return self.ccsem
        return self.dsem[k[1]]

    def _deps(self, reads, writes):
        need = {}

        def add(st):
            if st is None:
                return
            k, t = st
            if need.get(k, 0) < t:
                need[k] = t
        for r in reads:
            e = self.reg.get(r)
            if e:
                add(e[0])
        for w in writes:
            e = self.reg.get(w)
            if e:
                add(e[0])
                for k, t in e[1].items():
                    add((k, t))
        return need

    def _emit_waits(self, q, need):
        for k, t in need.items():
            if k == ("e", "pe") and q == "pe":
                continue
            if self.seen[q].get(k, 0) >= t:
                continue
            self.seen[q][k] = t
            sem = self._semh(k)
            self.prog[q].append(lambda e, sem=sem, t=t: e.wait_ge(sem, t))

    def _mark(self, reads, writes, st):
        k, t = st
        for r in reads:
            e = self.reg.setdefault(r, [None, {}])
            e[1][k] = max(e[1].get(k, 0), t)
        for w in writes:
            self.reg[w] = [st, {}]

    def op(self, q, fn, reads=(), writes=(), signal=True):
        if not self.on:
            return
        fn = _freeze(fn)
        need = self._deps(reads, writes)
        self._emit_waits(q, need)
        tick = self.cnt[q] + 1
        if signal:
            self.cnt[q] = tick
            sem = self.sem[q]
            self.prog[q].append(lambda e, fn=fn, sem=sem: fn(e).then_inc(sem, 1))
        else:
            self.prog[q].append(lambda e, fn=fn: fn(e))
        self._mark(reads, writes, (("e", q), tick))

    def dma(self, q, out, in_, reads=(), writes=()):
        if not self.on:
            return
        need = self._deps(reads, writes)
        ring = self.dring[q]
        i = ring[self.dnext[q] % len(ring)]
        self.dnext[q] += 1
        if self.dval[i] > 0:
            need[("d", i)] = max(need.get(("d", i), 0), self.dval[i])
        self._emit_waits(q, need)
        self.dval[i] += 16
        v = self.dval[i]
        sem = self.dsem[i]
        self.prog[q].append(lambda e, out=out, in_=in_, sem=sem: e.dma_start(out=out, in_=in_).then_inc(sem, 16))
        self._mark(reads, writes, (("d", i), v))

    def barrier(self):
        if not self.on:
            return
        for q in self.queues:
            need = {}
            for k in self.sem:
                if self.cnt[k] > 0 and k != q:
                    need[("e", k)] = self.cnt[k]
            for i in range(NDS):
                if self.dval[i] > 0:
                    need[("d", i)] = self.dval[i]
            self._emit_waits(q, need)


def build_program():
    nc = bass.Bass("TRN2", target_bir_lowering=False)
    S = Sched(nc)

    def din(name, shape, dt=F32):
        return nc.dram_tensor(name, list(shape), dt, kind="ExternalInput").ap()

    def dout(name, shape):
        return nc.dram_tensor(name, list(shape), F32, kind="ExternalOutput").ap()

    x_all = din("x_all", [NTH, 128, D])
    w_in = din("w_in", [D, INW])
    w_pa = din("w_pa", [1024, D])
    w_pr = din("w_pr", [D, D])
    w_out = din("w_out", [D, D])
    norm_g = din("norm_g", [1, D])
    gqk_in = din("gqk", [1, 20 * 64])
    esink_in = din("sinks", [1, 16])
    gn_g = din("gn_g", [1, D])
    gn_b = din("gn_b", [1, D])
    cache_k = din("cache_k", [16, 128, 256])
    cache_v = din("cache_v", [16, 128, 256])
    state = din("state", [16, 8, 128, 256])
    ropeR_in = din("ropeR", [128, NT, 2, 64])
    ropeA_in = din("ropeA", [128, NTH, 2, 8])
    kwsc_in = din("kwsc", [128, NT, 8])
    dmask_in = din("dmask", [128, 8, 2, 128])
    qgt_in = din("qgt", [128, 8, 128])
    qgs_in = din("qgs", [128, 8, 4])
    bsel_in = din("bsel", [128, 16])
    amask_in = din("amask", [128, 3, 128])
    smn_in = din("smn", [128, 64])
    smc_in = din("smc", [128, 16])
    coef_in = din("coef", [128, 9, 8])
    ident_in = din("ident", [128, 128])
    x_pre = din("x_pre", [NPRE, 128, D])
    ropeP_in = din("ropeP", [128, NPRE, 2, 64])
    kwp_in = din("kwp", [128, NPRE, 8])

    y_p = dout("y_p", [1024, D])
    y_s = dout("y_s", [64, D])
    wk_p = dout("wk_p", [128, 256])
    wv_p = dout("wv_p", [128, 256])
    ret_p = dout("ret_p", [8, 128, 256])
    wk_s = dout("wk_s", [16, 128, 256])
    wv_s = dout("wv_s", [16, 128, 256])
    ret_s = dout("ret_s", [16, 8, 128, 256])

    U = nc.dram_tensor("U_scr", [NTH * 128, INW], F32, kind="Internal").ap()
    OA = nc.dram_tensor("OA_scr", [NT * 128, 1024], BF16, kind="Internal").ap()

    uid = [0]

    def sb(st, shape, dt, name):
        uid[0] += 1
        return st.enter_context(nc.sbuf_tensor("%s_%d" % (name, uid[0]), list(shape), dt))

    top = ExitStack()
    Wh = {}
    Spre = sb(top, [128, 8, 256], F32, "Spre")
    ident32 = sb(top, [128, 128], F32, "ident32")
    identb = sb(top, [128, 128], BF16, "identb")
    psA = top.enter_context(nc.psum_tensor("psA", [128, 2, 512], F32))
    psT = top.enter_context(nc.psum_tensor("psT", [128, 2, 8, 128], BF16))
    psM = top.enter_context(nc.psum_tensor("psM", [128, 4, 512], F32))

    S.dma("sp", ident32[:], ident_in, writes=["ident32"])
    S.op("dve", lambda e: e.tensor_copy(out=identb[:], in_=ident32[:]), reads=["ident32"], writes=["identb"])

    wstate = {"n": 0}

    def load_w(src, c0, nkc, ncols=512):
        b = wstate["n"] % 2
        wstate["n"] += 1
        srcv = src.rearrange("(kc p) n -> p kc n", p=128)
        W = Wh["W"]
        for k0 in range(0, nkc, 4):
            S.dma("pool", W[:, b, k0:k0 + 4, 0:ncols], srcv[:, k0:k0 + 4, c0:c0 + ncols], writes=["W%d_%d" % (b, k0)])
        return b

    def wkeys(b, nkc):
        return ["W%d_%d" % (b, k0) for k0 in range(0, nkc, 4)]

    rr = {"ev": 0}

    def evq():
        rr["ev"] += 1
        return "act" if rr["ev"] % 2 else "dve"

    def copy_op(q, out, in_, reads, writes):
        if q == "act":
            S.op("act", lambda e: e.activation(out=out, in_=in_, func=AF.Copy), reads=reads, writes=writes)
        else:
            S.op(q, lambda e: e.tensor_copy(out=out, in_=in_), reads=reads, writes=writes)


    S.set_phase(1)
    with ExitStack() as stP:
        Wkv = sb(stP, [128, 16, 3072], BF16, "Wkv")
        xstP = sb(stP, [128, 3, D], F32, "xstP")
        gbcP = sb(stP, [128, D], F32, "gbcP")
        sqjP = sb(stP, [128, D], BF16, "sqjP")
        xnbP = sb(stP, [128, 3, D], BF16, "xnbP")
        xnTp = sb(stP, [128, 2, 16, 128], BF16, "xnTp")
        ropeP = sb(stP, [128, NPRE, 2, 64], F32, "ropeP")
        kwp = sb(stP, [128, NPRE, 8], F32, "kwp")
        ssP = sb(stP, [128, NPRE], F32, "ssP")
        rsP = sb(stP, [128, NPRE], F32, "rsP")
        taP = sb(stP, [128, 4, 64], F32, "taP")
        tbP = sb(stP, [128, 4, 64], F32, "tbP")
        kro = sb(stP, [128, 8, 128], F32, "kro")
        kwb = sb(stP, [128, 2, 1024], BF16, "kwb")
        vb = sb(stP, [128, 2, D], BF16, "vb")
        ztile = sb(stP, [128, 512], BF16, "ztile")
        srcv = w_in.rearrange("(kc p) n -> p kc n", p=128)
        for cb in range(6):
            for k0 in range(0, 16, 4):
                S.dma("pool", Wkv[:, k0:k0 + 4, cb * 512:(cb + 1) * 512], srcv[:, k0:k0 + 4, C_KR + cb * 512:C_KR + (cb + 1) * 512], writes=["Wkv%d_%d" % (cb, k0)])
        S.dma("sp", gbcP[:], norm_g[0].partition_broadcast(128), writes=["gbcP"])
        S.dma("sp", ropeP[:], ropeP_in, writes=["ropeP"])
        S.dma("sp", kwp[:], kwp_in, writes=["kwp"])
        itpc = [0]

        def prepA(t):
            b = t % 3
            S.dma("sp", xstP[:, b, :], x_pre[t], writes=["xstP%d" % b])
            S.op("act", lambda e, b=b, t=t: e.activation(out=sqjP[:], in_=xstP[:, b, :], func=AF.Square, accum_out=ssP[:, t:t + 1]),
                 reads=["xstP%d" % b], writes=["sqjP", "ssP%d" % t])
            S.op("dve", lambda e, t=t: e.tensor_scalar(out=rsP[:, t:t + 1], in0=ssP[:, t:t + 1], scalar1=1.0 / D, scalar2=EPS, op0=ALU.mult, op1=ALU.add),
                 reads=["ssP%d" % t], writes=["rsP%d" % t])
            S.op("act", lambda e, t=t: e.activation(out=rsP[:, t:t + 1], in_=rsP[:, t:t + 1], func=AF.Sqrt), reads=["rsP%d" % t], writes=["rsP%d" % t])
            S.op("dve", lambda e, t=t: e.reciprocal(out=rsP[:, t:t + 1], in_=rsP[:, t:t + 1]), reads=["rsP%d" % t], writes=["rsP%d" % t])
            S.op("dve", lambda e, b=b, t=t: e.scalar_tensor_tensor(out=xnbP[:, b, :], in0=xstP[:, b, :], scalar=rsP[:, t:t + 1], in1=gbcP[:], op0=ALU.mult, op1=ALU.mult),
                 reads=["xstP%d" % b, "rsP%d" % t, "gbcP"], writes=["xnbP%d" % b])

        def prepB(t):
            b = t % 3
            b2 = t % 2
            for kc in range(16):
                S.op("pe", lambda e, b=b, kc=kc: e.transpose(out=psT[:, kc // 8, kc % 8, :], in_=xnbP[:, b, kc * 128:(kc + 1) * 128], identity=identb[:]),
                     reads=["xnbP%d" % b, "identb"], writes=["psT%d" % (kc // 8)], signal=(kc % 8 == 7))
            for hh in range(2):
                copy_op(evq(), xnTp[:, b2, hh * 8:(hh + 1) * 8, :], psT[:, hh, :, :], ["psT%d" % hh], ["xnTp%d" % b2])

        def stateP(t):
            b = t % 2
            for h in range(8):
                S.op("pe", lambda e, b=b, h=h, t=t: e.matmul(psM[:, h // 2, (h % 2) * 256:(h % 2) * 256 + 256], lhsT=kwb[:, b, h * 128:(h + 1) * 128], rhs=vb[:, b, h * 256:(h + 1) * 256],
                                                         start=False, stop=(t == NPRE - 1), skip_group_check=True),
                     reads=["kwb%d_%d" % (b, h // 4), "vb%d_%d" % (b, h // 2)], writes=["psMacc"], signal=(h == 7))

        def mainP(t):
            b = t % 2
            for cb in range(6):
                pb = itpc[0] % 2
                itpc[0] += 1
                for kc in range(16):
                    S.op("pe", lambda e, pb=pb, kc=kc, b=b, cb=cb: e.matmul(psA[:, pb, :], lhsT=xnTp[:, b, kc, :], rhs=Wkv[:, kc, cb * 512:(cb + 1) * 512], start=(kc == 0), stop=(kc == 15)),
                         reads=["xnTp%d" % b, "Wkv%d_%d" % (cb, (kc // 4) * 4)], writes=["psA%d" % pb], signal=(kc == 15))
                if cb < 2:
                    xv = psA[:, pb, :].rearrange("p (h two d) -> p h two d", h=4, two=2)
                    x1 = xv[:, :, 0, :]
                    x2 = xv[:, :, 1, :]
                    cosb = ropeP[:, t, 0, :].unsqueeze(1).to_broadcast([128, 4, 64])
                    sinb = ropeP[:, t, 1, :].unsqueeze(1).to_broadcast([128, 4, 64])
                    ov = kro[:, cb * 4:(cb + 1) * 4, :].rearrange("p h (two d) -> p h two d", two=2)
                    rk = ["psA%d" % pb, "ropeP"]
                    S.op("dve", lambda e, x1=x1, cosb=cosb: e.tensor_tensor(out=taP[:], in0=x1, in1=cosb, op=ALU.mult), reads=rk, writes=["taP"])
                    S.op("dve", lambda e, x2=x2, sinb=sinb: e.tensor_tensor(out=tbP[:], in0=x2, in1=sinb, op=ALU.mult), reads=rk, writes=["tbP"])
                    S.op("dve", lambda e, ov=ov: e.tensor_tensor(out=ov[:, :, 0, :], in0=taP[:], in1=tbP[:], op=ALU.subtract), reads=["taP", "tbP"], writes=["kro%d" % cb])
                    S.op("dve", lambda e, x2=x2, cosb=cosb: e.tensor_tensor(out=taP[:], in0=x2, in1=cosb, op=ALU.mult), reads=rk, writes=["taP"])
                    S.op("dve", lambda e, x1=x1, sinb=sinb: e.tensor_tensor(out=tbP[:], in0=x1, in1=sinb, op=ALU.mult), reads=rk, writes=["tbP"])
                    S.op("dve", lambda e, ov=ov: e.tensor_tensor(out=ov[:, :, 1, :], in0=taP[:], in1=tbP[:], op=ALU.add), reads=["taP", "tbP"], writes=["kro%d" % cb])
                    S.op("dve", lambda e, b=b, t=t, cb=cb: e.tensor_tensor(out=kwb[:, b, cb * 512:(cb + 1) * 512].rearrange("p (h d) -> p h d", h=4), in0=kro[:, cb * 4:(cb + 1) * 4, :],
                                                                     in1=kwp[:, t, cb * 4:(cb + 1) * 4].unsqueeze(2).to_broadcast([128, 4, 128]), op=ALU.mult),
                         reads=["kro%d" % cb, "kwp"], writes=["kwb%d_%d" % (b, cb)])
                else:
                    vc = cb - 2
                    copy_op("act", vb[:, b, vc * 512:(vc + 1) * 512], psA[:, pb, :], ["psA%d" % pb], ["vb%d_%d" % (b, vc)])
                if cb == 0 and t > 0:
                    stateP(t - 1)

        S.op("pool", lambda e: e.memset(ztile[:], 0.0), writes=["ztile"])
        for bk in range(4):
            S.op("pe", lambda e, bk=bk: e.matmul(psM[:, bk, :], lhsT=ztile[:, 0:128], rhs=ztile[:, :], start=True, stop=False, skip_group_check=True),
                 reads=["ztile"], writes=["psMacc"], signal=(bk == 3))
        prepA(0)
        prepA(1)
        prepB(0)
        for t in range(NPRE):
            if t + 2 < NPRE:
                prepA(t + 2)
            if t + 1 < NPRE:
                prepB(t + 1)
            mainP(t)
        stateP(NPRE - 1)
        for h in range(8):
            S.op("dve", lambda e, h=h: e.tensor_copy(out=Spre[:, h, :], in_=psM[:, h // 2, (h % 2) * 256:(h % 2) * 256 + 256]), reads=["psMacc"], writes=["Spre%d" % h])
        S.barrier()
    bufA = sb(top, [128, NT, D], BF16, "bufA")
    bufB = sb(top, [128, 16, TOK], BF16, "bufB")

    def attn_gen():
        with ExitStack() as st7:
            oast = sb(st7, [128, 2, 1024], BF16, "oast")
            bufBf = bufB[:].rearrange("p a b -> p (a b)")
            qTa = bufA[0:64, :, :].rearrange("p t d -> p (t d)").rearrange("p (h c) -> p h c", h=16)
            kTa = bufBf[0:64, 0:5120].rearrange("p (h t) -> p h t", h=4)
            vaug = bufBf[:, 5120:7720].rearrange("p (t h d) -> p t h d", t=NTH, h=4)
            vmeta = sb(st7, [16, 4, 65], BF16, "vmeta")
            ropeA = sb(st7, [128, NTH, 2, 8], F32, "ropeA")
            gqk = sb(st7, [128, 20, 64], F32, "gqk")
            esink = sb(st7, [128, 16], F32, "esink")
            amask = sb(st7, [128, 3, 128], F32, "amask")
            amb = sb(st7, [128, 3, 128], BF16, "amb")
            smn = sb(st7, [128, 64], F32, "smn")
            smc = sb(st7, [128, 16], F32, "smc")
            smnb = sb(st7, [128, 64], BF16, "smnb")
            smcb = sb(st7, [128, 16], BF16, "smcb")
            k32 = sb(st7, [128, 2, 4, 64], F32, "k32")
            S.dma("sp", ropeA[:], ropeA_in, writes=["ropeA"])
            S.dma("sp", gqk[:].rearrange("p h d -> p (h d)"), gqk_in[0].partition_broadcast(128), writes=["gqk"])
            S.dma("sp", esink[:], esink_in[0].partition_broadcast(128), writes=["esink"])
            S.op("act", lambda e: e.activation(out=esink[:], in_=esink[:], func=AF.Exp), reads=["esink"], writes=["esink"])
            S.dma("sp", amask[:], amask_in, writes=["amask"])
            S.dma("sp", smn[:], smn_in, writes=["smn"])
            S.dma("sp", smc[:], smc_in, writes=["smc"])
            S.op("dve", lambda e: e.tensor_copy(out=amb[:], in_=amask[:]), reads=["amask"], writes=["amb"])
            S.op("dve", lambda e: e.tensor_copy(out=smnb[:], in_=smn[:]), reads=["smn"], writes=["smnb"])
            S.op("dve", lambda e: e.tensor_copy(out=smcb[:], in_=smc[:]), reads=["smc"], writes=["smcb"])
            S.op("pool", lambda e: e.memset(vaug[:], 1.0), writes=["vaug%d" % t for t in range(NTH)])
            with ExitStack() as stp:
                ua = sb(stp, [128, 2, 1536], F32, "ua")
                sqa = sb(stp, [128, 20, 64], F32, "sqa")
                xna = sb(stp, [128, 20, 64], F32, "xna")
                ssa = sb(stp, [128, 20], F32, "ssa")
                r1 = sb(stp, [128, 20, 8], F32, "r1")
                r2 = sb(stp, [128, 20, 8], F32, "r2")
                qkb = sb(stp, [128, 2, 20, 64], BF16, "qkb")
                for t in range(NTH):
                    b = t % 2
                    rows = slice(t * 128, (t + 1) * 128)
                    h0 = 0 if t < 9 else 16
                    c0 = 0 if t < 9 else 1024
                    def _ua_load(tt):
                        cc0 = 0 if tt < 9 else 1024
                        S.dma("sp", ua[:, tt % 2, cc0:1536], U[tt * 128:(tt + 1) * 128, cc0:1536], writes=["ua%d" % (tt % 2)])
                    if t == 0:
                        _ua_load(0)
                    if t + 1 < NTH:
                        _ua_load(t + 1)
                    xq = ua[:, b, 0:1280].rearrange("p (h d) -> p h d", d=64)[:, h0:20, :]
                    nh = 20 - h0
                    S.op("act", lambda e, xq=xq, h0=h0: e.activation(out=sqa[:, h0:20, :], in_=xq, func=AF.Square), reads=["ua%d" % b], writes=["sqa"])
                    S.op("dve", lambda e, h0=h0: e.reduce_sum(out=ssa[:, h0:20], in_=sqa[:, h0:20, :], axis=AX.X), reads=["sqa"], writes=["ssa"])
                    S.op("dve", lambda e, h0=h0: e.tensor_scalar(out=ssa[:, h0:20], in0=ssa[:, h0:20], scalar1=1.0 / 64, scalar2=EPS, op0=ALU.mult, op1=ALU.add), reads=["ssa"], writes=["ssa"])
                    S.op("act", lambda e, h0=h0: e.activation(out=ssa[:, h0:20], in_=ssa[:, h0:20], func=AF.Sqrt), reads=["ssa"], writes=["ssa"])
                    S.op("dve", lambda e, h0=h0: e.reciprocal(out=ssa[:, h0:20], in_=ssa[:, h0:20]), reads=["ssa"], writes=["ssa"])
                    S.op("dve", lambda e, xq=xq, h0=h0, nh=nh: e.tensor_tensor(out=xna[:, h0:20, :], in0=xq, in1=ssa[:, h0:20].unsqueeze(2).to_broadcast([128, nh, 64]), op=ALU.mult),
                         reads=["ua%d" % b, "ssa"], writes=["xna"])
                    S.op("dve", lambda e, h0=h0: e.tensor_tensor(out=xna[:, h0:20, :], in0=xna[:, h0:20, :], in1=gqk[:, h0:20, :], op=ALU.mult), reads=["xna", "gqk"], writes=["xna"])
                    cosb = ropeA[:, t, 0, :].unsqueeze(1).to_broadcast([128, nh, 8])
                    sinb = ropeA[:, t, 1, :].unsqueeze(1).to_broadcast([128, nh, 8])
                    x1 = xna[:, h0:20, 0:8]
                    x2 = xna[:, h0:20, 8:16]
                    S.op("dve", lambda e, h0=h0, b=b: e.tensor_copy(out=qkb[:, b, h0:20, :], in_=xna[:, h0:20, :]), reads=["xna"], writes=["qkb%d" % b])
                    if t in (7, 8):
                        ki = t - 7
                        S.op("dve", lambda e, ki=ki: e.tensor_copy(out=k32[:, ki, :, :], in_=xna[:, 16:20, :]), reads=["xna"], writes=["k32_%d" % ki])
                    S.op("dve", lambda e, x1=x1, cosb=cosb, h0=h0: e.tensor_tensor(out=r1[:, h0:20, :], in0=x1, in1=cosb, op=ALU.mult), reads=["xna", "ropeA"], writes=["r1"])
                    S.op("dve", lambda e, x2=x2, sinb=sinb, h0=h0: e.tensor_tensor(out=r2[:, h0:20, :], in0=x2, in1=sinb, op=ALU.mult), reads=["xna", "ropeA"], writes=["r2"])
                    S.op("dve", lambda e, h0=h0, b=b: e.tensor_tensor(out=qkb[:, b, h0:20, 0:8], in0=r1[:, h0:20, :], in1=r2[:, h0:20, :], op=ALU.subtract), reads=["r1", "r2"], writes=["qkb%d" % b])
                    if t in (7, 8):
                        S.op("dve", lambda e, ki=ki: e.tensor_tensor(out=k32[:, ki, :, 0:8], in0=r1[:, 16:20, :], in1=r2[:, 16:20, :], op=ALU.subtract), reads=["r1", "r2"], writes=["k32_%d" % ki])
                    S.op("dve", lambda e, x2=x2, cosb=cosb, h0=h0: e.tensor_tensor(out=r1[:, h0:20, :], in0=x2, in1=cosb, op=ALU.mult), reads=["xna", "ropeA"], writes=["r1"])
                    S.op("dve", lambda e, x1=x1, sinb=sinb, h0=h0: e.tensor_tensor(out=r2[:, h0:20, :], in0=x1, in1=sinb, op=ALU.mult), reads=["xna", "ropeA"], writes=["r2"])
                    S.op("dve", lambda e, h0=h0, b=b: e.tensor_tensor(out=qkb[:, b, h0:20, 8:16], in0=r1[:, h0:20, :], in1=r2[:, h0:20, :], op=ALU.add), reads=["r1", "r2"], writes=["qkb%d" % b])
                    if t in (7, 8):
                        S.op("dve", lambda e, ki=ki: e.tensor_tensor(out=k32[:, ki, :, 8:16], in0=r1[:, 16:20, :], in1=r2[:, 16:20, :], op=ALU.add), reads=["r1", "r2"], writes=["k32_%d" % ki])
                    copy_op("act", vaug[:, t, :, 0:64], ua[:, b, 1280:1536].rearrange("p (h d) -> p h d", d=64), ["ua%d" % b], ["vaug%d" % t])
                    cols = slice(t * 128, (t + 1) * 128)
                    yield 3
                    if t < 9:
                        for j in range(16):
                            S.op("pe", lambda e, b=b, j=j: e.transpose(out=psT[0:64, j // 8, j % 8, :], in_=qkb[:, b, j, :], identity=identb[:]),
                                 reads=["qkb%d" % b, "identb"], writes=["psT%d" % (j // 8)], signal=(j % 8 == 7))
                        for hh in range(2):
                            copy_op(evq(), qTa[:, hh * 8:(hh + 1) * 8, cols], psT[0:64, hh, :, :], ["psT%d" % hh], ["qTa%d" % t])
                    for j in range(4):
                        S.op("pe", lambda e, b=b, j=j: e.transpose(out=psT[0:64, 0, j, :], in_=qkb[:, b, 16 + j, :], identity=identb[:]),
                             reads=["qkb%d" % b, "identb"], writes=["psT0"], signal=(j == 3))
                    copy_op(evq(), kTa[:, :, cols], psT[0:64, 0, 0:4, :], ["psT0"], ["kTa%d" % t])
                    yield
                S.dma("sp", wk_p, k32[:, 0, :, :].rearrange("p h d -> p (h d)"), reads=["k32_0"])
                S.dma("sp", wv_p, U[7 * 128:8 * 128, C_VA:C_VA + 256], reads=["U7_2"])
                S.dma("sp", wk_s[:, 0:124, :], cache_k[:, 4:128, :])
                S.dma("sp", wv_s[:, 0:124, :], cache_v[:, 4:128, :])
                for bq in range(16):
                    S.dma("sp", wk_s[bq, 124:128, :], k32[bq * 4:(bq + 1) * 4, 1, :, :].rearrange("p h d -> p (h d)"), reads=["k32_1"])
                    S.dma("sp", wv_s[bq, 124:128, :], U[8 * 128 + bq * 4:8 * 128 + (bq + 1) * 4, C_VA:C_VA + 256], reads=["U8_2"])
                S.dma("sp", vmeta[:], vaug[64:80, 8, :, :], reads=["vaug8"], writes=["vmeta"])
                S.barrier()
            with ExitStack() as stc:
                PT = sb(stc, [128, 2, 3, 512], BF16, "PT")
                den = sb(stc, [128, 2, 4], F32, "den")
                it = 0
                for t in range(8):
                    tp = t - 1 if t > 0 else 9
                    cols = slice(t * 128, (t + 1) * 128)
                    pcols = slice(tp * 128, (tp + 1) * 128)
                    for h in range(4):
                        pb = it % 2
                        it += 1
                        qrhs = qTa[:, 4 * h:4 * h + 4, cols]
                        qk_ = ["qTa%d" % t]
                        S.op("pe", lambda e, h=h, cols=cols, qrhs=qrhs: e.matmul(psM[:, 0, :], lhsT=kTa[:, h, cols], rhs=qrhs, start=True, stop=True), reads=qk_ + ["kTa%d" % t], writes=["psM0"])
                        S.op("pe", lambda e, h=h, pcols=pcols, qrhs=qrhs: e.matmul(psM[:, 1, :], lhsT=kTa[:, h, pcols], rhs=qrhs, start=True, stop=True), reads=qk_ + ["kTa%d" % tp], writes=["psM1"])
                        S.op("pe", lambda e, h=h, qrhs=qrhs: e.matmul(psM[0:16, 2, :], lhsT=kTa[:, h, 1024 + 64:1024 + 80], rhs=qrhs, start=True, stop=True), reads=qk_ + ["kTa8"], writes=["psM2"])
                        for j in range(3):
                            np_ = 16 if j == 2 else 128
                            S.op("act", lambda e, j=j, pb=pb, np_=np_: e.activation(out=PT[0:np_, pb, j, :], in_=psM[0:np_, j, :], func=AF.Exp, scale=0.125),
                                 reads=["psM%d" % j], writes=["PT%d_%d" % (pb, j)])
                        for j in range(2):
                            mi = j if (j == 0 or t > 0) else 2
                            pv = PT[:, pb, j, :].rearrange("p (g q) -> p g q", g=4)
                            S.op("dve", lambda e, pv=pv, mi=mi: e.tensor_tensor(out=pv, in0=pv, in1=amb[:, mi, :].unsqueeze(1).to_broadcast([128, 4, 128]), op=ALU.mult),
                                 reads=["PT%d_%d" % (pb, j), "amb"], writes=["PT%d_%d" % (pb, j)])
                        yield 1
                        for g in range(4):
                            gc = slice(g * 128, (g + 1) * 128)
                            oap = psM[:, 3, g * 65:(g + 1) * 65]
                            S.op("pe", lambda e, oap=oap, pb=pb, gc=gc, t=t, h=h: e.matmul(oap, lhsT=PT[:, pb, 0, gc], rhs=vaug[:, t, h, :], start=True, stop=False),
                                 reads=["PT%d_0" % pb, "vaug%d" % t], writes=["psM3"], signal=False)
                            S.op("pe", lambda e, oap=oap, pb=pb, gc=gc, tp=tp, h=h: e.matmul(oap, lhsT=PT[:, pb, 1, gc], rhs=vaug[:, tp, h, :], start=False, stop=False),
                                 reads=["PT%d_1" % pb, "vaug%d" % tp], writes=["psM3"], signal=False)
                            S.op("pe", lambda e, oap=oap, pb=pb, gc=gc, h=h: e.matmul(oap, lhsT=PT[0:16, pb, 2, gc], rhs=vmeta[:, h, :], start=False, stop=True),
                                 reads=["PT%d_2" % pb, "vmeta"], writes=["psM3"], signal=(g == 3))
                        ov = psM[:, 3, 0:260].rearrange("p (g d) -> p g d", g=4)
                        S.op("dve", lambda e, ov=ov, pb=pb, h=h: e.tensor_tensor(out=den[:, pb, :], in0=ov[:, :, 64], in1=esink[:, 4 * h:4 * h + 4], op=ALU.add), reads=["psM3", "esink"], writes=["den%d" % pb])
                        S.op("dve", lambda e, pb=pb: e.reciprocal(out=den[:, pb, :], in_=den[:, pb, :]), reads=["den%d" % pb], writes=["den%d" % pb])
                        S.op("dve", lambda e, ov=ov, pb=pb, t=t, h=h: e.tensor_tensor(out=oast[:, t % 2, h * 256:(h + 1) * 256].rearrange("p (g d) -> p g d", g=4), in0=ov[:, :, 0:64],
                                                                                      in1=den[:, pb, :].unsqueeze(2).to_broadcast([128, 4, 64]), op=ALU.mult),
                             reads=["psM3", "den%d" % pb], writes=["oast%d" % (t % 2)])
                        yield
                    S.dma("sp", OA[t * 128:(t + 1) * 128, :], oast[:, t % 2, :], reads=["oast%d" % (t % 2)])
                S.barrier()
            with ExitStack() as sts:
                Kc = sb(sts, [128, 16, 256], BF16, "Kc")
                Vc = sb(sts, [128, 16, 4, 65], BF16, "Vc")
                KcT = sb(sts, [64, 16, 128], BF16, "KcT")
                PTn = sb(sts, [128, 256], BF16, "PTn")
                PTc = sb(sts, [128, 256], BF16, "PTc")
                OT = sb(sts, [128, 256], F32, "OT")
                den2 = sb(sts, [128, 4], F32, "den2")
                S.op("pool", lambda e: e.memset(Vc[:], 1.0), writes=["Vc"])
                S.dma("pool", Kc[:], cache_k.rearrange("b p c -> p b c"), writes=["Kc"])
                for bq in range(16):
                    S.dma("pool", Vc[:, bq, :, 0:64], cache_v[bq].rearrange("p (h d) -> p h d", h=4), reads=["Vc"], writes=["Vc"])
                S.op("pool", lambda e: e.memset(oast[:, 0, :], 0.0), writes=["oast0"])
                for h in range(4):
                    S.op("pe", lambda e, h=h: e.matmul(psM[0:80, 0, 0:256], lhsT=kTa[:, h, 1024:1104], rhs=qTa[:, 4 * h:4 * h + 4, 1024:1088], start=True, stop=True),
                         reads=["kTa8", "qTa8"], writes=["psM0"])
                    S.op("act", lambda e: e.activation(out=PTn[0:80, :], in_=psM[0:80, 0, 0:256], func=AF.Exp, scale=0.125), reads=["psM0"], writes=["PTn"])
                    pnv = PTn[0:80, :].rearrange("p (g q) -> p g q", g=4)
                    S.op("dve", lambda e, pnv=pnv: e.tensor_tensor(out=pnv, in0=pnv, in1=smnb[0:80, :].unsqueeze(1).to_broadcast([80, 4, 64]), op=ALU.mult), reads=["PTn", "smnb"], writes=["PTn"])
                    for bq in range(16):
                        S.op("pe", lambda e, bq=bq, h=h: e.transpose(out=psT[0:64, bq // 8, bq % 8, :], in_=Kc[:, bq, h * 64:(h + 1) * 64], identity=identb[:]),
                             reads=["Kc", "identb"], writes=["psT%d" % (bq // 8)], signal=(bq % 8 == 7))
                    for hh in range(2):
                        copy_op(evq(), KcT[:, hh * 8:(hh + 1) * 8, :], psT[0:64, hh, :, :], ["psT%d" % hh], ["KcT"])
                    yield 1
                    for bq in range(16):
                        S.op("pe", lambda e, bq=bq, h=h: e.matmul(psM[:, 1, bq * 16:(bq + 1) * 16], lhsT=KcT[:, bq, :],
                                                                  rhs=qTa[:, 4 * h:4 * h + 4, 1024 + 4 * bq:1024 + 4 * bq + 4], start=True, stop=True),
                             reads=["KcT", "qTa8"], writes=["psM1"], signal=(bq == 15))
                    S.op("act", lambda e: e.activation(out=PTc[:], in_=psM[:, 1, 0:256], func=AF.Exp, scale=0.125), reads=["psM1"], writes=["PTc"])
                    pcv = PTc[:].rearrange("p (b k) -> p b k", b=16)
                    S.op("dve", lambda e, pcv=pcv: e.tensor_tensor(out=pcv, in0=pcv, in1=smcb[:].unsqueeze(1).to_broadcast([128, 16, 16]), op=ALU.mult), reads=["PTc", "smcb"], writes=["PTc"])
                    yield 1
                    S.op("pe", lambda e, h=h: e.matmul(psM[0:65, 2, 0:256], lhsT=vaug[0:80, 8, h, :], rhs=PTn[0:80, :].rearrange("p (g b i) -> p b g i", g=4, b=16), start=True, stop=False),
                         reads=["vaug8", "PTn"], writes=["psM2"], signal=False)
                    for bq in range(16):
                        S.op("pe", lambda e, bq=bq, h=h: e.matmul(psM[0:65, 2, bq * 16:(bq + 1) * 16], lhsT=Vc[:, bq, h, :],
                                                                  rhs=PTc[:, bq * 16:(bq + 1) * 16], start=False, stop=(bq == 15)),
                             reads=["Vc", "PTc"], writes=["psM2"], signal=(bq == 15))
                    copy_op("act", OT[0:65, :].rearrange("p (g b i) -> p g b i", g=4, b=16), psM[0:65, 2, 0:256].rearrange("p (b g i) -> p g b i", b=16, g=4), ["psM2"], ["OT"])
                    yield 1
                    for g in range(4):
                        S.op("pe", lambda e, g=g: e.transpose(out=psM[0:64, 3, g * 65:(g + 1) * 65], in_=OT[0:65, g * 64:(g + 1) * 64], identity=ident32[0:65, 0:65]),
                             reads=["OT", "ident32"], writes=["psM3"], signal=(g == 3))
                    ov = psM[0:64, 3, 0:260].rearrange("p (g d) -> p g d", g=4)
                    S.op("dve", lambda e, ov=ov, h=h: e.tensor_tensor(out=den2[0:64, :], in0=ov[:, :, 64], in1=esink[0:64, 4 * h:4 * h + 4], op=ALU.add), reads=["psM3", "esink"], writes=["den2"])
                    S.op("dve", lambda e: e.reciprocal(out=den2[0:64, :], in_=den2[0:64, :]), reads=["den2"], writes=["den2"])
                    S.op("dve", lambda e, ov=ov, h=h: e.tensor_tensor(out=oast[0:64, 0, h * 256:(h + 1) * 256].rearrange("p (g d) -> p g d", g=4), in0=ov[:, :, 0:64],
                                                                      in1=den2[0:64, :].unsqueeze(2).to_broadcast([64, 4, 64]), op=ALU.mult),
                         reads=["psM3", "den2"], writes=["oast0"])
                    yield
                S.dma("sp", OA[8 * 128:9 * 128, :], oast[:, 0, :], reads=["oast0"])
                S.barrier()
        yield

    S.set_phase(2)
    with ExitStack() as st:
        xnT = sb(st, [128, 16, NTH * 128], BF16, "xnT")
        with ExitStack() as st1:
            xst = sb(st1, [128, 2, D], F32, "xst")
            gbc = sb(st1, [128, D], F32, "gbc")
            xnb = sb(st1, [128, 2, D], BF16, "xnb")
            sqj = sb(st1, [128, D], BF16, "sqj")
            ss = sb(st1, [128, NTH], F32, "ss")
            rstd = sb(st1, [128, NTH], F32, "rstd")
            S.dma("sp", gbc[:], norm_g[0].partition_broadcast(128), writes=["gbc"])
            for t in range(NTH):
                b = t % 2
                S.dma("sp", xst[:, b, :], x_all[t], writes=["xst%d" % b])
                S.op("act", lambda e, b=b, t=t: e.activation(out=sqj[:], in_=xst[:, b, :], func=AF.Square, accum_out=ss[:, t:t + 1]),
                     reads=["xst%d" % b], writes=["sqj", "ss%d" % t])
                S.op("dve", lambda e, t=t: e.tensor_scalar(out=rstd[:, t:t + 1], in0=ss[:, t:t + 1], scalar1=1.0 / D, scalar2=EPS, op0=ALU.mult, op1=ALU.add),
                     reads=["ss%d" % t], writes=["rstd%d" % t])
                S.op("act", lambda e, t=t: e.activation(out=rstd[:, t:t + 1], in_=rstd[:, t:t + 1], func=AF.Sqrt), reads=["rstd%d" % t], writes=["rstd%d" % t])
                S.op("dve", lambda e, t=t: e.reciprocal(out=rstd[:, t:t + 1], in_=rstd[:, t:t + 1]), reads=["rstd%d" % t], writes=["rstd%d" % t])
                S.op("dve", lambda e, b=b, t=t: e.scalar_tensor_tensor(out=xnb[:, b, :], in0=xst[:, b, :], scalar=rstd[:, t:t + 1], in1=gbc[:], op0=ALU.mult, op1=ALU.mult),
                     reads=["xst%d" % b, "rstd%d" % t, "gbc"], writes=["xnb%d" % b])
                for kc in range(16):
                    S.op("pe", lambda e, b=b, kc=kc: e.transpose(out=psT[:, kc // 8, kc % 8, :], in_=xnb[:, b, kc * 128:(kc + 1) * 128], identity=identb[:]),
                         reads=["xnb%d" % b, "identb"], writes=["psT%d" % (kc // 8)], signal=(kc % 8 == 7))
                for hh in range(2):
                    copy_op(evq(), xnT[:, hh * 8:(hh + 1) * 8, t * 128:(t + 1) * 128], psT[:, hh, :, :], ["psT%d" % hh], ["xnT%d" % t])
            S.barrier()
        S.set_phase(3)
        with ExitStack() as st2:
            ust = sb(st2, [128, 4, 512], F32, "ust")
            W = sb(st2, [128, 2, 16, 512], BF16, "W")
            Wh["W"] = W
            nblk = INW // 512

            def inproj_gen():
                bufs = {0: load_w(w_in, 0, 16)}
                it = 0
                for c in range(nblk):
                    if c + 1 < nblk:
                        bufs[c + 1] = load_w(w_in, (c + 1) * 512, 16)
                    wb = bufs[c]
                    tiles = list(range(NT)) + ([9] if c == 2 else [])
                    for t in tiles:
                        pb = it % 2
                        for kc in range(16):
                            S.op("pe", lambda e, pb=pb, kc=kc, t=t, wb=wb: e.matmul(psA[:, pb, :], lhsT=xnT[:, kc, t * 128:(t + 1) * 128], rhs=W[:, wb, kc, :], start=(kc == 0), stop=(kc == 15)),
                                 reads=["xnT%d" % t] + wkeys(wb, 16), writes=["psA%d" % pb], signal=(kc == 15))
                        ub = it % 4
                        copy_op("act", ust[:, ub, :], psA[:, pb, :], ["psA%d" % pb], ["ust%d" % ub])
                        S.dma("sp", U[t * 128:(t + 1) * 128, c * 512:(c + 1) * 512], ust[:, ub, :], reads=["ust%d" % ub], writes=["U%d_%d" % (t, c)])
                        it += 1
                        yield c

            gB = None
            alive = True
            wait = 0
            for c in inproj_gen():
                if c >= 3 and OVERLAP:
                    if gB is None:
                        S.barrier()
                        gB = attn_gen()
                    wait -= 1
                    if alive and wait <= 0:
                        r = next(gB, "END")
                        if r == "END":
                            alive = False
                        else:
                            wait = r or 1
            if gB is None:
                gB = attn_gen()
            for _ in gB:
                pass
            S.barrier()

    with ExitStack() as st:
        RT = bufB
        mr = bufA
        Sfin_keep = None
        with ExitStack() as stR:
            olocal = bufA
            qgall = sb(stR, [128, 8, 1024], BF16, "qgall")
            Send = sb(stR, [128, 8, 256], F32, "Send")
            Sm = sb(stR, [128, 8, 256], F32, "Sm")
            with ExitStack() as st3:
                qT = bufB[:, 0:8, :]
                kT = bufB[:, 8:16, :]
                kw = sb(st3, [128, NT, 1024], BF16, "kw")
                vtm = sb(st3, [128, NT, D], BF16, "vtm")
                qgs = sb(st3, [128, 8, 4], F32, "qgs")
                bsel = sb(st3, [128, 16], F32, "bsel")
                S.dma("sp", qgs[:], qgs_in, writes=["qgs"])
                S.dma("sp", bsel[:], bsel_in, writes=["bsel"])
                S.set_phase(4)
                with ExitStack() as stp:
                    uq = sb(stp, [128, 2, 2048], F32, "uq")
                    ropeR = sb(stp, [128, NT, 2, 64], F32, "ropeR")
                    kwsc = sb(stp, [128, NT, 8], F32, "kwsc")
                    qgt = sb(stp, [128, 8, 128], F32, "qgt")
                    S.dma("sp", ropeR[:], ropeR_in, writes=["ropeR"])
                    S.dma("sp", kwsc[:], kwsc_in, writes=["kwsc"])
                    S.dma("sp", qgt[:], qgt_in, writes=["qgt"])
                    ta = sb(stp, [128, 16, 64], F32, "ta")
                    tb = sb(stp, [128, 16, 64], F32, "tb")
                    qkr = sb(stp, [128, 1, 16, 128], BF16, "qkr")
                    for t in range(NT):
                        b = 0
                        ub_ = t % 2
                        rows = slice(t * 128, (t + 1) * 128)
                        if t == 0:
                            S.dma("sp", uq[:, 0, :], U[0:128, C_QR:C_QR + 2048], writes=["uq0"])
                        if t + 1 < NT:
                            S.dma("sp", uq[:, (t + 1) % 2, :], U[(t + 1) * 128:(t + 2) * 128, C_QR:C_QR + 2048], writes=["uq%d" % ((t + 1) % 2)])
                        for vq in range(4 if SUB != "a" else 0):
                            S.dma("pool", vtm[:, t, vq * 512:(vq + 1) * 512], U[rows, C_VR + vq * 512:C_VR + (vq + 1) * 512], writes=["vtm%d_%d" % (t, vq)])
                        xv = uq[:, ub_, :].rearrange("p (h two d) -> p h two d", h=16, two=2)
                        x1 = xv[:, :, 0, :]
                        x2 = xv[:, :, 1, :]
                        cosb = ropeR[:, t, 0, :].unsqueeze(1).to_broadcast([128, 16, 64])
                        sinb = ropeR[:, t, 1, :].unsqueeze(1).to_broadcast([128, 16, 64])
                        ov = qkr[:, b, :, :].rearrange("p h (two d) -> p h two d", two=2)
                        rk = ["uq%d" % ub_, "ropeR"]
                        S.op("dve", lambda e, x1=x1, cosb=cosb: e.tensor_tensor(out=ta[:], in0=x1, in1=cosb, op=ALU.mult), reads=rk, writes=["ta"])
                        S.op("dve", lambda e, x2=x2, sinb=sinb: e.tensor_tensor(out=tb[:], in0=x2, in1=sinb, op=ALU.mult), reads=rk, writes=["tb"])
                        S.op("dve", lambda e, ov=ov: e.tensor_tensor(out=ov[:, :, 0, :], in0=ta[:], in1=tb[:], op=ALU.subtract), reads=["ta", "tb"], writes=["qkr%d" % b])
                        S.op("dve", lambda e, x2=x2, cosb=cosb: e.tensor_tensor(out=ta[:], in0=x2, in1=cosb, op=ALU.mult), reads=rk, writes=["ta"])
                        S.op("dve", lambda e, x1=x1, sinb=sinb: e.tensor_tensor(out=tb[:], in0=x1, in1=sinb, op=ALU.mult), reads=rk, writes=["tb"])
                        S.op("dve", lambda e, ov=ov: e.tensor_tensor(out=ov[:, :, 1, :], in0=ta[:], in1=tb[:], op=ALU.add), reads=["ta", "tb"], writes=["qkr%d" % b])
                        S.op("dve", lambda e, b=b, t=t: e.tensor_tensor(out=kw[:, t, :].rearrange("p (h d) -> p h d", h=8), in0=qkr[:, b, 8:16, :],
                                                                        in1=kwsc[:, t, :].unsqueeze(2).to_broadcast([128, 8, 128]), op=ALU.mult),
                             reads=["qkr%d" % b, "kwsc"], writes=["kw%d" % t])
                        if SUB == "b":
                            continue
                        for j in range(16):
                            S.op("pe", lambda e, b=b, j=j: e.transpose(out=psT[:, j // 8, j % 8, :], in_=qkr[:, b, j, :], identity=identb[:]),
                                 reads=["qkr%d" % b, "identb"], writes=["psT%d" % (j // 8)], signal=(j % 8 == 7))
                        cols = slice(t * 128, (t + 1) * 128)
                        copy_op("act", qT[:, :, cols], psT[:, 0, :, :], ["psT0"], ["qT%d" % t])
                        if t < 8:
                            S.op("dve", lambda e, cols=cols: e.tensor_tensor(out=qgall[:, :, cols], in0=qT[:, :, cols], in1=qgt[:], op=ALU.mult),
                                 reads=["qT%d" % t, "qgt"], writes=["qg%d" % t])
                        copy_op("act", kT[:, :, cols], psT[:, 1, :, :], ["psT1"], ["kT%d" % t])
                    S.barrier()
                S.set_phase(5)
                with ExitStack() as stc:
                    innerT = sb(stc, [128, 2, 128], BF16, "innerT")
                    dmask = sb(stc, [128, 8, 2, 128], F32, "dmask")
                    S.dma("sp", dmask[:], dmask_in, writes=["dmask"])
                    Slbf = sb(stc, [128, 8, 2, 256], BF16, "Slbf")
                    qz = sb(stc, [128, 16 * 68], F32, "qz")
                    kz = sb(stc, [128, 16, 128], BF16, "kz")
                    S32 = sb(stc, [128, 3, 4, 256], F32, "S32")
                    S.op("pool", lambda e: e.memset(qz[:], 0.0), writes=["qz"])
                    S.op("pool", lambda e: e.memset(olocal[:, 8, :], 0.0), writes=["ol8"])
                    qd_ctr = 0
                    for t in range(8):
                        cols = slice(t * 128, (t + 1) * 128)
                        vk = ["vtm%d_%d" % (t, i) for i in range(4)]
                        for h in range(8):
                            par = h % 2
                            ba, bc = 2 * par, 2 * par + 1
                            hc = slice(h * 256, (h + 1) * 256)
                            g128 = float(GAM[h] ** 128)
                            S.op("pe", lambda e, h=h, cols=cols, ba=ba: e.matmul(psM[:, ba, 0:128], lhsT=kT[:, h, cols], rhs=qT[:, h, cols], start=True, stop=True),
                                 reads=["kT%d" % t, "qT%d" % t], writes=["pb%d" % par])
                            S.op("dve", lambda e, h=h, par=par, ba=ba: e.tensor_tensor(out=innerT[:, par, :], in0=psM[:, ba, 0:128], in1=dmask[:, h, 0, :], op=ALU.mult),
                                 reads=["pb%d" % par, "dmask"], writes=["innerT%d" % par])
                            S.op("pe", lambda e, par=par, ba=ba, t=t, hc=hc: e.matmul(psM[:, ba, 128:384], lhsT=innerT[:, par, :], rhs=vtm[:, t, hc], start=True, stop=(t == 0)),
                                 reads=["innerT%d" % par] + vk, writes=["pb%d" % par], signal=(t == 0))
                            if t > 0:
                                S.op("pe", lambda e, h=h, cols=cols, ba=ba, t=t: e.matmul(psM[:, ba, 128:384], lhsT=qgall[:, h, cols], rhs=Slbf[:, h, t % 2, :], start=False, stop=True),
                                     reads=["qg%d" % t, "Slbf%d_%d" % (h, t % 2)], writes=["pb%d" % par])
                            copy_op("act", olocal[:, t, hc], psM[:, ba, 128:384], ["pb%d" % par], ["ol%d_%d" % (t, h)])
                            S.op("pe", lambda e, h=h, t=t, hc=hc, bc=bc: e.matmul(psM[:, bc, 0:256], lhsT=kw[:, t, h * 128:(h + 1) * 128], rhs=vtm[:, t, hc], start=True, stop=True),
                                 reads=["kw%d" % t] + vk, writes=["pc%d" % par])
                            if t == 0:
                                S.op("dve", lambda e, h=h, bc=bc: e.tensor_copy(out=Send[:, h, :], in_=psM[:, bc, 0:256]), reads=["pc%d" % par], writes=["Send%d" % h])
                            else:
                                S.op("dve", lambda e, h=h, g128=g128, bc=bc: e.scalar_tensor_tensor(out=Send[:, h, :], in0=Send[:, h, :], scalar=g128, in1=psM[:, bc, 0:256], op0=ALU.mult, op1=ALU.add),
                                     reads=["pc%d" % par, "Send%d" % h], writes=["Send%d" % h])
                            if t < 7:
                                copy_op("act", Slbf[:, h, (t + 1) % 2, :], Send[:, h, :], ["Send%d" % h], ["Slbf%d_%d" % (h, (t + 1) % 2)])
                    S.barrier()

                    def issue_state_load(idx):
                        hh_, qd_ = idx // 4, idx % 4
                        bi_ = idx % 3
                        src_ = state[qd_ * 4:(qd_ + 1) * 4, hh_, :, :].rearrange("b p d -> p b d")
                        S.dma("sp", S32[:, bi_, :, :], src_, writes=["S32_%d" % bi_])
                    issue_state_load(0)
                    issue_state_load(1)
                    for h in range(8):
                        hc = slice(h * 256, (h + 1) * 256)
                        g4 = float(GAM[h] ** 4)
                        cols8 = slice(1024, 1152)
                        S.op("pe", lambda e, h=h: e.matmul(psM[:, 0, 0:128], lhsT=kT[:, h, cols8], rhs=qT[:, h, cols8], start=True, stop=True),
                             reads=["kT8", "qT8"], writes=["psM0"])
                        S.op("dve", lambda e, h=h: e.tensor_tensor(out=innerT[:, 0, :], in0=psM[:, 0, 0:128], in1=dmask[:, h, 1, :], op=ALU.mult),
                             reads=["psM0", "dmask"], writes=["innerT0"])
                        S.op("dve", lambda e, h=h: e.tensor_tensor(out=qz[:].rearrange("p (b s) -> p b s", s=68)[:, :, 0:4],
                                                                   in0=qT[:, h, 1024:1088].rearrange("p (b i) -> p b i", i=4),
                                                                   in1=qgs[:, h, :].unsqueeze(1).to_broadcast([128, 16, 4]), op=ALU.mult),
                             reads=["qT8", "qgs"], writes=["qz"])
                        S.op("dve", lambda e, h=h: e.tensor_tensor(out=kz[0:64, :, :], in0=kw[0:64, 8, h * 128:(h + 1) * 128].unsqueeze(1).to_broadcast([64, 16, 128]),
                                                                   in1=bsel[0:64, :].unsqueeze(2).to_broadcast([64, 16, 128]), op=ALU.mult),
                             reads=["kw8", "bsel"], writes=["kz"])
                        S.op("pe", lambda e, hc=hc: e.matmul(psM[0:64, 1, 0:256], lhsT=innerT[:, 0, 0:64], rhs=vtm[:, 8, hc], start=True, stop=False),
                             reads=["innerT0", "vtm8_0", "vtm8_1", "vtm8_2", "vtm8_3"], writes=["psM1"], signal=False)
                        for qd in range(4):
                            sbuf_i = qd_ctr % 3
                            if qd_ctr + 2 < 32:
                                issue_state_load(qd_ctr + 2)
                            qd_ctr += 1
                            for bb in range(4):
                                b = qd * 4 + bb
                                last = (b == 15)
                                S.op("pe", lambda e, b=b, bb=bb, sbuf_i=sbuf_i, last=last: e.matmul(psM[0:64, 1, 0:256], lhsT=qz[:, b * 64:(b + 1) * 64], rhs=S32[:, sbuf_i, bb, :], start=False, stop=last),
                                     reads=["qz", "S32_%d" % sbuf_i], writes=["psM1"], signal=last)
                            for bb in range(4):
                                b = qd * 4 + bb
                                pci = b % 4
                                pcap = psM[:, 2 + pci, 0:256] if pci < 2 else psA[:, pci - 2, 0:256]
                                pck = "pcs%d" % pci
                                S.op("pe", lambda e, b=b, pcap=pcap, hc=hc: e.matmul(pcap, lhsT=kz[0:64, b, :], rhs=vtm[0:64, 8, hc], start=True, stop=True),
                                     reads=["kz", "vtm8_0", "vtm8_1", "vtm8_2", "vtm8_3"], writes=[pck])
                                S.op("dve", lambda e, sbuf_i=sbuf_i, bb=bb, pcap=pcap, g4=g4: e.scalar_tensor_tensor(out=S32[:, sbuf_i, bb, :], in0=S32[:, sbuf_i, bb, :], scalar=g4, in1=pcap, op0=ALU.mult, op1=ALU.add),
                                     reads=[pck, "S32_%d" % sbuf_i], writes=["S32_%d" % sbuf_i])
                            S.dma("act", ret_s[qd * 4:(qd + 1) * 4, h, :, :].rearrange("b p d -> p b d"), S32[:, sbuf_i, :, :], reads=["S32_%d" % sbuf_i])
                        copy_op("act", olocal[0:64, 8, hc], psM[0:64, 1, 0:256], ["psM1"], ["ol8"])
                        S.op("pe", lambda e, h=h, hc=hc: e.matmul(psM[:, 2, 0:256], lhsT=kw[64:80, 8, h * 128:(h + 1) * 128], rhs=vtm[64:80, 8, hc], start=True, stop=True),
                             reads=["kw8", "vtm8_0", "vtm8_1", "vtm8_2", "vtm8_3"], writes=["pcs0"])
                        S.op("dve", lambda e, h=h: e.tensor_copy(out=Sm[:, h, :], in_=psM[:, 2, 0:256]), reads=["pcs0"], writes=["Sm%d" % h])
                S.barrier()
            S.set_phase(6)
            with ExitStack() as st5:
                coef = sb(st5, [128, 9, 8], F32, "coef")
                Sst = sb(st5, [128, 8, 256], F32, "Sst")
                Sstb = sb(st5, [128, 8, 256], BF16, "Sstb")
                of32 = sb(st5, [128, 2, 8, 256], F32, "of32")
                sq32 = sb(st5, [128, 8, 256], F32, "sq32")
                zr = sb(st5, [128, 2, D], F32, "zr")
                gg = sb(st5, [128, D], F32, "gg")
                gb = sb(st5, [128, D], F32, "gb")
                Rb = sb(st5, [128, 2, D], BF16, "Rb")
                stt = sb(st5, [128, 2, 6, 8], F32, "stt")
                S.dma("sp", coef[:], coef_in, writes=["coef"])
                S.dma("sp", gg[:], gn_g[0].partition_broadcast(128), writes=["gg"])
                S.dma("sp", gb[:], gn_b[0].partition_broadcast(128), writes=["gb"])
                for h in range(8):
                    S.op("dve", lambda e, h=h: e.scalar_tensor_tensor(out=Sst[:, h, :], in0=Sm[:, h, :], scalar=coef[:, 8, h:h + 1], in1=Spre[:, h, :], op0=ALU.mult, op1=ALU.add),
                         reads=["Sm%d" % h, "coef"], writes=["Sst%d" % h])
                sk = ["Sst%d" % h for h in range(8)]
                copy_op("act", Sstb[:], Sst[:], sk, ["Sstb"])
                for h in range(8):
                    g1024 = float(GAM[h] ** 1024)
                    S.op("dve", lambda e, h=h, g1024=g1024: e.scalar_tensor_tensor(out=Send[:, h, :], in0=Sst[:, h, :], scalar=g1024, in1=Send[:, h, :], op0=ALU.mult, op1=ALU.add),
                         reads=["Sst%d" % h, "Send%d" % h], writes=["Send%d" % h])
                S.dma("sp", ret_p.rearrange("h p d -> p h d"), Send[:], reads=["Send%d" % h for h in range(8)])
                def p5A(t):
                    rows = slice(t * 128, (t + 1) * 128)
                    cols = slice(t * 128, (t + 1) * 128)
                    b2 = t % 2
                    ofk, zk, sk2 = "of32_%d" % b2, "zr%d" % b2, "stt%d" % b2
                    S.dma("sp", zr[:, b2, :], U[rows, C_ZR:C_ZR + 2048], writes=[zk])
                    if t < 8:
                        for h in range(8):
                            pm = h % 4
                            gt = float(GAM[h] ** (128 * t))
                            S.op("pe", lambda e, h=h, cols=cols, pm=pm: e.matmul(psM[:, pm, 0:256], lhsT=qgall[:, h, cols], rhs=Sstb[:, h, :], start=True, stop=True),
                                 reads=["qg%d" % t, "Sstb"], writes=["psM%d" % pm])
                            S.op("dve", lambda e, h=h, t=t, pm=pm, gt=gt, b2=b2: e.scalar_tensor_tensor(out=of32[:, b2, h, :], in0=psM[:, pm, 0:256], scalar=gt, in1=olocal[:, t, h * 256:(h + 1) * 256], op0=ALU.mult, op1=ALU.add),
                                 reads=["psM%d" % pm, "ol%d_%d" % (t, h)], writes=[ofk])
                    else:
                        S.op("dve", lambda e, b2=b2: e.tensor_copy(out=of32[:, b2, :, :].rearrange("p h d -> p (h d)"), in_=olocal[:, 8, :]), reads=["ol8"], writes=[ofk])
                    S.op("dve", lambda e, b2=b2: e.reduce_sum(out=stt[:, b2, 0, :], in_=of32[:, b2, :, :], axis=AX.X), reads=[ofk], writes=[sk2])
                    S.op("act", lambda e, b2=b2: e.activation(out=sq32[:], in_=of32[:, b2, :, :], func=AF.Square), reads=[ofk], writes=["sq32"])
                    S.op("dve", lambda e, b2=b2: e.reduce_sum(out=stt[:, b2, 1, :], in_=sq32[:], axis=AX.X), reads=["sq32", sk2], writes=[sk2])
                    S.op("dve", lambda e, b2=b2: e.tensor_scalar(out=stt[:, b2, 2, :], in0=stt[:, b2, 0, :], scalar1=1.0 / 256, scalar2=None, op0=ALU.mult), reads=[sk2], writes=[sk2])
                    S.op("dve", lambda e, b2=b2: e.tensor_tensor(out=stt[:, b2, 3, :], in0=stt[:, b2, 2, :], in1=stt[:, b2, 2, :], op=ALU.mult), reads=[sk2], writes=[sk2])
                    S.op("dve", lambda e, b2=b2: e.scalar_tensor_tensor(out=stt[:, b2, 4, :], in0=stt[:, b2, 1, :], scalar=1.0 / 256, in1=stt[:, b2, 3, :], op0=ALU.mult, op1=ALU.subtract), reads=[sk2], writes=[sk2])
                    S.op("dve", lambda e, b2=b2: e.tensor_scalar(out=stt[:, b2, 4, :], in0=stt[:, b2, 4, :], scalar1=GN_EPS, scalar2=None, op0=ALU.add), reads=[sk2], writes=[sk2])
                    S.op("act", lambda e, b2=b2: e.activation(out=stt[:, b2, 4, :], in_=stt[:, b2, 4, :], func=AF.Sqrt), reads=[sk2], writes=[sk2])
                    S.op("dve", lambda e, b2=b2: e.reciprocal(out=stt[:, b2, 4, :], in_=stt[:, b2, 4, :]), reads=[sk2], writes=[sk2])
                    S.op("dve", lambda e, b2=b2: e.scalar_tensor_tensor(out=stt[:, b2, 5, :], in0=stt[:, b2, 2, :], scalar=-1.0, in1=stt[:, b2, 4, :], op0=ALU.mult, op1=ALU.mult), reads=[sk2], writes=[sk2])

                def p5B(t):
                    rows = slice(t * 128, (t + 1) * 128)
                    cols = slice(t * 128, (t + 1) * 128)
                    b2 = t % 2
                    ofk, zk, sk2 = "of32_%d" % b2, "zr%d" % b2, "stt%d" % b2
                    for h in range(8):
                        S.op("act", lambda e, b2=b2, h=h: e.activation(out=of32[:, b2, h, :], in_=of32[:, b2, h, :], func=AF.Identity, bias=stt[:, b2, 5, h:h + 1], scale=stt[:, b2, 4, h:h + 1]),
                             reads=[ofk, sk2], writes=[ofk])
                    ofl = of32[:, b2, :, :].rearrange("p h d -> p (h d)")
                    S.op("pool", lambda e, ofl=ofl: e.tensor_tensor(out=ofl, in0=ofl, in1=gg[:], op=ALU.mult), reads=[ofk, "gg"], writes=[ofk])
                    S.op("pool", lambda e, ofl=ofl: e.tensor_tensor(out=ofl, in0=ofl, in1=gb[:], op=ALU.add), reads=[ofk, "gb"], writes=[ofk])
                    S.op("act", lambda e, b2=b2: e.activation(out=zr[:, b2, :], in_=zr[:, b2, :], func=AF.Silu), reads=[zk], writes=[zk])
                    rb = t % 2
                    S.op("dve", lambda e, ofl=ofl, rb=rb, b2=b2: e.tensor_tensor(out=Rb[:, rb, :], in0=ofl, in1=zr[:, b2, :], op=ALU.mult), reads=[ofk, zk], writes=["Rb%d" % rb])
                    for kc in range(16):
                        S.op("pe", lambda e, rb=rb, kc=kc: e.transpose(out=psT[:, kc // 8, kc % 8, :], in_=Rb[:, rb, kc * 128:(kc + 1) * 128], identity=identb[:]),
                             reads=["Rb%d" % rb, "identb"], writes=["psT%d" % (kc // 8)], signal=(kc % 8 == 7))
                    for hh in range(2):
                        copy_op("act", RT[:, hh * 8:(hh + 1) * 8, cols], psT[:, hh, :, :], ["psT%d" % hh], ["RT%d" % t])

                p5A(0)
                for t in range(NT):
                    if t + 1 < NT:
                        p5A(t + 1)
                    p5B(t)
                S.barrier()
        S.set_phase(7)
        with ExitStack() as st6:
            gr = sb(st6, [128, 2, 512], F32, "gr")
            W = sb(st6, [128, 2, 16, 512], BF16, "W")
            Wh["W"] = W
            bufs = {0: load_w(w_pr, 0, 16)}
            it = 0
            for c in range(4):
                if c + 1 < 4:
                    bufs[c + 1] = load_w(w_pr, (c + 1) * 512, 16)
                wb = bufs[c]
                for t in range(NT):
                    pb = it % 2
                    rows = slice(t * 128, (t + 1) * 128)
                    def _gr_load(cc, tt, ii):
                        S.dma("sp", gr[:, ii % 2, :], U[tt * 128:(tt + 1) * 128, C_GR + cc * 512:C_GR + (cc + 1) * 512], writes=["gr%d" % (ii % 2)])
                    if it == 0:
                        _gr_load(0, 0, 0)
                    nxt = (c, t + 1) if t + 1 < NT else ((c + 1, 0) if c + 1 < 4 else None)
                    if nxt is not None:
                        _gr_load(nxt[0], nxt[1], it + 1)
                    S.op("act", lambda e, pb=pb: e.activation(out=gr[:, pb, :], in_=gr[:, pb, :], func=AF.Sigmoid), reads=["gr%d" % pb], writes=["gr%d" % pb])
                    for kc in range(16):
                        S.op("pe", lambda e, pb=pb, kc=kc, t=t, wb=wb: e.matmul(psA[:, pb, :], lhsT=RT[:, kc, t * 128:(t + 1) * 128], rhs=W[:, wb, kc, :], start=(kc == 0), stop=(kc == 15)),
                             reads=["RT%d" % t] + wkeys(wb, 16), writes=["psA%d" % pb], signal=(kc == 15))
                    S.op("dve", lambda e, pb=pb, t=t, c=c: e.tensor_tensor(out=mr[:, t, c * 512:(c + 1) * 512], in0=psA[:, pb, :], in1=gr[:, pb, :], op=ALU.mult),
                         reads=["psA%d" % pb, "gr%d" % pb], writes=["mr%d_%d" % (t, c)])
                    it += 1
            S.barrier()

        with ExitStack() as stA:
            AT = bufB[:, 0:8, :]
            with ExitStack() as st7:
                S.set_phase(11)
                with ExitStack() as st9:
                    za = sb(st9, [128, 2, 1024], F32, "za")
                    Ab = sb(st9, [128, 2, 1024], BF16, "Ab")
                    oab = sb(st9, [128, 2, 1024], BF16, "oab")
                    for t in range(NT):
                        b = t % 2
                        rows = slice(t * 128, (t + 1) * 128)
                        cols = slice(t * 128, (t + 1) * 128)
                        S.dma("sp", za[:, b, :], U[rows, C_ZA:C_ZA + 1024], writes=["za%d" % b])
                        S.dma("sp", oab[:, b, :], OA[rows, :], writes=["oab%d" % b])
                        S.op("act", lambda e, b=b: e.activation(out=za[:, b, :], in_=za[:, b, :], func=AF.Silu), reads=["za%d" % b], writes=["za%d" % b])
                        S.op("dve", lambda e, b=b, t=t: e.tensor_tensor(out=Ab[:, b, :], in0=oab[:, b, :], in1=za[:, b, :], op=ALU.mult), reads=["oab%d" % b, "za%d" % b], writes=["Ab%d" % b])
                        for kc in range(8):
                            S.op("pe", lambda e, b=b, kc=kc: e.transpose(out=psT[:, 0, kc, :], in_=Ab[:, b, kc * 128:(kc + 1) * 128], identity=identb[:]),
                                 reads=["Ab%d" % b, "identb"], writes=["psT0"], signal=(kc == 7))
                        copy_op(evq(), AT[:, :, cols], psT[:, 0, :, :], ["psT0"], ["AT%d" % t])
                S.barrier()
            S.set_phase(12)
            with ExitStack() as st9b:
                ga = sb(st9b, [128, 2, 512], F32, "ga")
                W = sb(st9b, [128, 2, 16, 512], BF16, "W")
                Wh["W"] = W
                tmpm = sb(st9b, [128, 2, 512], F32, "tmpm")
                bufs = {0: load_w(w_pa, 0, 8)}
                it = 0
                for c in range(4):
                    if c + 1 < 4:
                        bufs[c + 1] = load_w(w_pa, (c + 1) * 512, 8)
                    wb = bufs[c]
                    for t in range(NT):
                        pb = it % 2
                        rows = slice(t * 128, (t + 1) * 128)
                        def _ga_load(cc, tt, ii):
                            S.dma("sp", ga[:, ii % 2, :], U[tt * 128:(tt + 1) * 128, C_GA + cc * 512:C_GA + (cc + 1) * 512], writes=["ga%d" % (ii % 2)])
                        if it == 0:
                            _ga_load(0, 0, 0)
                        nxt = (c, t + 1) if t + 1 < NT else ((c + 1, 0) if c + 1 < 4 else None)
                        if nxt is not None:
                            _ga_load(nxt[0], nxt[1], it + 1)
                        S.op("act", lambda e, pb=pb: e.activation(out=ga[:, pb, :], in_=ga[:, pb, :], func=AF.Sigmoid), reads=["ga%d" % pb], writes=["ga%d" % pb])
                        for kc in range(8):
                            S.op("pe", lambda e, pb=pb, kc=kc, t=t, wb=wb: e.matmul(psA[:, pb, :], lhsT=AT[:, kc, t * 128:(t + 1) * 128], rhs=W[:, wb, kc, :], start=(kc == 0), stop=(kc == 7)),
                                 reads=["AT%d" % t] + wkeys(wb, 8), writes=["psA%d" % pb], signal=(kc == 7))
                        S.op("dve", lambda e, pb=pb: e.tensor_tensor(out=tmpm[:, pb, :], in0=psA[:, pb, :], in1=ga[:, pb, :], op=ALU.mult),
                             reads=["psA%d" % pb, "ga%d" % pb], writes=["tmpm%d" % pb])
                        mk = "mr%d_%d" % (t, c)
                        S.op("dve", lambda e, pb=pb, t=t, c=c: e.tensor_tensor(out=mr[:, t, c * 512:(c + 1) * 512], in0=tmpm[:, pb, :], in1=mr[:, t, c * 512:(c + 1) * 512], op=ALU.add),
                             reads=["tmpm%d" % pb, mk], writes=[mk])
                        it += 1
                S.barrier()
                for t in range(NT):
                    cols = slice(t * 128, (t + 1) * 128)
                    for kc in range(16):
                        S.op("pe", lambda e, t=t, kc=kc: e.transpose(out=psT[:, kc // 8, kc % 8, :], in_=mr[:, t, kc * 128:(kc + 1) * 128], identity=identb[:]),
                             reads=["mr%d_%d" % (t, kc // 4), "identb"], writes=["psT%d" % (kc // 8)], signal=(kc % 8 == 7))
                    for hh in range(2):
                        copy_op(evq(), RT[:, hh * 8:(hh + 1) * 8, cols], psT[:, hh, :, :], ["psT%d" % hh], ["RT%d" % t])
                S.barrier()
        S.set_phase(13)
        with ExitStack() as st10:
            xr = sb(st10, [128, 2, 512], F32, "xr")
            W = sb(st10, [128, 2, 16, 512], BF16, "W")
            Wh["W"] = W
            yo = sb(st10, [128, 2, 512], F32, "yo")
            bufs = {0: load_w(w_out, 0, 16)}
            it = 0
            for c in range(4):
                if c + 1 < 4:
                    bufs[c + 1] = load_w(w_out, (c + 1) * 512, 16)
                wb = bufs[c]
                cs = slice(c * 512, (c + 1) * 512)
                for t in range(NT):
                    pb = it % 2
                    def _xr_load(cc, tt, ii):
                        S.dma("sp", xr[:, ii % 2, :], x_all[tt, :, cc * 512:(cc + 1) * 512], writes=["xr%d" % (ii % 2)])
                    if it == 0:
                        _xr_load(0, 0, 0)
                    nxt = (c, t + 1) if t + 1 < NT else ((c + 1, 0) if c + 1 < 4 else None)
                    if nxt is not None:
                        _xr_load(nxt[0], nxt[1], it + 1)
                    for kc in range(16):
                        S.op("pe", lambda e, pb=pb, kc=kc, t=t, wb=wb: e.matmul(psA[:, pb, :], lhsT=RT[:, kc, t * 128:(t + 1) * 128], rhs=W[:, wb, kc, :], start=(kc == 0), stop=(kc == 15)),
                             reads=["RT%d" % t] + wkeys(wb, 16), writes=["psA%d" % pb], signal=(kc == 15))
                    S.op("dve", lambda e, pb=pb: e.tensor_tensor(out=yo[:, pb, :], in0=psA[:, pb, :], in1=xr[:, pb, :], op=ALU.add),
                         reads=["psA%d" % pb, "xr%d" % pb], writes=["yo%d" % pb])
                    if t < 8:
                        S.dma("act", y_p[t * 128:(t + 1) * 128, cs], yo[:, pb, :], reads=["yo%d" % pb])
                    else:
                        S.dma("act", y_s[:, cs], yo[0:64, pb, :], reads=["yo%d" % pb])
                    it += 1
            S.barrier()

    S.set_phase(0)
    S.barrier()
    with nc.Block() as block:
        @block.tensor
        def _(e):
            for f in S.prog["pe"]:
                f(e)

        @block.scalar
        def _(e):
            for f in S.prog["act"]:
                f(e)

        @block.vector
        def _(e):
            for f in S.prog["dve"]:
                f(e)

        @block.gpsimd
        def _(e):
            for f in S.prog["pool"]:
                f(e)

        @block.sync
        def _(e):
            for f in S.prog["sp"]:
                f(e)
    top.close()
    return nc


def _tables(core):
    s, r = core // 4, core % 4
    gam = GAM
    p = np.arange(128)
    posR = np.zeros((128, NT), np.float64)
    posA = np.zeros((128, NTH), np.float64)
    for t in range(8):
        posR[:, t] = 16 + r * 1024 + t * 128 + p
        posA[:, t] = posR[:, t]
    p8 = np.zeros(128)
    p8[:64] = 16384 + (p[:64] % 4)
    p8[64:80] = np.arange(16)
    posR[:, 8] = p8
    posA[:, 8] = p8
    posA[:, 9] = 16 + r * 1024 - 128 + p
    invR = np.exp(-math.log(10000.0) * 2.0 * np.arange(64, dtype=np.float32) / 128).astype(np.float32)
    invA = np.exp(-math.log(500000.0) * 2.0 * np.arange(8, dtype=np.float32) / 16).astype(np.float32)
    angR = posR.astype(np.float32)[:, :, None] * invR[None, None, :]
    angA = posA.astype(np.float32)[:, :, None] * invA[None, None, :]
    ropeR = np.stack([np.cos(angR), np.sin(angR)], axis=2).astype(np.float32)
    ropeA = np.stack([np.cos(angA), np.sin(angA)], axis=2).astype(np.float32)
    sc = 128.0 ** -0.5
    kwsc = np.zeros((128, NT, 8), np.float64)
    for h in range(8):
        kwsc[:, :8, h] = (gam[h] ** (127 - p))[:, None] * sc
        kwsc[:64, 8, h] = gam[h] ** (3 - (p[:64] % 4)) * sc
        kwsc[64:80, 8, h] = gam[h] ** (15 - np.arange(16)) * sc
    dmask = np.zeros((128, 8, 2, 128), np.float64)
    j = p[:, None]
    i = p[None, :]
    for h in range(8):
        dmask[:, h, 0, :] = np.where(i >= j, gam[h] ** np.maximum(i - j, 0), 0.0) * sc
        m1 = (i < 64) & (j < 64) & ((i // 4) == (j // 4)) & (i >= j)
        dmask[:, h, 1, :] = np.where(m1, gam[h] ** np.maximum(i - j, 0), 0.0) * sc
    qgt = np.zeros((128, 8, 128), np.float64)
    qgs = np.zeros((128, 8, 4), np.float64)
    for h in range(8):
        qgt[:, h, :] = (gam[h] ** (p + 1))[None, :]
        qgs[:, h, :] = (gam[h] ** (np.arange(4) + 1))[None, :]
    bsel = np.zeros((128, 16), np.float32)
    for q in range(64):
        bsel[q, q // 4] = 1.0
    amask = np.zeros((128, 3, 128), np.float32)
    amask[:, 0, :] = (j <= i)
    amask[:, 1, :] = (j > i)
    amask[:, 2, :] = (j > i) if r > 0 else 0.0
    smn = np.zeros((128, 64), np.float32)
    for kk in range(64):
        for q in range(64):
            if kk // 4 == q // 4 and kk % 4 <= q % 4:
                smn[kk, q] = 1.0
    smn[64:80, :] = 1.0
    smc = np.zeros((128, 16), np.float32)
    for g in range(4):
        for ii in range(4):
            smc[:, g * 4 + ii] = (p > ii)
    posP = np.zeros((128, NPRE), np.float64)
    kwp = np.zeros((128, NPRE, 8), np.float64)
    for jslot in range(3):
        ch = r - 1 - jslot
        for tt in range(8):
            ti = jslot * 8 + tt
            posP[:, ti] = 16 + max(ch, 0) * 1024 + tt * 128 + p
            for h in range(8):
                kwp[:, ti, h] = gam[h] ** (1024.0 * jslot + 1023 - (tt * 128 + p)) * sc
    angP = posP.astype(np.float32)[:, :, None] * invR[None, None, :]
    ropeP = np.stack([np.cos(angP), np.sin(angP)], axis=2).astype(np.float32)
    coef = np.zeros((128, 9, 8), np.float64)
    for h in range(8):
        for rp in range(r):
            coef[:, 4 * s + rp, h] = gam[h] ** (1024 * (r - 1 - rp))
        coef[:, 8, h] = gam[h] ** (1024 * r)
    return dict(ropeR=ropeR, ropeA=ropeA, kwsc=kwsc.astype(np.float32), dmask=dmask.astype(np.float32),
                qgt=qgt.astype(np.float32), qgs=qgs.astype(np.float32), bsel=bsel, amask=amask, smn=smn, smc=smc,
                coef=coef.astype(np.float32), ident=np.eye(128, dtype=np.float32), ropeP=ropeP, kwp=kwp.astype(np.float32))


_NC_CACHE = {}


def kernel(x_prompt, x_sample, cache_win_k, cache_win_v, state_ret, meta_tokens, norm_gain, w_in,
           q_norm_gain, k_norm_gain, attn_sinks, ret_gn_gain, ret_gn_bias, w_branch_attn, w_branch_ret, w_out):
    f = np.float32
    x_prompt = np.asarray(x_prompt, f)
    x_sample = np.asarray(x_sample, f)
    ck = np.asarray(cache_win_k, f)[0].reshape(128, 128, 256)
    cv = np.asarray(cache_win_v, f)[0].reshape(128, 128, 256)
    st = np.asarray(state_ret, f)[0]
    meta = np.asarray(meta_tokens, f)
    w_in_ = np.ascontiguousarray(np.asarray(w_in, f)[0])
    w_pa_ = np.ascontiguousarray(np.asarray(w_branch_attn, f)[0])
    w_pr_ = np.ascontiguousarray(np.asarray(w_branch_ret, f)[0])
    w_out_ = np.ascontiguousarray(np.asarray(w_out, f)[0])
    gqk = np.concatenate([np.tile(np.asarray(q_norm_gain, f)[0], 16), np.tile(np.asarray(k_norm_gain, f)[0], 4)])[None, :]
    if "nc" not in _NC_CACHE:
        _NC_CACHE["nc"] = build_program()
    nc = _NC_CACHE["nc"]
    in_maps = []
    for c in range(NCORES):
        s, r = c // 4, c % 4
        xa = np.zeros((NTH, 128, D), f)
        xa[:8] = x_prompt[s, r * 1024:(r + 1) * 1024].reshape(8, 128, D)
        xa[8, :64] = x_sample[16 * c:16 * c + 16].reshape(64, D)
        xa[8, 64:80] = meta
        if r > 0:
            xa[9] = x_prompt[s, r * 1024 - 128:r * 1024]
        m = dict(x_all=xa, w_in=w_in_, w_pa=w_pa_, w_pr=w_pr_, w_out=w_out_,
                 norm_g=np.asarray(norm_gain, f).reshape(1, D), gqk=np.ascontiguousarray(gqk),
                 sinks=np.asarray(attn_sinks, f).reshape(1, 16),
                 gn_g=np.asarray(ret_gn_gain, f).reshape(1, D), gn_b=np.asarray(ret_gn_bias, f).reshape(1, D),
                 cache_k=np.ascontiguousarray(ck[16 * c:16 * c + 16]), cache_v=np.ascontiguousarray(cv[16 * c:16 * c + 16]),
                 state=np.ascontiguousarray(st[16 * c:16 * c + 16]))
        xp = np.zeros((NPRE, 128, D), f)
        for jslot in range(3):
            ch = r - 1 - jslot
            if ch >= 0:
                xp[jslot * 8:(jslot + 1) * 8] = x_prompt[s, ch * 1024:(ch + 1) * 1024].reshape(8, 128, D)
        m["x_pre"] = xp
        m.update(_tables(c))
        in_maps.append(m)
    res = run_bass_kernel_spmd(nc, in_maps, core_ids=list(range(NCORES)))
    R = res.results
    y_prompt = np.zeros((2, 4096, D), f)
    y_sample = np.zeros((128, 4, D), f)
    wkp = np.zeros((1, 2, 128, 4, 64), f)
    wvp = np.zeros((1, 2, 128, 4, 64), f)
    retp = np.zeros((1, 2, 8, 128, 256), f)
    wks = np.zeros((1, 128, 128, 4, 64), f)
    wvs = np.zeros((1, 128, 128, 4, 64), f)
    rets = np.zeros((1, 128, 8, 128, 256), f)
    for c in range(NCORES):
        s, r = c // 4, c % 4
        y_prompt[s, r * 1024:(r + 1) * 1024] = R[c]["y_p"]
        y_sample[16 * c:16 * c + 16] = R[c]["y_s"].reshape(16, 4, D)
        if r == 3:
            wkp[0, s] = R[c]["wk_p"].reshape(128, 4, 64)
            wvp[0, s] = R[c]["wv_p"].reshape(128, 4, 64)
            retp[0, s] = R[c]["ret_p"]
        wks[0, 16 * c:16 * c + 16] = R[c]["wk_s"].reshape(16, 128, 4, 64)
        wvs[0, 16 * c:16 * c + 16] = R[c]["wv_s"].reshape(16, 128, 4, 64)
        rets[0, 16 * c:16 * c + 16] = R[c]["ret_s"]
    return (y_prompt, y_sample, wkp, wvp, retp, wks, wvs, rets)
```

```python
import math
import os
import types
from contextlib import ExitStack
import numpy as np
import concourse.bass as bass
import concourse.mybir as mybir
from concourse.bass_utils import run_bass_kernel_spmd

F32 = mybir.dt.float32
BF16 = mybir.dt.bfloat16
ALU = mybir.AluOpType
AF = mybir.ActivationFunctionType
AX = mybir.AxisListType

NCORES = 8
D = 2048
INW = 12800
NT = 9
NTH = 10
TOK = NT * 128
EPS = 1e-6
GN_EPS = 1e-5
C_QA, C_KA, C_VA, C_ZA, C_QR, C_KR, C_VR, C_ZR, C_GA, C_GR = 0, 1024, 1280, 1536, 2560, 3584, 4608, 6656, 8704, 10752
NDS = 30
SUB = os.environ.get("MK_SUB", "")
OVERLAP = os.environ.get("MK_OVERLAP", "1") == "1"
NPRE = 24

_lg = np.log(1.0 - np.exp(np.linspace(np.log(1.0 / 32), np.log(1.0 / 512), 8))).astype(np.float32)
GAM = np.exp(_lg.astype(np.float64))


def _freeze(fn):
    if fn.__closure__ is None:
        return fn
    cells = []
    for c in fn.__closure__:
        try:
            cells.append(types.CellType(c.cell_contents))
        except ValueError:
            cells.append(c)
    return types.FunctionType(fn.__code__, fn.__globals__, fn.__name__, fn.__defaults__, tuple(cells))


class Sched:
    def __init__(self, nc):
        self.nc = nc
        self.sem = {k: nc.alloc_semaphore(name="s_" + k) for k in ["pe", "act", "dve", "pool"]}
        self.cnt = {k: 0 for k in self.sem}
        self.dsem = [nc.alloc_semaphore(name="d%d" % i) for i in range(NDS)]
        self.dval = [0] * NDS
        self.dnext = {"sp": 0, "pool": 0, "act": 0}
        self.dring = {"sp": list(range(0, 14)), "pool": list(range(14, 24)), "act": list(range(24, NDS))}
        self.ccsem = nc.alloc_semaphore(name="ccs")
        self.queues = ["pe", "act", "dve", "pool", "sp"]
        self.seen = {q: {} for q in self.queues}
        self.reg = {}
        self.prog = {q: [] for q in self.queues}
        self.phase = 0
        self.maxphase = int(os.environ.get("MK_MAXPHASE", "99"))
        self.minphase = int(os.environ.get("MK_MINPHASE", "0"))

    def set_phase(self, n):
        self.phase = n

    @property
    def on(self):
        return self.phase <= self.maxphase and (self.phase == 0 or self.phase >= self.minphase)

    def _semh(self, k):
        if k[0] == "e":
            return self.sem[k[1]]
        if k[0] == "c":
            return self.ccsem
        return self.dsem[k[1]]

    def _deps(self, reads, writes):
        need = {}

        def add(st):
            if st is None:
                return
            k, t = st
            if need.get(k, 0) < t:
                need[k] = t
        for r in reads:
            e = self.reg.get(r)
            if e:
                add(e[0])
        for w in writes:
            e = self.reg.get(w)
            if e:
                add(e[0])
                for k, t in e[1].items():
                    add((k, t))
        return need

    def _emit_waits(self, q, need):
        for k, t in need.items():
            if k == ("e", "pe") and q == "pe":
                continue
            if self.seen[q].get(k, 0) >= t:
                continue
            self.seen[q][k] = t
            sem = self._semh(k)
            self.prog[q].append(lambda e, sem=sem, t=t: e.wait_ge(sem, t))

    def _mark(self, reads, writes, st):
        k, t = st
        for r in reads:
            e = self.reg.setdefault(r, [None, {}])
            e[1][k] = max(e[1].get(k, 0), t)
        for w in writes:
            self.reg[w] = [st, {}]

    def op(self, q, fn, reads=(), writes=(), signal=True):
        if not self.on:
            return
        fn = _freeze(fn)
        need = self._deps(reads, writes)
        self._emit_waits(q, need)
        tick = self.cnt[q] + 1
        if signal:
            self.cnt[q] = tick
            sem = self.sem[q]
            self.prog[q].append(lambda e, fn=fn, sem=sem: fn(e).then_inc(sem, 1))
        else:
            self.prog[q].append(lambda e, fn=fn: fn(e))
        self._mark(reads, writes, (("e", q), tick))

    def dma(self, q, out, in_, reads=(), writes=()):
        if not self.on:
            return
        need = self._deps(reads, writes)
        ring = self.dring[q]
        i = ring[self.dnext[q] % len(ring)]
        self.dnext[q] += 1
        if self.dval[i] > 0:
            need[("d", i)] = max(need.get(("d", i), 0), self.dval[i])
        self._emit_waits(q, need)
        self.dval[i] += 16
        v = self.dval[i]
        sem = self.dsem[i]
        self.prog[q].append(lambda e, out=out, in_=in_, sem=sem: e.dma_start(out=out, in_=in_).then_inc(sem, 16))
        self._mark(reads, writes, (("d", i), v))

    def barrier(self):
        if not self.on:
            return
        for q in self.queues:
            need = {}
            for k in self.sem:
                if self.cnt[k] > 0 and k != q:
                    need[("e", k)] = self.cnt[k]
            for i in range(NDS):
                if self.dval[i] > 0:
                    need[("d", i)] = self.dval[i]
            self._emit_waits(q, need)


def build_program():
    nc = bass.Bass("TRN2", target_bir_lowering=False)
    S = Sched(nc)

    def din(name, shape, dt=F32):
        return nc.dram_tensor(name, list(shape), dt, kind="ExternalInput").ap()

    def dout(name, shape):
        return nc.dram_tensor(name, list(shape), F32, kind="ExternalOutput").ap()

    x_all = din("x_all", [NTH, 128, D])
    w_in = din("w_in", [D, INW])
    w_pa = din("w_pa", [1024, D])
    w_pr = din("w_pr", [D, D])
    w_out = din("w_out", [D, D])
    norm_g = din("norm_g", [1, D])
    gqk_in = din("gqk", [1, 20 * 64])
    esink_in = din("sinks", [1, 16])
    gn_g = din("gn_g", [1, D])
    gn_b = din("gn_b", [1, D])
    cache_k = din("cache_k", [16, 128, 256])
    cache_v = din("cache_v", [16, 128, 256])
    state = din("state", [16, 8, 128, 256])
    ropeR_in = din("ropeR", [128, NT, 2, 64])
    ropeA_in = din("ropeA", [128, NTH, 2, 8])
    kwsc_in = din("kwsc", [128, NT, 8])
    dmask_in = din("dmask", [128, 8, 2, 128])
    qgt_in = din("qgt", [128, 8, 128])
    qgs_in = din("qgs", [128, 8, 4])
    bsel_in = din("bsel", [128, 16])
    amask_in = din("amask", [128, 3, 128])
    smn_in = din("smn", [128, 64])
    smc_in = din("smc", [128, 16])
    coef_in = din("coef", [128, 9, 8])
    ident_in = din("ident", [128, 128])
    x_pre = din("x_pre", [NPRE, 128, D])
    ropeP_in = din("ropeP", [128, NPRE, 2, 64])
    kwp_in = din("kwp", [128, NPRE, 8])

    y_p = dout("y_p", [1024, D])
    y_s = dout("y_s", [64, D])
    wk_p = dout("wk_p", [128, 256])
    wv_p = dout("wv_p", [128, 256])
    ret_p = dout("ret_p", [8, 128, 256])
    wk_s = dout("wk_s", [16, 128, 256])
    wv_s = dout("wv_s", [16, 128, 256])
    ret_s = dout("ret_s", [16, 8, 128, 256])

    U = nc.dram_tensor("U_scr", [NTH * 128, INW], F32, kind="Internal").ap()
    OA = nc.dram_tensor("OA_scr", [NT * 128, 1024], BF16, kind="Internal").ap()

    uid = [0]

    def sb(st, shape, dt, name):
        uid[0] += 1
        return st.enter_context(nc.sbuf_tensor("%s_%d" % (name, uid[0]), list(shape), dt))

    top = ExitStack()
    Wh = {}
    Spre = sb(top, [128, 8, 256], F32, "Spre")
    ident32 = sb(top, [128, 128], F32, "ident32")
    identb = sb(top, [128, 128], BF16, "identb")
    psA = top.enter_context(nc.psum_tensor("psA", [128, 2, 512], F32))
    psT = top.enter_context(nc.psum_tensor("psT", [128, 2, 8, 128], BF16))
    psM = top.enter_context(nc.psum_tensor("psM", [128, 4, 512], F32))

    S.dma("sp", ident32[:], ident_in, writes=["ident32"])
    S.op("dve", lambda e: e.tensor_copy(out=identb[:], in_=ident32[:]), reads=["ident32"], writes=["identb"])

    wstate = {"n": 0}

    def load_w(src, c0, nkc, ncols=512):
        b = wstate["n"] % 2
        wstate["n"] += 1
        srcv = src.rearrange("(kc p) n -> p kc n", p=128)
        W = Wh["W"]
        for k0 in range(0, nkc, 4):
            S.dma("pool", W[:, b, k0:k0 + 4, 0:ncols], srcv[:, k0:k0 + 4, c0:c0 + ncols], writes=["W%d_%d" % (b, k0)])
        return b

    def wkeys(b, nkc):
        return ["W%d_%d" % (b, k0) for k0 in range(0, nkc, 4)]

    rr = {"ev": 0}

    def evq():
        rr["ev"] += 1
        return "act" if rr["ev"] % 2 else "dve"

    def copy_op(q, out, in_, reads, writes):
        if q == "act":
            S.op("act", lambda e: e.activation(out=out, in_=in_, func=AF.Copy), reads=reads, writes=writes)
        else:
            S.op(q, lambda e: e.tensor_copy(out=out, in_=in_), reads=reads, writes=writes)


    S.set_phase(1)
    with ExitStack() as stP:
        Wkv = sb(stP, [128, 16, 3072], BF16, "Wkv")
        xstP = sb(stP, [128, 3, D], F32, "xstP")
        gbcP = sb(stP, [128, D], F32, "gbcP")
        sqjP = sb(stP, [128, D], BF16, "sqjP")
        xnbP = sb(stP, [128, 3, D], BF16, "xnbP")
        xnTp = sb(stP, [128, 2, 16, 128], BF16, "xnTp")
        ropeP = sb(stP, [128, NPRE, 2, 64], F32, "ropeP")
        kwp = sb(stP, [128, NPRE, 8], F32, "kwp")
        ssP = sb(stP, [128, NPRE], F32, "ssP")
        rsP = sb(stP, [128, NPRE], F32, "rsP")
        taP = sb(stP, [128, 4, 64], F32, "taP")
        tbP = sb(stP, [128, 4, 64], F32, "tbP")
        kro = sb(stP, [128, 8, 128], F32, "kro")
        kwb = sb(stP, [128, 2, 1024], BF16, "kwb")
        vb = sb(stP, [128, 2, D], BF16, "vb")
        ztile = sb(stP, [128, 512], BF16, "ztile")
        srcv = w_in.rearrange("(kc p) n -> p kc n", p=128)
        for cb in range(6):
            for k0 in range(0, 16, 4):
                S.dma("pool", Wkv[:, k0:k0 + 4, cb * 512:(cb + 1) * 512], srcv[:, k0:k0 + 4, C_KR + cb * 512:C_KR + (cb + 1) * 512], writes=["Wkv%d_%d" % (cb, k0)])
        S.dma("sp", gbcP[:], norm_g[0].partition_broadcast(128), writes=["gbcP"])
        S.dma("sp", ropeP[:], ropeP_in, writes=["ropeP"])
        S.dma("sp", kwp[:], kwp_in, writes=["kwp"])
        itpc = [0]

        def prepA(t):
            b = t % 3
            S.dma("sp", xstP[:, b, :], x_pre[t], writes=["xstP%d" % b])
            S.op("act", lambda e, b=b, t=t: e.activation(out=sqjP[:], in_=xstP[:, b, :], func=AF.Square, accum_out=ssP[:, t:t + 1]),
                 reads=["xstP%d" % b], writes=["sqjP", "ssP%d" % t])
            S.op("dve", lambda e, t=t: e.tensor_scalar(out=rsP[:, t:t + 1], in0=ssP[:, t:t + 1], scalar1=1.0 / D, scalar2=EPS, op0=ALU.mult, op1=ALU.add),
                 reads=["ssP%d" % t], writes=["rsP%d" % t])
            S.op("act", lambda e, t=t: e.activation(out=rsP[:, t:t + 1], in_=rsP[:, t:t + 1], func=AF.Sqrt), reads=["rsP%d" % t], writes=["rsP%d" % t])
            S.op("dve", lambda e, t=t: e.reciprocal(out=rsP[:, t:t + 1], in_=rsP[:, t:t + 1]), reads=["rsP%d" % t], writes=["rsP%d" % t])
            S.op("dve", lambda e, b=b, t=t: e.scalar_tensor_tensor(out=xnbP[:, b, :], in0=xstP[:, b, :], scalar=rsP[:, t:t + 1], in1=gbcP[:], op0=ALU.mult, op1=ALU.mult),
                 reads=["xstP%d" % b, "rsP%d" % t, "gbcP"], writes=["xnbP%d" % b])

        def prepB(t):
            b = t % 3
            b2 = t % 2
            for kc in range(16):
                S.op("pe", lambda e, b=b, kc=kc: e.transpose(out=psT[:, kc // 8, kc % 8, :], in_=xnbP[:, b, kc * 128:(kc + 1) * 128], identity=identb[:]),
                     reads=["xnbP%d" % b, "identb"], writes=["psT%d" % (kc // 8)], signal=(kc % 8 == 7))
            for hh in range(2):
                copy_op(evq(), xnTp[:, b2, hh * 8:(hh + 1) * 8, :], psT[:, hh, :, :], ["psT%d" % hh], ["xnTp%d" % b2])

        def stateP(t):
            b = t % 2
            for h in range(8):
                S.op("pe", lambda e, b=b, h=h, t=t: e.matmul(psM[:, h // 2, (h % 2) * 256:(h % 2) * 256 + 256], lhsT=kwb[:, b, h * 128:(h + 1) * 128], rhs=vb[:, b, h * 256:(h + 1) * 256],
                                                         start=False, stop=(t == NPRE - 1), skip_group_check=True),
                     reads=["kwb%d_%d" % (b, h // 4), "vb%d_%d" % (b, h // 2)], writes=["psMacc"], signal=(h == 7))

        def mainP(t):
            b = t % 2
            for cb in range(6):
                pb = itpc[0] % 2
                itpc[0] += 1
                for kc in range(16):
                    S.op("pe", lambda e, pb=pb, kc=kc, b=b, cb=cb: e.matmul(psA[:, pb, :], lhsT=xnTp[:, b, kc, :], rhs=Wkv[:, kc, cb * 512:(cb + 1) * 512], start=(kc == 0), stop=(kc == 15)),
                         reads=["xnTp%d" % b, "Wkv%d_%d" % (cb, (kc // 4) * 4)], writes=["psA%d" % pb], signal=(kc == 15))
                if cb < 2:
                    xv = psA[:, pb, :].rearrange("p (h two d) -> p h two d", h=4, two=2)
                    x1 = xv[:, :, 0, :]
                    x2 = xv[:, :, 1, :]
                    cosb = ropeP[:, t, 0, :].unsqueeze(1).to_broadcast([128, 4, 64])
                    sinb = ropeP[:, t, 1, :].unsqueeze(1).to_broadcast([128, 4, 64])
                    ov = kro[:, cb * 4:(cb + 1) * 4, :].rearrange("p h (two d) -> p h two d", two=2)
                    rk = ["psA%d" % pb, "ropeP"]
                    S.op("dve", lambda e, x1=x1, cosb=cosb: e.tensor_tensor(out=taP[:], in0=x1, in1=cosb, op=ALU.mult), reads=rk, writes=["taP"])
                    S.op("dve", lambda e, x2=x2, sinb=sinb: e.tensor_tensor(out=tbP[:], in0=x2, in1=sinb, op=ALU.mult), reads=rk, writes=["tbP"])
                    S.op("dve", lambda e, ov=ov: e.tensor_tensor(out=ov[:, :, 0, :], in0=taP[:], in1=tbP[:], op=ALU.subtract), reads=["taP", "tbP"], writes=["kro%d" % cb])
                    S.op("dve", lambda e, x2=x2, cosb=cosb: e.tensor_tensor(out=taP[:], in0=x2, in1=cosb, op=ALU.mult), reads=rk, writes=["taP"])
                    S.op("dve", lambda e, x1=x1, sinb=sinb: e.tensor_tensor(out=tbP[:], in0=x1, in1=sinb, op=ALU.mult), reads=rk, writes=["tbP"])
                    S.op("dve", lambda e, ov=ov: e.tensor_tensor(out=ov[:, :, 1, :], in0=taP[:], in1=tbP[:], op=ALU.add), reads=["taP", "tbP"], writes=["kro%d" % cb])
                    S.op("dve", lambda e, b=b, t=t, cb=cb: e.tensor_tensor(out=kwb[:, b, cb * 512:(cb + 1) * 512].rearrange("p (h d) -> p h d", h=4), in0=kro[:, cb * 4:(cb + 1) * 4, :],
                                                                     in1=kwp[:, t, cb * 4:(cb + 1) * 4].unsqueeze(2).to_broadcast([128, 4, 128]), op=ALU.mult),
                         reads=["kro%d" % cb, "kwp"], writes=["kwb%d_%d" % (b, cb)])
                else:
                    vc = cb - 2
                    copy_op("act", vb[:, b, vc * 512:(vc + 1) * 512], psA[:, pb, :], ["psA%d" % pb], ["vb%d_%d" % (b, vc)])
                if cb == 0 and t > 0:
                    stateP(t - 1)

        S.op("pool", lambda e: e.memset(ztile[:], 0.0), writes=["ztile"])
        for bk in range(4):
            S.op("pe", lambda e, bk=bk: e.matmul(psM[:, bk, :], lhsT=ztile[:, 0:128], rhs=ztile[:, :], start=True, stop=False, skip_group_check=True),
                 reads=["ztile"], writes=["psMacc"], signal=(bk == 3))
        prepA(0)
        prepA(1)
        prepB(0)
        for t in range(NPRE):
            if t + 2 < NPRE:
                prepA(t + 2)
            if t + 1 < NPRE:
                prepB(t + 1)
            mainP(t)
        stateP(NPRE - 1)
        for h in range(8):
            S.op("dve", lambda e, h=h: e.tensor_copy(out=Spre[:, h, :], in_=psM[:, h // 2, (h % 2) * 256:(h % 2) * 256 + 256]), reads=["psMacc"], writes=["Spre%d" % h])
        S.barrier()
    bufA = sb(top, [128, NT, D], BF16, "bufA")
    bufB = sb(top, [128, 16, TOK], BF16, "bufB")

    def attn_gen():
        with ExitStack() as st7:
            oast = sb(st7, [128, 2, 1024], BF16, "oast")
            bufBf = bufB[:].rearrange("p a b -> p (a b)")
            qTa = bufA[0:64, :, :].rearrange("p t d -> p (t d)").rearrange("p (h c) -> p h c", h=16)
            kTa = bufBf[0:64, 0:5120].rearrange("p (h t) -> p h t", h=4)
            vaug = bufBf[:, 5120:7720].rearrange("p (t h d) -> p t h d", t=NTH, h=4)
            vmeta = sb(st7, [16, 4, 65], BF16, "vmeta")
            ropeA = sb(st7, [128, NTH, 2, 8], F32, "ropeA")
            gqk = sb(st7, [128, 20, 64], F32, "gqk")
            esink = sb(st7, [128, 16], F32, "esink")
            amask = sb(st7, [128, 3, 128], F32, "amask")
            amb = sb(st7, [128, 3, 128], BF16, "amb")
            smn = sb(st7, [128, 64], F32, "smn")
            smc = sb(st7, [128, 16], F32, "smc")
            smnb = sb(st7, [128, 64], BF16, "smnb")
            smcb = sb(st7, [128, 16], BF16, "smcb")
            k32 = sb(st7, [128, 2, 4, 64], F32, "k32")
            S.dma("sp", ropeA[:], ropeA_in, writes=["ropeA"])
            S.dma("sp", gqk[:].rearrange("p h d -> p (h d)"), gqk_in[0].partition_broadcast(128), writes=["gqk"])
            S.dma("sp", esink[:], esink_in[0].partition_broadcast(128), writes=["esink"])
            S.op("act", lambda e: e.activation(out=esink[:], in_=esink[:], func=AF.Exp), reads=["esink"], writes=["esink"])
            S.dma("sp", amask[:], amask_in, writes=["amask"])
            S.dma("sp", smn[:], smn_in, writes=["smn"])
            S.dma("sp", smc[:], smc_in, writes=["smc"])
            S.op("dve", lambda e: e.tensor_copy(out=amb[:], in_=amask[:]), reads=["amask"], writes=["amb"])
            S.op("dve", lambda e: e.tensor_copy(out=smnb[:], in_=smn[:]), reads=["smn"], writes=["smnb"])
            S.op("dve", lambda e: e.tensor_copy(out=smcb[:], in_=smc[:]), reads=["smc"], writes=["smcb"])
            S.op("pool", lambda e: e.memset(vaug[:], 1.0), writes=["vaug%d" % t for t in range(NTH)])
            with ExitStack() as stp:
                ua = sb(stp, [128, 2, 1536], F32, "ua")
                sqa = sb(stp, [128, 20, 64], F32, "sqa")
                xna = sb(stp, [128, 20, 64], F32, "xna")
                ssa = sb(stp, [128, 20], F32, "ssa")
                r1 = sb(stp, [128, 20, 8], F32, "r1")
                r2 = sb(stp, [128, 20, 8], F32, "r2")
                qkb = sb(stp, [128, 2, 20, 64], BF16, "qkb")
                for t in range(NTH):
                    b = t % 2
                    rows = slice(t * 128, (t + 1) * 128)
                    h0 = 0 if t < 9 else 16
                    c0 = 0 if t < 9 else 1024
                    def _ua_load(tt):
                        cc0 = 0 if tt < 9 else 1024
                        S.dma("sp", ua[:, tt % 2, cc0:1536], U[tt * 128:(tt + 1) * 128, cc0:1536], writes=["ua%d" % (tt % 2)])
                    if t == 0:
                        _ua_load(0)
                    if t + 1 < NTH:
                        _ua_load(t + 1)
                    xq = ua[:, b, 0:1280].rearrange("p (h d) -> p h d", d=64)[:, h0:20, :]
                    nh = 20 - h0
                    S.op("act", lambda e, xq=xq, h0=h0: e.activation(out=sqa[:, h0:20, :], in_=xq, func=AF.Square), reads=["ua%d" % b], writes=["sqa"])
                    S.op("dve", lambda e, h0=h0: e.reduce_sum(out=ssa[:, h0:20], in_=sqa[:, h0:20, :], axis=AX.X), reads=["sqa"], writes=["ssa"])
                    S.op("dve", lambda e, h0=h0: e.tensor_scalar(out=ssa[:, h0:20], in0=ssa[:, h0:20], scalar1=1.0 / 64, scalar2=EPS, op0=ALU.mult, op1=ALU.add), reads=["ssa"], writes=["ssa"])
                    S.op("act", lambda e, h0=h0: e.activation(out=ssa[:, h0:20], in_=ssa[:, h0:20], func=AF.Sqrt), reads=["ssa"], writes=["ssa"])
                    S.op("dve", lambda e, h0=h0: e.reciprocal(out=ssa[:, h0:20], in_=ssa[:, h0:20]), reads=["ssa"], writes=["ssa"])
                    S.op("dve", lambda e, xq=xq, h0=h0, nh=nh: e.tensor_tensor(out=xna[:, h0:20, :], in0=xq, in1=ssa[:, h0:20].unsqueeze(2).to_broadcast([128, nh, 64]), op=ALU.mult),
                         reads=["ua%d" % b, "ssa"], writes=["xna"])
                    S.op("dve", lambda e, h0=h0: e.tensor_tensor(out=xna[:, h0:20, :], in0=xna[:, h0:20, :], in1=gqk[:, h0:20, :], op=ALU.mult), reads=["xna", "gqk"], writes=["xna"])
                    cosb = ropeA[:, t, 0, :].unsqueeze(1).to_broadcast([128, nh, 8])
                    sinb = ropeA[:, t, 1, :].unsqueeze(1).to_broadcast([128, nh, 8])
                    x1 = xna[:, h0:20, 0:8]
                    x2 = xna[:, h0:20, 8:16]
                    S.op("dve", lambda e, h0=h0, b=b: e.tensor_copy(out=qkb[:, b, h0:20, :], in_=xna[:, h0:20, :]), reads=["xna"], writes=["qkb%d" % b])
                    if t in (7, 8):
                        ki = t - 7
                        S.op("dve", lambda e, ki=ki: e.tensor_copy(out=k32[:, ki, :, :], in_=xna[:, 16:20, :]), reads=["xna"], writes=["k32_%d" % ki])
                    S.op("dve", lambda e, x1=x1, cosb=cosb, h0=h0: e.tensor_tensor(out=r1[:, h0:20, :], in0=x1, in1=cosb, op=ALU.mult), reads=["xna", "ropeA"], writes=["r1"])
                    S.op("dve", lambda e, x2=x2, sinb=sinb, h0=h0: e.tensor_tensor(out=r2[:, h0:20, :], in0=x2, in1=sinb, op=ALU.mult), reads=["xna", "ropeA"], writes=["r2"])
                    S.op("dve", lambda e, h0=h0, b=b: e.tensor_tensor(out=qkb[:, b, h0:20, 0:8], in0=r1[:, h0:20, :], in1=r2[:, h0:20, :], op=ALU.subtract), reads=["r1", "r2"], writes=["qkb%d" % b])
                    if t in (7, 8):
                        S.op("dve", lambda e, ki=ki: e.tensor_tensor(out=k32[:, ki, :, 0:8], in0=r1[:, 16:20, :], in1=r2[:, 16:20, :], op=ALU.subtract), reads=["r1", "r2"], writes=["k32_%d" % ki])
                    S.op("dve", lambda e, x2=x2, cosb=cosb, h0=h0: e.tensor_tensor(out=r1[:, h0:20, :], in0=x2, in1=cosb, op=ALU.mult), reads=["xna", "ropeA"], writes=["r1"])
                    S.op("dve", lambda e, x1=x1, sinb=sinb, h0=h0: e.tensor_tensor(out=r2[:, h0:20, :], in0=x1, in1=sinb, op=ALU.mult), reads=["xna", "ropeA"], writes=["r2"])
                    S.op("dve", lambda e, h0=h0, b=b: e.tensor_tensor(out=qkb[:, b, h0:20, 8:16], in0=r1[:, h0:20, :], in1=r2[:, h0:20, :], op=ALU.add), reads=["r1", "r2"], writes=["qkb%d" % b])
                    if t in (7, 8):
                        S.op("dve", lambda e, ki=ki: e.tensor_tensor(out=k32[:, ki, :, 8:16], in0=r1[:, 16:20, :], in1=r2[:, 16:20, :], op=ALU.add), reads=["r1", "r2"], writes=["k32_%d" % ki])
                    copy_op("act", vaug[:, t, :, 0:64], ua[:, b, 1280:1536].rearrange("p (h d) -> p h d", d=64), ["ua%d" % b], ["vaug%d" % t])
                    cols = slice(t * 128, (t + 1) * 128)
                    yield 3
                    if t < 9:
                        for j in range(16):
                            S.op("pe", lambda e, b=b, j=j: e.transpose(out=psT[0:64, j // 8, j % 8, :], in_=qkb[:, b, j, :], identity=identb[:]),
                                 reads=["qkb%d" % b, "identb"], writes=["psT%d" % (j // 8)], signal=(j % 8 == 7))
                        for hh in range(2):
                            copy_op(evq(), qTa[:, hh * 8:(hh + 1) * 8, cols], psT[0:64, hh, :, :], ["psT%d" % hh], ["qTa%d" % t])
                    for j in range(4):
                        S.op("pe", lambda e, b=b, j=j: e.transpose(out=psT[0:64, 0, j, :], in_=qkb[:, b, 16 + j, :], identity=identb[:]),
                             reads=["qkb%d" % b, "identb"], writes=["psT0"], signal=(j == 3))
                    copy_op(evq(), kTa[:, :, cols], psT[0:64, 0, 0:4, :], ["psT0"], ["kTa%d" % t])
                    yield
                S.dma("sp", wk_p, k32[:, 0, :, :].rearrange("p h d -> p (h d)"), reads=["k32_0"])
                S.dma("sp", wv_p, U[7 * 128:8 * 128, C_VA:C_VA + 256], reads=["U7_2"])
                S.dma("sp", wk_s[:, 0:124, :], cache_k[:, 4:128, :])
                S.dma("sp", wv_s[:, 0:124, :], cache_v[:, 4:128, :])
                for bq in range(16):
                    S.dma("sp", wk_s[bq, 124:128, :], k32[bq * 4:(bq + 1) * 4, 1, :, :].rearrange("p h d -> p (h d)"), reads=["k32_1"])
                    S.dma("sp", wv_s[bq, 124:128, :], U[8 * 128 + bq * 4:8 * 128 + (bq + 1) * 4, C_VA:C_VA + 256], reads=["U8_2"])
                S.dma("sp", vmeta[:], vaug[64:80, 8, :, :], reads=["vaug8"], writes=["vmeta"])
                S.barrier()
            with ExitStack() as stc:
                PT = sb(stc, [128, 2, 3, 512], BF16, "PT")
                den = sb(stc, [128, 2, 4], F32, "den")
                it = 0
                for t in range(8):
                    tp = t - 1 if t > 0 else 9
                    cols = slice(t * 128, (t + 1) * 128)
                    pcols = slice(tp * 128, (tp + 1) * 128)
                    for h in range(4):
                        pb = it % 2
                        it += 1
                        qrhs = qTa[:, 4 * h:4 * h + 4, cols]
                        qk_ = ["qTa%d" % t]
                        S.op("pe", lambda e, h=h, cols=cols, qrhs=qrhs: e.matmul(psM[:, 0, :], lhsT=kTa[:, h, cols], rhs=qrhs, start=True, stop=True), reads=qk_ + ["kTa%d" % t], writes=["psM0"])
                        S.op("pe", lambda e, h=h, pcols=pcols, qrhs=qrhs: e.matmul(psM[:, 1, :], lhsT=kTa[:, h, pcols], rhs=qrhs, start=True, stop=True), reads=qk_ + ["kTa%d" % tp], writes=["psM1"])
                        S.op("pe", lambda e, h=h, qrhs=qrhs: e.matmul(psM[0:16, 2, :], lhsT=kTa[:, h, 1024 + 64:1024 + 80], rhs=qrhs, start=True, stop=True), reads=qk_ + ["kTa8"], writes=["psM2"])
                        for j in range(3):
                            np_ = 16 if j == 2 else 128
                            S.op("act", lambda e, j=j, pb=pb, np_=np_: e.activation(out=PT[0:np_, pb, j, :], in_=psM[0:np_, j, :], func=AF.Exp, scale=0.125),
                                 reads=["psM%d" % j], writes=["PT%d_%d" % (pb, j)])
                        for j in range(2):
                            mi = j if (j == 0 or t > 0) else 2
                            pv = PT[:, pb, j, :].rearrange("p (g q) -> p g q", g=4)
                            S.op("dve", lambda e, pv=pv, mi=mi: e.tensor_tensor(out=pv, in0=pv, in1=amb[:, mi, :].unsqueeze(1).to_broadcast([128, 4, 128]), op=ALU.mult),
                                 reads=["PT%d_%d" % (pb, j), "amb"], writes=["PT%d_%d" % (pb, j)])
                        yield 1
                        for g in range(4):
                            gc = slice(g * 128, (g + 1) * 128)
                            oap = psM[:, 3, g * 65:(g + 1) * 65]
                            S.op("pe", lambda e, oap=oap, pb=pb, gc=gc, t=t, h=h: e.matmul(oap, lhsT=PT[:, pb, 0, gc], rhs=vaug[:, t, h, :], start=True, stop=False),
                                 reads=["PT%d_0" % pb, "vaug%d" % t], writes=["psM3"], signal=False)
                            S.op("pe", lambda e, oap=oap, pb=pb, gc=gc, tp=tp, h=h: e.matmul(oap, lhsT=PT[:, pb, 1, gc], rhs=vaug[:, tp, h, :], start=False, stop=False),
                                 reads=["PT%d_1" % pb, "vaug%d" % tp], writes=["psM3"], signal=False)
                            S.op("pe", lambda e, oap=oap, pb=pb, gc=gc, h=h: e.matmul(oap, lhsT=PT[0:16, pb, 2, gc], rhs=vmeta[:, h, :], start=False, stop=True),
                                 reads=["PT%d_2" % pb, "vmeta"], writes=["psM3"], signal=(g == 3))
                        ov = psM[:, 3, 0:260].rearrange("p (g d) -> p g d", g=4)
                        S.op("dve", lambda e, ov=ov, pb=pb, h=h: e.tensor_tensor(out=den[:, pb, :], in0=ov[:, :, 64], in1=esink[:, 4 * h:4 * h + 4], op=ALU.add), reads=["psM3", "esink"], writes=["den%d" % pb])
                        S.op("dve", lambda e, pb=pb: e.reciprocal(out=den[:, pb, :], in_=den[:, pb, :]), reads=["den%d" % pb], writes=["den%d" % pb])
                        S.op("dve", lambda e, ov=ov, pb=pb, t=t, h=h: e.tensor_tensor(out=oast[:, t % 2, h * 256:(h + 1) * 256].rearrange("p (g d) -> p g d", g=4), in0=ov[:, :, 0:64],
                                                                                      in1=den[:, pb, :].unsqueeze(2).to_broadcast([128, 4, 64]), op=ALU.mult),
                             reads=["psM3", "den%d" % pb], writes=["oast%d" % (t % 2)])
                        yield
                    S.dma("sp", OA[t * 128:(t + 1) * 128, :], oast[:, t % 2, :], reads=["oast%d" % (t % 2)])
                S.barrier()
            with ExitStack() as sts:
                Kc = sb(sts, [128, 16, 256], BF16, "Kc")
                Vc = sb(sts, [128, 16, 4, 65], BF16, "Vc")
                KcT = sb(sts, [64, 16, 128], BF16, "KcT")
                PTn = sb(sts, [128, 256], BF16, "PTn")
                PTc = sb(sts, [128, 256], BF16, "PTc")
                OT = sb(sts, [128, 256], F32, "OT")
                den2 = sb(sts, [128, 4], F32, "den2")
                S.op("pool", lambda e: e.memset(Vc[:], 1.0), writes=["Vc%d" % q_ for q_ in range(16)])
                S.dma("pool", Kc[:], cache_k.rearrange("b p c -> p b c"), writes=["Kc"])
                for bq in range(16):
                    S.dma("pool", Vc[:, bq, :, 0:64], cache_v[bq].rearrange("p (h d) -> p h d", h=4), writes=["Vc%d" % bq])
                S.op("pool", lambda e: e.memset(oast[:, 0, :], 0.0), writes=["oast0"])
                for h in range(4):
                    S.op("pe", lambda e, h=h: e.matmul(psM[0:80, 0, 0:256], lhsT=kTa[:, h, 1024:1104], rhs=qTa[:, 4 * h:4 * h + 4, 1024:1088], start=True, stop=True),
                         reads=["kTa8", "qTa8"], writes=["psM0"])
                    S.op("act", lambda e: e.activation(out=PTn[0:80, :], in_=psM[0:80, 0, 0:256], func=AF.Exp, scale=0.125), reads=["psM0"], writes=["PTn"])
                    pnv = PTn[0:80, :].rearrange("p (g q) -> p g q", g=4)
                    S.op("dve", lambda e, pnv=pnv: e.tensor_tensor(out=pnv, in0=pnv, in1=smnb[0:80, :].unsqueeze(1).to_broadcast([80, 4, 64]), op=ALU.mult), reads=["PTn", "smnb"], writes=["PTn"])
                    for bq in range(16):
                        S.op("pe", lambda e, bq=bq, h=h: e.transpose(out=psT[0:64, bq // 8, bq % 8, :], in_=Kc[:, bq, h * 64:(h + 1) * 64], identity=identb[:]),
                             reads=["Kc", "identb"], writes=["psT%d" % (bq // 8)], signal=(bq % 8 == 7))
                    for hh in range(2):
                        copy_op(evq(), KcT[:, hh * 8:(hh + 1) * 8, :], psT[0:64, hh, :, :], ["psT%d" % hh], ["KcT"])
                    yield 1
                    for bq in range(16):
                        S.op("pe", lambda e, bq=bq, h=h: e.matmul(psM[:, 1, bq * 16:(bq + 1) * 16], lhsT=KcT[:, bq, :],
                                                                  rhs=qTa[:, 4 * h:4 * h + 4, 1024 + 4 * bq:1024 + 4 * bq + 4], start=True, stop=True),
                             reads=["KcT", "qTa8"], writes=["psM1"], signal=(bq == 15))
                    S.op("act", lambda e: e.activation(out=PTc[:], in_=psM[:, 1, 0:256], func=AF.Exp, scale=0.125), reads=["psM1"], writes=["PTc"])
                    pcv = PTc[:].rearrange("p (b k) -> p b k", b=16)
                    S.op("dve", lambda e, pcv=pcv: e.tensor_tensor(out=pcv, in0=pcv, in1=smcb[:].unsqueeze(1).to_broadcast([128, 16, 16]), op=ALU.mult), reads=["PTc", "smcb"], writes=["PTc"])
                    yield 1
                    S.op("pe", lambda e, h=h: e.matmul(psM[0:65, 2, 0:256], lhsT=vaug[0:80, 8, h, :], rhs=PTn[0:80, :].rearrange("p (g b i) -> p b g i", g=4, b=16), start=True, stop=False),
                         reads=["vaug8", "PTn"], writes=["psM2"], signal=False)
                    for bq in range(16):
                        S.op("pe", lambda e, bq=bq, h=h: e.matmul(psM[0:65, 2, bq * 16:(bq + 1) * 16], lhsT=Vc[:, bq, h, :],
                                                                  rhs=PTc[:, bq * 16:(bq + 1) * 16], start=False, stop=(bq == 15)),
                             reads=["Vc%d" % bq, "PTc"], writes=["psM2"], signal=(bq == 15))
                    copy_op("act", OT[0:65, :].rearrange("p (g b i) -> p g b i", g=4, b=16), psM[0:65, 2, 0:256].rearrange("p (b g i) -> p g b i", b=16, g=4), ["psM2"], ["OT"])
                    yield 1
                    for g in range(4):
                        S.op("pe", lambda e, g=g: e.transpose(out=psM[0:64, 3, g * 65:(g + 1) * 65], in_=OT[0:65, g * 64:(g + 1) * 64], identity=ident32[0:65, 0:65]),
                             reads=["OT", "ident32"], writes=["psM3"], signal=(g == 3))
                    ov = psM[0:64, 3, 0:260].rearrange("p (g d) -> p g d", g=4)
                    S.op("dve", lambda e, ov=ov, h=h: e.tensor_tensor(out=den2[0:64, :], in0=ov[:, :, 64], in1=esink[0:64, 4 * h:4 * h + 4], op=ALU.add), reads=["psM3", "esink"], writes=["den2"])
                    S.op("dve", lambda e: e.reciprocal(out=den2[0:64, :], in_=den2[0:64, :]), reads=["den2"], writes=["den2"])
                    S.op("dve", lambda e, ov=ov, h=h: e.tensor_tensor(out=oast[0:64, 0, h * 256:(h + 1) * 256].rearrange("p (g d) -> p g d", g=4), in0=ov[:, :, 0:64],
                                                                      in1=den2[0:64, :].unsqueeze(2).to_broadcast([64, 4, 64]), op=ALU.mult),
                         reads=["psM3", "den2"], writes=["oast0"])
                    yield
                S.dma("sp", OA[8 * 128:9 * 128, :], oast[:, 0, :], reads=["oast0"])
                S.barrier()
        yield

    S.set_phase(2)
    with ExitStack() as st:
        xnT = sb(st, [128, 16, NTH * 128], BF16, "xnT")
        with ExitStack() as st1:
            xst = sb(st1, [128, 2, D], F32, "xst")
            gbc = sb(st1, [128, D], F32, "gbc")
            xnb = sb(st1, [128, 2, D], BF16, "xnb")
            sqj = sb(st1, [128, D], BF16, "sqj")
            ss = sb(st1, [128, NTH], F32, "ss")
            rstd = sb(st1, [128, NTH], F32, "rstd")
            S.dma("sp", gbc[:], norm_g[0].partition_broadcast(128), writes=["gbc"])
            for t in range(NTH):
                b = t % 2
                S.dma("sp", xst[:, b, :], x_all[t], writes=["xst%d" % b])
                S.op("act", lambda e, b=b, t=t: e.activation(out=sqj[:], in_=xst[:, b, :], func=AF.Square, accum_out=ss[:, t:t + 1]),
                     reads=["xst%d" % b], writes=["sqj", "ss%d" % t])
                S.op("dve", lambda e, t=t: e.tensor_scalar(out=rstd[:, t:t + 1], in0=ss[:, t:t + 1], scalar1=1.0 / D, scalar2=EPS, op0=ALU.mult, op1=ALU.add),
                     reads=["ss%d" % t], writes=["rstd%d" % t])
                S.op("act", lambda e, t=t: e.activation(out=rstd[:, t:t + 1], in_=rstd[:, t:t + 1], func=AF.Sqrt), reads=["rstd%d" % t], writes=["rstd%d" % t])
                S.op("dve", lambda e, t=t: e.reciprocal(out=rstd[:, t:t + 1], in_=rstd[:, t:t + 1]), reads=["rstd%d" % t], writes=["rstd%d" % t])
                S.op("dve", lambda e, b=b, t=t: e.scalar_tensor_tensor(out=xnb[:, b, :], in0=xst[:, b, :], scalar=rstd[:, t:t + 1], in1=gbc[:], op0=ALU.mult, op1=ALU.mult),
                     reads=["xst%d" % b, "rstd%d" % t, "gbc"], writes=["xnb%d" % b])
                for kc in range(16):
                    S.op("pe", lambda e, b=b, kc=kc: e.transpose(out=psT[:, kc // 8, kc % 8, :], in_=xnb[:, b, kc * 128:(kc + 1) * 128], identity=identb[:]),
                         reads=["xnb%d" % b, "identb"], writes=["psT%d" % (kc // 8)], signal=(kc % 8 == 7))
                for hh in range(2):
                    copy_op(evq(), xnT[:, hh * 8:(hh + 1) * 8, t * 128:(t + 1) * 128], psT[:, hh, :, :], ["psT%d" % hh], ["xnT%d" % t])
            S.barrier()
        S.set_phase(3)
        with ExitStack() as st2:
            ust = sb(st2, [128, 4, 512], F32, "ust")
            W = sb(st2, [128, 2, 16, 512], BF16, "W")
            Wh["W"] = W
            nblk = INW // 512

            def inproj_gen():
                bufs = {0: load_w(w_in, 0, 16)}
                it = 0
                for c in range(nblk):
                    if c + 1 < nblk:
                        bufs[c + 1] = load_w(w_in, (c + 1) * 512, 16)
                    wb = bufs[c]
                    tiles = list(range(NT)) + ([9] if c == 2 else [])
                    for t in tiles:
                        pb = it % 2
                        for kc in range(16):
                            S.op("pe", lambda e, pb=pb, kc=kc, t=t, wb=wb: e.matmul(psA[:, pb, :], lhsT=xnT[:, kc, t * 128:(t + 1) * 128], rhs=W[:, wb, kc, :], start=(kc == 0), stop=(kc == 15)),
                                 reads=["xnT%d" % t] + wkeys(wb, 16), writes=["psA%d" % pb], signal=(kc == 15))
                        ub = it % 4
                        copy_op("act", ust[:, ub, :], psA[:, pb, :], ["psA%d" % pb], ["ust%d" % ub])
                        S.dma("sp", U[t * 128:(t + 1) * 128, c * 512:(c + 1) * 512], ust[:, ub, :], reads=["ust%d" % ub], writes=["U%d_%d" % (t, c)])
                        it += 1
                        yield c

            gB = None
            alive = True
            wait = 0
            for c in inproj_gen():
                if c >= 3 and OVERLAP:
                    if gB is None:
                        S.barrier()
                        gB = attn_gen()
                    wait -= 1
                    if alive and wait <= 0:
                        r = next(gB, "END")
                        if r == "END":
                            alive = False
                        else:
                            wait = r or 1
            if gB is None:
                gB = attn_gen()
            for _ in gB:
                pass
            S.barrier()

    with ExitStack() as st:
        RT = bufB
        mr = bufA
        Sfin_keep = None
        with ExitStack() as stR:
            olocal = bufA
            qgall = sb(stR, [128, 8, 1024], BF16, "qgall")
            Send = sb(stR, [128, 8, 256], F32, "Send")
            Sm = sb(stR, [128, 8, 256], F32, "Sm")
            with ExitStack() as st3:
                qT = bufB[:, 0:8, :]
                kT = bufB[:, 8:16, :]
                kw = sb(st3, [128, NT, 1024], BF16, "kw")
                vtm = sb(st3, [128, NT, D], BF16, "vtm")
                qgs = sb(st3, [128, 8, 4], F32, "qgs")
                bsel = sb(st3, [128, 16], F32, "bsel")
                S.dma("sp", qgs[:], qgs_in, writes=["qgs"])
                S.dma("sp", bsel[:], bsel_in, writes=["bsel"])
                S.set_phase(4)
                with ExitStack() as stp:
                    uq = sb(stp, [128, 1, 2048], F32, "uq")
                    ropeR = sb(stp, [128, NT, 2, 64], F32, "ropeR")
                    kwsc = sb(stp, [128, NT, 8], F32, "kwsc")
                    qgt = sb(stp, [128, 8, 128], F32, "qgt")
                    S.dma("sp", ropeR[:], ropeR_in, writes=["ropeR"])
                    S.dma("sp", kwsc[:], kwsc_in, writes=["kwsc"])
                    S.dma("sp", qgt[:], qgt_in, writes=["qgt"])
                    ta = sb(stp, [128, 16, 64], F32, "ta")
                    tb = sb(stp, [128, 16, 64], F32, "tb")
                    qkr = sb(stp, [128, 1, 16, 128], BF16, "qkr")
                    for t in range(NT):
                        b = 0
                        rows = slice(t * 128, (t + 1) * 128)
                        S.dma("sp", uq[:, b, :], U[rows, C_QR:C_QR + 2048], writes=["uq%d" % b])
                        for vq in range(4 if SUB != "a" else 0):
                            S.dma("pool", vtm[:, t, vq * 512:(vq + 1) * 512], U[rows, C_VR + vq * 512:C_VR + (vq + 1) * 512], writes=["vtm%d_%d" % (t, vq)])
                        xv = uq[:, b, :].rearrange("p (h two d) -> p h two d", h=16, two=2)
                        x1 = xv[:, :, 0, :]
                        x2 = xv[:, :, 1, :]
                        cosb = ropeR[:, t, 0, :].unsqueeze(1).to_broadcast([128, 16, 64])
                        sinb = ropeR[:, t, 1, :].unsqueeze(1).to_broadcast([128, 16, 64])
                        ov = qkr[:, b, :, :].rearrange("p h (two d) -> p h two d", two=2)
                        rk = ["uq%d" % b, "ropeR"]
                        S.op("dve", lambda e, x1=x1, cosb=cosb: e.tensor_tensor(out=ta[:], in0=x1, in1=cosb, op=ALU.mult), reads=rk, writes=["ta"])
                        S.op("dve", lambda e, x2=x2, sinb=sinb: e.tensor_tensor(out=tb[:], in0=x2, in1=sinb, op=ALU.mult), reads=rk, writes=["tb"])
                        S.op("dve", lambda e, ov=ov: e.tensor_tensor(out=ov[:, :, 0, :], in0=ta[:], in1=tb[:], op=ALU.subtract), reads=["ta", "tb"], writes=["qkr%d" % b])
                        S.op("dve", lambda e, x2=x2, cosb=cosb: e.tensor_tensor(out=ta[:], in0=x2, in1=cosb, op=ALU.mult), reads=rk, writes=["ta"])
                        S.op("dve", lambda e, x1=x1, sinb=sinb: e.tensor_tensor(out=tb[:], in0=x1, in1=sinb, op=ALU.mult), reads=rk, writes=["tb"])
                        S.op("dve", lambda e, ov=ov: e.tensor_tensor(out=ov[:, :, 1, :], in0=ta[:], in1=tb[:], op=ALU.add), reads=["ta", "tb"], writes=["qkr%d" % b])
                        S.op("dve", lambda e, b=b, t=t: e.tensor_tensor(out=kw[:, t, :].rearrange("p (h d) -> p h d", h=8), in0=qkr[:, b, 8:16, :],
                                                                        in1=kwsc[:, t, :].unsqueeze(2).to_broadcast([128, 8, 128]), op=ALU.mult),
                             reads=["qkr%d" % b, "kwsc"], writes=["kw%d" % t])
                        if SUB == "b":
                            continue
                        for j in range(16):
                            S.op("pe", lambda e, b=b, j=j: e.transpose(out=psT[:, j // 8, j % 8, :], in_=qkr[:, b, j, :], identity=identb[:]),
                                 reads=["qkr%d" % b, "identb"], writes=["psT%d" % (j // 8)], signal=(j % 8 == 7))
                        cols = slice(t * 128, (t + 1) * 128)
                        copy_op("act", qT[:, :, cols], psT[:, 0, :, :], ["psT0"], ["qT%d" % t])
                        if t < 8:
                            S.op("dve", lambda e, cols=cols: e.tensor_tensor(out=qgall[:, :, cols], in0=qT[:, :, cols], in1=qgt[:], op=ALU.mult),
                                 reads=["qT%d" % t, "qgt"], writes=["qg%d" % t])
                        copy_op("act", kT[:, :, cols], psT[:, 1, :, :], ["psT1"], ["kT%d" % t])
                    S.barrier()
                S.set_phase(5)
                with ExitStack() as stc:
                    innerT = sb(stc, [128, 2, 128], BF16, "innerT")
                    dmask = sb(stc, [128, 8, 2, 128], F32, "dmask")
                    S.dma("sp", dmask[:], dmask_in, writes=["dmask"])
                    Slbf = sb(stc, [128, 8, 2, 256], BF16, "Slbf")
                    qz = sb(stc, [128, 16 * 68], F32, "qz")
                    kz = sb(stc, [128, 16, 128], BF16, "kz")
                    S32 = sb(stc, [128, 3, 4, 256], F32, "S32")
                    S.op("pool", lambda e: e.memset(qz[:], 0.0), writes=["qz"])
                    S.op("pool", lambda e: e.memset(olocal[:, 8, :], 0.0), writes=["ol8"])
                    qd_ctr = 0
                    for t in range(8):
                        cols = slice(t * 128, (t + 1) * 128)
                        vk = ["vtm%d_%d" % (t, i) for i in range(4)]
                        for h in range(8):
                            par = h % 2
                            ba, bc = 2 * par, 2 * par + 1
                            hc = slice(h * 256, (h + 1) * 256)
                            g128 = float(GAM[h] ** 128)
                            S.op("pe", lambda e, h=h, cols=cols, ba=ba: e.matmul(psM[:, ba, 0:128], lhsT=kT[:, h, cols], rhs=qT[:, h, cols], start=True, stop=True),
                                 reads=["kT%d" % t, "qT%d" % t], writes=["pb%d" % par])
                            S.op("dve", lambda e, h=h, par=par, ba=ba: e.tensor_tensor(out=innerT[:, par, :], in0=psM[:, ba, 0:128], in1=dmask[:, h, 0, :], op=ALU.mult),
                                 reads=["pb%d" % par, "dmask"], writes=["innerT%d" % par])
                            S.op("pe", lambda e, par=par, ba=ba, t=t, hc=hc: e.matmul(psM[:, ba, 128:384], lhsT=innerT[:, par, :], rhs=vtm[:, t, hc], start=True, stop=(t == 0)),
                                 reads=["innerT%d" % par] + vk, writes=["pb%d" % par], signal=(t == 0))
                            if t > 0:
                                S.op("pe", lambda e, h=h, cols=cols, ba=ba, t=t: e.matmul(psM[:, ba, 128:384], lhsT=qgall[:, h, cols], rhs=Slbf[:, h, t % 2, :], start=False, stop=True),
                                     reads=["qg%d" % t, "Slbf%d_%d" % (h, t % 2)], writes=["pb%d" % par])
                            copy_op("act", olocal[:, t, hc], psM[:, ba, 128:384], ["pb%d" % par], ["ol%d_%d" % (t, h)])
                            S.op("pe", lambda e, h=h, t=t, hc=hc, bc=bc: e.matmul(psM[:, bc, 0:256], lhsT=kw[:, t, h * 128:(h + 1) * 128], rhs=vtm[:, t, hc], start=True, stop=True),
                                 reads=["kw%d" % t] + vk, writes=["pc%d" % par])
                            if t == 0:
                                S.op("dve", lambda e, h=h, bc=bc: e.tensor_copy(out=Send[:, h, :], in_=psM[:, bc, 0:256]), reads=["pc%d" % par], writes=["Send%d" % h])
                            else:
                                S.op("dve", lambda e, h=h, g128=g128, bc=bc: e.scalar_tensor_tensor(out=Send[:, h, :], in0=Send[:, h, :], scalar=g128, in1=psM[:, bc, 0:256], op0=ALU.mult, op1=ALU.add),
                                     reads=["pc%d" % par, "Send%d" % h], writes=["Send%d" % h])
                            if t < 7:
                                copy_op("act", Slbf[:, h, (t + 1) % 2, :], Send[:, h, :], ["Send%d" % h], ["Slbf%d_%d" % (h, (t + 1) % 2)])
                    S.barrier()

                    def issue_state_load(idx):
                        hh_, qd_ = idx // 4, idx % 4
                        bi_ = idx % 3
                        src_ = state[qd_ * 4:(qd_ + 1) * 4, hh_, :, :].rearrange("b p d -> p b d")
                        S.dma("sp", S32[:, bi_, :, :], src_, writes=["S32_%d" % bi_])
                    issue_state_load(0)
                    issue_state_load(1)
                    for h in range(8):
                        hc = slice(h * 256, (h + 1) * 256)
                        g4 = float(GAM[h] ** 4)
                        cols8 = slice(1024, 1152)
                        S.op("pe", lambda e, h=h: e.matmul(psM[:, 0, 0:128], lhsT=kT[:, h, cols8], rhs=qT[:, h, cols8], start=True, stop=True),
                             reads=["kT8", "qT8"], writes=["psM0"])
                        S.op("dve", lambda e, h=h: e.tensor_tensor(out=innerT[:, 0, :], in0=psM[:, 0, 0:128], in1=dmask[:, h, 1, :], op=ALU.mult),
                             reads=["psM0", "dmask"], writes=["innerT0"])
                        S.op("dve", lambda e, h=h: e.tensor_tensor(out=qz[:].rearrange("p (b s) -> p b s", s=68)[:, :, 0:4],
                                                                   in0=qT[:, h, 1024:1088].rearrange("p (b i) -> p b i", i=4),
                                                                   in1=qgs[:, h, :].unsqueeze(1).to_broadcast([128, 16, 4]), op=ALU.mult),
                             reads=["qT8", "qgs"], writes=["qz"])
                        S.op("dve", lambda e, h=h: e.tensor_tensor(out=kz[0:64, :, :], in0=kw[0:64, 8, h * 128:(h + 1) * 128].unsqueeze(1).to_broadcast([64, 16, 128]),
                                                                   in1=bsel[0:64, :].unsqueeze(2).to_broadcast([64, 16, 128]), op=ALU.mult),
                             reads=["kw8", "bsel"], writes=["kz"])
                        S.op("pe", lambda e, hc=hc: e.matmul(psM[0:64, 1, 0:256], lhsT=innerT[:, 0, 0:64], rhs=vtm[:, 8, hc], start=True, stop=False),
                             reads=["innerT0", "vtm8_0", "vtm8_1", "vtm8_2", "vtm8_3"], writes=["psM1"], signal=False)
                        for qd in range(4):
                            sbuf_i = qd_ctr % 3
                            if qd_ctr + 2 < 32:
                                issue_state_load(qd_ctr + 2)
                            qd_ctr += 1
                            for bb in range(4):
                                b = qd * 4 + bb
                                last = (b == 15)
                                S.op("pe", lambda e, b=b, bb=bb, sbuf_i=sbuf_i, last=last: e.matmul(psM[0:64, 1, 0:256], lhsT=qz[:, b * 64:(b + 1) * 64], rhs=S32[:, sbuf_i, bb, :], start=False, stop=last),
                                     reads=["qz", "S32_%d" % sbuf_i], writes=["psM1"], signal=last)
                            for bb in range(4):
                                b = qd * 4 + bb
                                pci = b % 4
                                pcap = psM[:, 2 + pci, 0:256] if pci < 2 else psA[:, pci - 2, 0:256]
                                pck = "pcs%d" % pci
                                S.op("pe", lambda e, b=b, pcap=pcap, hc=hc: e.matmul(pcap, lhsT=kz[0:64, b, :], rhs=vtm[0:64, 8, hc], start=True, stop=True),
                                     reads=["kz", "vtm8_0", "vtm8_1", "vtm8_2", "vtm8_3"], writes=[pck])
                                S.op("dve", lambda e, sbuf_i=sbuf_i, bb=bb, pcap=pcap, g4=g4: e.scalar_tensor_tensor(out=S32[:, sbuf_i, bb, :], in0=S32[:, sbuf_i, bb, :], scalar=g4, in1=pcap, op0=ALU.mult, op1=ALU.add),
                                     reads=[pck, "S32_%d" % sbuf_i], writes=["S32_%d" % sbuf_i])
                            S.dma("act", ret_s[qd * 4:(qd + 1) * 4, h, :, :].rearrange("b p d -> p b d"), S32[:, sbuf_i, :, :], reads=["S32_%d" % sbuf_i])
                        copy_op("act", olocal[0:64, 8, hc], psM[0:64, 1, 0:256], ["psM1"], ["ol8"])
                        S.op("pe", lambda e, h=h, hc=hc: e.matmul(psM[:, 2, 0:256], lhsT=kw[64:80, 8, h * 128:(h + 1) * 128], rhs=vtm[64:80, 8, hc], start=True, stop=True),
                             reads=["kw8", "vtm8_0", "vtm8_1", "vtm8_2", "vtm8_3"], writes=["pcs0"])
                        S.op("dve", lambda e, h=h: e.tensor_copy(out=Sm[:, h, :], in_=psM[:, 2, 0:256]), reads=["pcs0"], writes=["Sm%d" % h])
                S.barrier()
            S.set_phase(6)
            with ExitStack() as st5:
                coef = sb(st5, [128, 9, 8], F32, "coef")
                Sst = sb(st5, [128, 8, 256], F32, "Sst")
                Sstb = sb(st5, [128, 8, 256], BF16, "Sstb")
                of32 = sb(st5, [128, 2, 8, 256], F32, "of32")
                sq32 = sb(st5, [128, 8, 256], F32, "sq32")
                zr = sb(st5, [128, 2, D], F32, "zr")
                gg = sb(st5, [128, D], F32, "gg")
                gb = sb(st5, [128, D], F32, "gb")
                Rb = sb(st5, [128, 2, D], BF16, "Rb")
                stt = sb(st5, [128, 2, 6, 8], F32, "stt")
                S.dma("sp", coef[:], coef_in, writes=["coef"])
                S.dma("sp", gg[:], gn_g[0].partition_broadcast(128), writes=["gg"])
                S.dma("sp", gb[:], gn_b[0].partition_broadcast(128), writes=["gb"])
                for h in range(8):
                    S.op("dve", lambda e, h=h: e.scalar_tensor_tensor(out=Sst[:, h, :], in0=Sm[:, h, :], scalar=coef[:, 8, h:h + 1], in1=Spre[:, h, :], op0=ALU.mult, op1=ALU.add),
                         reads=["Sm%d" % h, "coef"], writes=["Sst%d" % h])
                sk = ["Sst%d" % h for h in range(8)]
                copy_op("act", Sstb[:], Sst[:], sk, ["Sstb"])
                for h in range(8):
                    g1024 = float(GAM[h] ** 1024)
                    S.op("dve", lambda e, h=h, g1024=g1024: e.scalar_tensor_tensor(out=Send[:, h, :], in0=Sst[:, h, :], scalar=g1024, in1=Send[:, h, :], op0=ALU.mult, op1=ALU.add),
                         reads=["Sst%d" % h, "Send%d" % h], writes=["Send%d" % h])
                S.dma("sp", ret_p.rearrange("h p d -> p h d"), Send[:], reads=["Send%d" % h for h in range(8)])
                def p5A(t):
                    rows = slice(t * 128, (t + 1) * 128)
                    cols = slice(t * 128, (t + 1) * 128)
                    b2 = t % 2
                    ofk, zk, sk2 = "of32_%d" % b2, "zr%d" % b2, "stt%d" % b2
                    S.dma("sp", zr[:, b2, :], U[rows, C_ZR:C_ZR + 2048], writes=[zk])
                    if t < 8:
                        for h in range(8):
                            pm = h % 4
                            gt = float(GAM[h] ** (128 * t))
                            S.op("pe", lambda e, h=h, cols=cols, pm=pm: e.matmul(psM[:, pm, 0:256], lhsT=qgall[:, h, cols], rhs=Sstb[:, h, :], start=True, stop=True),
                                 reads=["qg%d" % t, "Sstb"], writes=["psM%d" % pm])
                            S.op("dve", lambda e, h=h, t=t, pm=pm, gt=gt, b2=b2: e.scalar_tensor_tensor(out=of32[:, b2, h, :], in0=psM[:, pm, 0:256], scalar=gt, in1=olocal[:, t, h * 256:(h + 1) * 256], op0=ALU.mult, op1=ALU.add),
                                 reads=["psM%d" % pm, "ol%d_%d" % (t, h)], writes=[ofk])
                    else:
                        S.op("dve", lambda e, b2=b2: e.tensor_copy(out=of32[:, b2, :, :].rearrange("p h d -> p (h d)"), in_=olocal[:, 8, :]), reads=["ol8"], writes=[ofk])
                    S.op("dve", lambda e, b2=b2: e.reduce_sum(out=stt[:, b2, 0, :], in_=of32[:, b2, :, :], axis=AX.X), reads=[ofk], writes=[sk2])
                    S.op("act", lambda e, b2=b2: e.activation(out=sq32[:], in_=of32[:, b2, :, :], func=AF.Square), reads=[ofk], writes=["sq32"])
                    S.op("dve", lambda e, b2=b2: e.reduce_sum(out=stt[:, b2, 1, :], in_=sq32[:], axis=AX.X), reads=["sq32", sk2], writes=[sk2])
                    S.op("dve", lambda e, b2=b2: e.tensor_scalar(out=stt[:, b2, 2, :], in0=stt[:, b2, 0, :], scalar1=1.0 / 256, scalar2=None, op0=ALU.mult), reads=[sk2], writes=[sk2])
                    S.op("dve", lambda e, b2=b2: e.tensor_tensor(out=stt[:, b2, 3, :], in0=stt[:, b2, 2, :], in1=stt[:, b2, 2, :], op=ALU.mult), reads=[sk2], writes=[sk2])
                    S.op("dve", lambda e, b2=b2: e.scalar_tensor_tensor(out=stt[:, b2, 4, :], in0=stt[:, b2, 1, :], scalar=1.0 / 256, in1=stt[:, b2, 3, :], op0=ALU.mult, op1=ALU.subtract), reads=[sk2], writes=[sk2])
                    S.op("dve", lambda e, b2=b2: e.tensor_scalar(out=stt[:, b2, 4, :], in0=stt[:, b2, 4, :], scalar1=GN_EPS, scalar2=None, op0=ALU.add), reads=[sk2], writes=[sk2])
                    S.op("act", lambda e, b2=b2: e.activation(out=stt[:, b2, 4, :], in_=stt[:, b2, 4, :], func=AF.Sqrt), reads=[sk2], writes=[sk2])
                    S.op("dve", lambda e, b2=b2: e.reciprocal(out=stt[:, b2, 4, :], in_=stt[:, b2, 4, :]), reads=[sk2], writes=[sk2])
                    S.op("dve", lambda e, b2=b2: e.scalar_tensor_tensor(out=stt[:, b2, 5, :], in0=stt[:, b2, 2, :], scalar=-1.0, in1=stt[:, b2, 4, :], op0=ALU.mult, op1=ALU.mult), reads=[sk2], writes=[sk2])

                def p5B(t):
                    rows = slice(t * 128, (t + 1) * 128)
                    cols = slice(t * 128, (t + 1) * 128)
                    b2 = t % 2
                    ofk, zk, sk2 = "of32_%d" % b2, "zr%d" % b2, "stt%d" % b2
                    for h in range(8):
                        S.op("act", lambda e, b2=b2, h=h: e.activation(out=of32[:, b2, h, :], in_=of32[:, b2, h, :], func=AF.Identity, bias=stt[:, b2, 5, h:h + 1], scale=stt[:, b2, 4, h:h + 1]),
                             reads=[ofk, sk2], writes=[ofk])
                    ofl = of32[:, b2, :, :].rearrange("p h d -> p (h d)")
                    S.op("pool", lambda e, ofl=ofl: e.tensor_tensor(out=ofl, in0=ofl, in1=gg[:], op=ALU.mult), reads=[ofk, "gg"], writes=[ofk])
                    S.op("pool", lambda e, ofl=ofl: e.tensor_tensor(out=ofl, in0=ofl, in1=gb[:], op=ALU.add), reads=[ofk, "gb"], writes=[ofk])
                    S.op("act", lambda e, b2=b2: e.activation(out=zr[:, b2, :], in_=zr[:, b2, :], func=AF.Silu), reads=[zk], writes=[zk])
                    rb = t % 2
                    S.op("dve", lambda e, ofl=ofl, rb=rb, b2=b2: e.tensor_tensor(out=Rb[:, rb, :], in0=ofl, in1=zr[:, b2, :], op=ALU.mult), reads=[ofk, zk], writes=["Rb%d" % rb])
                    for kc in range(16):
                        S.op("pe", lambda e, rb=rb, kc=kc: e.transpose(out=psT[:, kc // 8, kc % 8, :], in_=Rb[:, rb, kc * 128:(kc + 1) * 128], identity=identb[:]),
                             reads=["Rb%d" % rb, "identb"], writes=["psT%d" % (kc // 8)], signal=(kc % 8 == 7))
                    for hh in range(2):
                        copy_op("act", RT[:, hh * 8:(hh + 1) * 8, cols], psT[:, hh, :, :], ["psT%d" % hh], ["RT%d" % t])

                p5A(0)
                for t in range(NT):
                    if t + 1 < NT:
                        p5A(t + 1)
                    p5B(t)
                S.barrier()
        S.set_phase(7)
        with ExitStack() as st6:
            gr = sb(st6, [128, 2, 512], F32, "gr")
            W = sb(st6, [128, 2, 16, 512], BF16, "W")
            Wh["W"] = W
            bufs = {0: load_w(w_pr, 0, 16)}
            it = 0
            for c in range(4):
                if c + 1 < 4:
                    bufs[c + 1] = load_w(w_pr, (c + 1) * 512, 16)
                wb = bufs[c]
                for t in range(NT):
                    pb = it % 2
                    rows = slice(t * 128, (t + 1) * 128)
                    def _gr_load(cc, tt, ii):
                        S.dma("sp", gr[:, ii % 2, :], U[tt * 128:(tt + 1) * 128, C_GR + cc * 512:C_GR + (cc + 1) * 512], writes=["gr%d" % (ii % 2)])
                    if it == 0:
                        _gr_load(0, 0, 0)
                    nxt = (c, t + 1) if t + 1 < NT else ((c + 1, 0) if c + 1 < 4 else None)
                    if nxt is not None:
                        _gr_load(nxt[0], nxt[1], it + 1)
                    S.op("act", lambda e, pb=pb: e.activation(out=gr[:, pb, :], in_=gr[:, pb, :], func=AF.Sigmoid), reads=["gr%d" % pb], writes=["gr%d" % pb])
                    for kc in range(16):
                        S.op("pe", lambda e, pb=pb, kc=kc, t=t, wb=wb: e.matmul(psA[:, pb, :], lhsT=RT[:, kc, t * 128:(t + 1) * 128], rhs=W[:, wb, kc, :], start=(kc == 0), stop=(kc == 15)),
                             reads=["RT%d" % t] + wkeys(wb, 16), writes=["psA%d" % pb], signal=(kc == 15))
                    S.op("dve", lambda e, pb=pb, t=t, c=c: e.tensor_tensor(out=mr[:, t, c * 512:(c + 1) * 512], in0=psA[:, pb, :], in1=gr[:, pb, :], op=ALU.mult),
                         reads=["psA%d" % pb, "gr%d" % pb], writes=["mr%d_%d" % (t, c)])
                    it += 1
            S.barrier()

        with ExitStack() as stA:
            AT = bufB[:, 0:8, :]
            with ExitStack() as st7:
                S.set_phase(11)
                with ExitStack() as st9:
                    za = sb(st9, [128, 2, 1024], F32, "za")
                    Ab = sb(st9, [128, 2, 1024], BF16, "Ab")
                    oab = sb(st9, [128, 2, 1024], BF16, "oab")
                    for t in range(NT):
                        b = t % 2
                        rows = slice(t * 128, (t + 1) * 128)
                        cols = slice(t * 128, (t + 1) * 128)
                        S.dma("sp", za[:, b, :], U[rows, C_ZA:C_ZA + 1024], writes=["za%d" % b])
                        S.dma("sp", oab[:, b, :], OA[rows, :], writes=["oab%d" % b])
                        S.op("act", lambda e, b=b: e.activation(out=za[:, b, :], in_=za[:, b, :], func=AF.Silu), reads=["za%d" % b], writes=["za%d" % b])
                        S.op("dve", lambda e, b=b, t=t: e.tensor_tensor(out=Ab[:, b, :], in0=oab[:, b, :], in1=za[:, b, :], op=ALU.mult), reads=["oab%d" % b, "za%d" % b], writes=["Ab%d" % b])
                        for kc in range(8):
                            S.op("pe", lambda e, b=b, kc=kc: e.transpose(out=psT[:, 0, kc, :], in_=Ab[:, b, kc * 128:(kc + 1) * 128], identity=identb[:]),
                                 reads=["Ab%d" % b, "identb"], writes=["psT0"], signal=(kc == 7))
                        copy_op(evq(), AT[:, :, cols], psT[:, 0, :, :], ["psT0"], ["AT%d" % t])
                S.barrier()
            S.set_phase(12)
            with ExitStack() as st9b:
                ga = sb(st9b, [128, 2, 512], F32, "ga")
                W = sb(st9b, [128, 2, 16, 512], BF16, "W")
                Wh["W"] = W
                tmpm = sb(st9b, [128, 2, 512], F32, "tmpm")
                bufs = {0: load_w(w_pa, 0, 8)}
                it = 0
                for c in range(4):
                    if c + 1 < 4:
                        bufs[c + 1] = load_w(w_pa, (c + 1) * 512, 8)
                    wb = bufs[c]
                    for t in range(NT):
                        pb = it % 2
                        rows = slice(t * 128, (t + 1) * 128)
                        def _ga_load(cc, tt, ii):
                            S.dma("sp", ga[:, ii % 2, :], U[tt * 128:(tt + 1) * 128, C_GA + cc * 512:C_GA + (cc + 1) * 512], writes=["ga%d" % (ii % 2)])
                        if it == 0:
                            _ga_load(0, 0, 0)
                        nxt = (c, t + 1) if t + 1 < NT else ((c + 1, 0) if c + 1 < 4 else None)
                        if nxt is not None:
                            _ga_load(nxt[0], nxt[1], it + 1)
                        S.op("act", lambda e, pb=pb: e.activation(out=ga[:, pb, :], in_=ga[:, pb, :], func=AF.Sigmoid), reads=["ga%d" % pb], writes=["ga%d" % pb])
                        for kc in range(8):
                            S.op("pe", lambda e, pb=pb, kc=kc, t=t, wb=wb: e.matmul(psA[:, pb, :], lhsT=AT[:, kc, t * 128:(t + 1) * 128], rhs=W[:, wb, kc, :], start=(kc == 0), stop=(kc == 7)),
                                 reads=["AT%d" % t] + wkeys(wb, 8), writes=["psA%d" % pb], signal=(kc == 7))
                        S.op("dve", lambda e, pb=pb: e.tensor_tensor(out=tmpm[:, pb, :], in0=psA[:, pb, :], in1=ga[:, pb, :], op=ALU.mult),
                             reads=["psA%d" % pb, "ga%d" % pb], writes=["tmpm%d" % pb])
                        mk = "mr%d_%d" % (t, c)
                        S.op("dve", lambda e, pb=pb, t=t, c=c: e.tensor_tensor(out=mr[:, t, c * 512:(c + 1) * 512], in0=tmpm[:, pb, :], in1=mr[:, t, c * 512:(c + 1) * 512], op=ALU.add),
                             reads=["tmpm%d" % pb, mk], writes=[mk])
                        it += 1
                S.barrier()
                for t in range(NT):
                    cols = slice(t * 128, (t + 1) * 128)
                    for kc in range(16):
                        S.op("pe", lambda e, t=t, kc=kc: e.transpose(out=psT[:, kc // 8, kc % 8, :], in_=mr[:, t, kc * 128:(kc + 1) * 128], identity=identb[:]),
                             reads=["mr%d_%d" % (t, kc // 4), "identb"], writes=["psT%d" % (kc // 8)], signal=(kc % 8 == 7))
                    for hh in range(2):
                        copy_op(evq(), RT[:, hh * 8:(hh + 1) * 8, cols], psT[:, hh, :, :], ["psT%d" % hh], ["RT%d" % t])
                S.barrier()
        S.set_phase(13)
        with ExitStack() as st10:
            xr = sb(st10, [128, 2, 512], F32, "xr")
            W = sb(st10, [128, 2, 16, 512], BF16, "W")
            Wh["W"] = W
            yo = sb(st10, [128, 2, 512], F32, "yo")
            bufs = {0: load_w(w_out, 0, 16)}
            it = 0
            for c in range(4):
                if c + 1 < 4:
                    bufs[c + 1] = load_w(w_out, (c + 1) * 512, 16)
                wb = bufs[c]
                cs = slice(c * 512, (c + 1) * 512)
                for t in range(NT):
                    pb = it % 2
                    def _xr_load(cc, tt, ii):
                        S.dma("sp", xr[:, ii % 2, :], x_all[tt, :, cc * 512:(cc + 1) * 512], writes=["xr%d" % (ii % 2)])
                    if it == 0:
                        _xr_load(0, 0, 0)
                    nxt = (c, t + 1) if t + 1 < NT else ((c + 1, 0) if c + 1 < 4 else None)
                    if nxt is not None:
                        _xr_load(nxt[0], nxt[1], it + 1)
                    for kc in range(16):
                        S.op("pe", lambda e, pb=pb, kc=kc, t=t, wb=wb: e.matmul(psA[:, pb, :], lhsT=RT[:, kc, t * 128:(t + 1) * 128], rhs=W[:, wb, kc, :], start=(kc == 0), stop=(kc == 15)),
                             reads=["RT%d" % t] + wkeys(wb, 16), writes=["psA%d" % pb], signal=(kc == 15))
                    S.op("dve", lambda e, pb=pb: e.tensor_tensor(out=yo[:, pb, :], in0=psA[:, pb, :], in1=xr[:, pb, :], op=ALU.add),
                         reads=["psA%d" % pb, "xr%d" % pb], writes=["yo%d" % pb])
                    if t < 8:
                        S.dma("act", y_p[t * 128:(t + 1) * 128, cs], yo[:, pb, :], reads=["yo%d" % pb])
                    else:
                        S.dma("act", y_s[:, cs], yo[0:64, pb, :], reads=["yo%d" % pb])
                    it += 1
            S.barrier()

    S.set_phase(0)
    S.barrier()
    with nc.Block() as block:
        @block.tensor
        def _(e):
            for f in S.prog["pe"]:
                f(e)

        @block.scalar
        def _(e):
            for f in S.prog["act"]:
                f(e)

        @block.vector
        def _(e):
            for f in S.prog["dve"]:
                f(e)

        @block.gpsimd
        def _(e):
            for f in S.prog["pool"]:
                f(e)

        @block.sync
        def _(e):
            for f in S.prog["sp"]:
                f(e)
    top.close()
    return nc


def _tables(core):
    s, r = core // 4, core % 4
    gam = GAM
    p = np.arange(128)
    posR = np.zeros((128, NT), np.float64)
    posA = np.zeros((128, NTH), np.float64)
    for t in range(8):
        posR[:, t] = 16 + r * 1024 + t * 128 + p
        posA[:, t] = posR[:, t]
    p8 = np.zeros(128)
    p8[:64] = 16384 + (p[:64] % 4)
    p8[64:80] = np.arange(16)
    posR[:, 8] = p8
    posA[:, 8] = p8
    posA[:, 9] = 16 + r * 1024 - 128 + p
    invR = np.exp(-math.log(10000.0) * 2.0 * np.arange(64, dtype=np.float32) / 128).astype(np.float32)
    invA = np.exp(-math.log(500000.0) * 2.0 * np.arange(8, dtype=np.float32) / 16).astype(np.float32)
    angR = posR.astype(np.float32)[:, :, None] * invR[None, None, :]
    angA = posA.astype(np.float32)[:, :, None] * invA[None, None, :]
    ropeR = np.stack([np.cos(angR), np.sin(angR)], axis=2).astype(np.float32)
    ropeA = np.stack([np.cos(angA), np.sin(angA)], axis=2).astype(np.float32)
    sc = 128.0 ** -0.5
    kwsc = np.zeros((128, NT, 8), np.float64)
    for h in range(8):
        kwsc[:, :8, h] = (gam[h] ** (127 - p))[:, None] * sc
        kwsc[:64, 8, h] = gam[h] ** (3 - (p[:64] % 4)) * sc
        kwsc[64:80, 8, h] = gam[h] ** (15 - np.arange(16)) * sc
    dmask = np.zeros((128, 8, 2, 128), np.float64)
    j = p[:, None]
    i = p[None, :]
    for h in range(8):
        dmask[:, h, 0, :] = np.where(i >= j, gam[h] ** np.maximum(i - j, 0), 0.0) * sc
        m1 = (i < 64) & (j < 64) & ((i // 4) == (j // 4)) & (i >= j)
        dmask[:, h, 1, :] = np.where(m1, gam[h] ** np.maximum(i - j, 0), 0.0) * sc
    qgt = np.zeros((128, 8, 128), np.float64)
    qgs = np.zeros((128, 8, 4), np.float64)
    for h in range(8):
        qgt[:, h, :] = (gam[h] ** (p + 1))[None, :]
        qgs[:, h, :] = (gam[h] ** (np.arange(4) + 1))[None, :]
    bsel = np.zeros((128, 16), np.float32)
    for q in range(64):
        bsel[q, q // 4] = 1.0
    amask = np.zeros((128, 3, 128), np.float32)
    amask[:, 0, :] = (j <= i)
    amask[:, 1, :] = (j > i)
    amask[:, 2, :] = (j > i) if r > 0 else 0.0
    smn = np.zeros((128, 64), np.float32)
    for kk in range(64):
        for q in range(64):
            if kk // 4 == q // 4 and kk % 4 <= q % 4:
                smn[kk, q] = 1.0
    smn[64:80, :] = 1.0
    smc = np.zeros((128, 16), np.float32)
    for g in range(4):
        for ii in range(4):
            smc[:, g * 4 + ii] = (p > ii)
    posP = np.zeros((128, NPRE), np.float64)
    kwp = np.zeros((128, NPRE, 8), np.float64)
    for jslot in range(3):
        ch = r - 1 - jslot
        for tt in range(8):
            ti = jslot * 8 + tt
            posP[:, ti] = 16 + max(ch, 0) * 1024 + tt * 128 + p
            for h in range(8):
                kwp[:, ti, h] = gam[h] ** (1024.0 * jslot + 1023 - (tt * 128 + p)) * sc
    angP = posP.astype(np.float32)[:, :, None] * invR[None, None, :]
    ropeP = np.stack([np.cos(angP), np.sin(angP)], axis=2).astype(np.float32)
    coef = np.zeros((128, 9, 8), np.float64)
    for h in range(8):
        for rp in range(r):
            coef[:, 4 * s + rp, h] = gam[h] ** (1024 * (r - 1 - rp))
        coef[:, 8, h] = gam[h] ** (1024 * r)
    return dict(ropeR=ropeR, ropeA=ropeA, kwsc=kwsc.astype(np.float32), dmask=dmask.astype(np.float32),
                qgt=qgt.astype(np.float32), qgs=qgs.astype(np.float32), bsel=bsel, amask=amask, smn=smn, smc=smc,
                coef=coef.astype(np.float32), ident=np.eye(128, dtype=np.float32), ropeP=ropeP, kwp=kwp.astype(np.float32))


_NC_CACHE = {}


def kernel(x_prompt, x_sample, cache_win_k, cache_win_v, state_ret, meta_tokens, norm_gain, w_in,
           q_norm_gain, k_norm_gain, attn_sinks, ret_gn_gain, ret_gn_bias, w_branch_attn, w_branch_ret, w_out):
    f = np.float32
    x_prompt = np.asarray(x_prompt, f)
    x_sample = np.asarray(x_sample, f)
    ck = np.asarray(cache_win_k, f)[0].reshape(128, 128, 256)
    cv = np.asarray(cache_win_v, f)[0].reshape(128, 128, 256)
    st = np.asarray(state_ret, f)[0]
    meta = np.asarray(meta_tokens, f)
    w_in_ = np.ascontiguousarray(np.asarray(w_in, f)[0])
    w_pa_ = np.ascontiguousarray(np.asarray(w_branch_attn, f)[0])
    w_pr_ = np.ascontiguousarray(np.asarray(w_branch_ret, f)[0])
    w_out_ = np.ascontiguousarray(np.asarray(w_out, f)[0])
    gqk = np.concatenate([np.tile(np.asarray(q_norm_gain, f)[0], 16), np.tile(np.asarray(k_norm_gain, f)[0], 4)])[None, :]
    if "nc" not in _NC_CACHE:
        _NC_CACHE["nc"] = build_program()
    nc = _NC_CACHE["nc"]
    in_maps = []
    for c in range(NCORES):
        s, r = c // 4, c % 4
        xa = np.zeros((NTH, 128, D), f)
        xa[:8] = x_prompt[s, r * 1024:(r + 1) * 1024].reshape(8, 128, D)
        xa[8, :64] = x_sample[16 * c:16 * c + 16].reshape(64, D)
        xa[8, 64:80] = meta
        if r > 0:
            xa[9] = x_prompt[s, r * 1024 - 128:r * 1024]
        m = dict(x_all=xa, w_in=w_in_, w_pa=w_pa_, w_pr=w_pr_, w_out=w_out_,
                 norm_g=np.asarray(norm_gain, f).reshape(1, D), gqk=np.ascontiguousarray(gqk),
                 sinks=np.asarray(attn_sinks, f).reshape(1, 16),
                 gn_g=np.asarray(ret_gn_gain, f).reshape(1, D), gn_b=np.asarray(ret_gn_bias, f).reshape(1, D),
                 cache_k=np.ascontiguousarray(ck[16 * c:16 * c + 16]), cache_v=np.ascontiguousarray(cv[16 * c:16 * c + 16]),
                 state=np.ascontiguousarray(st[16 * c:16 * c + 16]))
        xp = np.zeros((NPRE, 128, D), f)
        for jslot in range(3):
            ch = r - 1 - jslot
            if ch >= 0:
                xp[jslot * 8:(jslot + 1) * 8] = x_prompt[s, ch * 1024:(ch + 1) * 1024].reshape(8, 128, D)
        m["x_pre"] = xp
        m.update(_tables(c))
        in_maps.append(m)
    res = run_bass_kernel_spmd(nc, in_maps, core_ids=list(range(NCORES)))
    R = res.results
    y_prompt = np.zeros((2, 4096, D), f)
    y_sample = np.zeros((128, 4, D), f)
    wkp = np.zeros((1, 2, 128, 4, 64), f)
    wvp = np.zeros((1, 2, 128, 4, 64), f)
    retp = np.zeros((1, 2, 8, 128, 256), f)
    wks = np.zeros((1, 128, 128, 4, 64), f)
    wvs = np.zeros((1, 128, 128, 4, 64), f)
    rets = np.zeros((1, 128, 8, 128, 256), f)
    for c in range(NCORES):
        s, r = c // 4, c % 4
        y_prompt[s, r * 1024:(r + 1) * 1024] = R[c]["y_p"]
        y_sample[16 * c:16 * c + 16] = R[c]["y_s"].reshape(16, 4, D)
        if r == 3:
            wkp[0, s] = R[c]["wk_p"].reshape(128, 4, 64)
            wvp[0, s] = R[c]["wv_p"].reshape(128, 4, 64)
            retp[0, s] = R[c]["ret_p"]
        wks[0, 16 * c:16 * c + 16] = R[c]["wk_s"].reshape(16, 128, 4, 64)
        wvs[0, 16 * c:16 * c + 16] = R[c]["wv_s"].reshape(16, 128, 4, 64)
        rets[0, 16 * c:16 * c + 16] = R[c]["ret_s"]
    return (y_prompt, y_sample, wkp, wvp, retp, wks, wvs, rets)
```
